# Optimizing a Trainium2 kernel written in Bass

```python
import math
import jax
import jax.numpy as jnp
from jax import lax
import numpy as np

D_MODEL = 1024
BATCH = 8
SEQ = 4096
DEPTH = 2

GRID_W = 64
CTX_LEN = 256
N_BRANCH = 4
MIX_WIDTH = D_MODEL // N_BRANCH
POOL_WINDOWS = (2, 4, 8, 16)
POOL_GROUPS = len(POOL_WINDOWS)
FNET_GROUPS = 4
HYENA_ORDER = 2
HYENA_BANDS = 16
HYENA_POS_DIM = 1 + 2 * HYENA_BANDS
HYENA_FILTER_HIDDEN = 64
HYENA_DECAY_TARGET = 1e-2
HYENA_FAST_DECAY = 0.3
HYENA_SLOW_DECAY = 1.5
HYENA_DECAY_SHIFT = 0.05
SHORT_CONV = 3
NA_HEAD_DIM = 64
NA_HEADS = MIX_WIDTH // NA_HEAD_DIM
NA_WIN_H = 8
NA_WIN_W = 16
FFN_HIDDEN = 2816
EPS = 1e-6
NEG_INF = -1e30

POOL_OFF = 0
FNET_OFF = POOL_OFF + MIX_WIDTH
HYENA_OFF = FNET_OFF + MIX_WIDTH
NA_OFF = HYENA_OFF + 3 * MIX_WIDTH
GATE_OFF = NA_OFF + 3 * MIX_WIDTH
IN_WIDTH = GATE_OFF + N_BRANCH * D_MODEL

kernel_name = 'hybrid_pool_fnet_hyena_natten_dit_block'


def rmsnorm(x, g):
    xf = x.astype(jnp.float32)
    y = xf * lax.rsqrt(jnp.mean(xf * xf, axis=-1, keepdims=True) + EPS)
    return (y * g.astype(jnp.float32)).astype(x.dtype)


def ada_norm(x, g, shift, scale):
    return rmsnorm(x, g) * (1 + scale) + shift


def depthwise_conv(x, w, b):
    K = w.shape[0]
    pad = K // 2
    L = x.shape[1]
    xp = jnp.pad(x, ((0, 0), (pad, K - 1 - pad), (0, 0)))
    return sum(xp[:, j:j + L] * w[j] for j in range(K)) + b


def pool_mixer(u, pool_w, pool_scale):
    B, L, _ = u.shape
    ug = u.reshape(B, L, POOL_GROUPS, -1).astype(jnp.float32)
    csum = jnp.concatenate([jnp.zeros_like(ug[:, :1]), jnp.cumsum(ug, axis=1)], axis=1)
    t = np.arange(L)[:, None]
    win = np.array(POOL_WINDOWS)[None, :]
    lo = np.clip(t - win // 2, 0, L)
    hi = np.clip(t - win // 2 + win, 0, L)
    grp = np.arange(POOL_GROUPS)[None, :]
    cnt = (hi - lo).astype(np.float32)[None, :, :, None]
    y = (csum[:, hi, grp] - csum[:, lo, grp]) / cnt - ug
    y = jnp.einsum('blgc,gcd->blgd', y.astype(u.dtype), pool_w)
    return y.reshape(B, L, -1) * pool_scale


def fourier_mixer(u):
    B, L, _ = u.shape
    ug = u.reshape(B, L, FNET_GROUPS, -1).astype(jnp.float32)
    y = jnp.fft.fft2(ug, axes=(1, 3), norm='ortho').real
    return y.reshape(B, L, -1).astype(u.dtype)


def hyena_filters(L, p):
    f32 = jnp.float32
    t = jnp.linspace(0.0, 1.0, L, dtype=f32)[:, None]
    bands = jnp.linspace(1e-4, HYENA_BANDS - 1, HYENA_BANDS, dtype=f32)[None, :]
    ang = (2.0 * math.pi / L) * jnp.arange(L, dtype=f32)[:, None] * bands
    feats = jnp.concatenate([t, jnp.cos(ang), -jnp.sin(ang)], axis=-1)
    freq = p['hyena_freq'].astype(f32)
    h = jnp.sin(freq * (feats @ p['hyena_filt_w1'].astype(f32) + p['hyena_filt_b1'].astype(f32)))
    h = jnp.sin(freq * (h @ p['hyena_filt_w2'].astype(f32) + p['hyena_filt_b2'].astype(f32)))
    h = (h @ p['hyena_filt_w3'].astype(f32) + p['hyena_filt_b3'].astype(f32))
    h = h.reshape(L, HYENA_ORDER, 2, MIX_WIDTH)
    deltas = jnp.linspace(math.log(HYENA_DECAY_TARGET) / HYENA_SLOW_DECAY,
                          math.log(HYENA_DECAY_TARGET) / HYENA_FAST_DECAY, MIX_WIDTH, dtype=f32)
    decay = jnp.exp(-t[:, :, None, None] * jnp.abs(deltas))
    h = h * (decay + HYENA_DECAY_SHIFT)
    return h / (jnp.sum(jnp.abs(h), axis=(0, 2), keepdims=True) + EPS)


def long_conv(z, hf, hb, skip):
    L, C = hf.shape
    k = jnp.concatenate([hf, jnp.zeros((1, C), hf.dtype), hb[1:][::-1]], axis=0)
    zf = jnp.fft.rfft(z, n=2 * L, axis=1)
    kf = jnp.fft.rfft(k, axis=0)
    y = jnp.fft.irfft(zf * kf[None], n=2 * L, axis=1)[:, :L]
    return y + skip * z


def hyena_mixer(u, p):
    u_c = depthwise_conv(u, p['hyena_conv_w'], p['hyena_conv_b']).astype(jnp.float32)
    v, x1, x2 = jnp.split(u_c, 3, axis=-1)
    h = hyena_filters(u.shape[1], p)
    skip = p['hyena_skip'].astype(jnp.float32)
    z = v
    for n, gate in enumerate((x1, x2)):
        z = gate * long_conv(z, h[:, n, 0], h[:, n, 1], skip[n])
    return z.astype(u.dtype)


def neighbourhood_attention(q, k, v, k_ctx, v_ctx, rpb):
    B, L, _ = q.shape
    rows = L // GRID_W
    kh = min(NA_WIN_H, rows)
    grid = (B, rows, GRID_W, NA_HEADS, NA_HEAD_DIM)
    q, k, v = q.reshape(grid), k.reshape(grid), v.reshape(grid)
    kc = k_ctx.reshape(B, -1, NA_HEADS, NA_HEAD_DIM)
    vc = v_ctx.reshape(B, -1, NA_HEADS, NA_HEAD_DIM)
    r = np.arange(rows)
    r0 = np.clip(r - kh // 2, 0, rows - kh)
    row_idx = r0[:, None] + np.arange(kh)[None, :]
    col = np.arange(GRID_W)
    c0 = np.clip(col - NA_WIN_W // 2, 0, GRID_W - NA_WIN_W)
    col_ok = (col[None, :] >= c0[:, None]) & (col[None, :] < c0[:, None] + NA_WIN_W)
    dr = row_idx - r[:, None] + (NA_WIN_H - 1)
    dc = np.clip(col[None, :] - col[:, None], 1 - NA_WIN_W, NA_WIN_W - 1) + (NA_WIN_W - 1)
    bias = rpb.astype(jnp.float32)[:, dr[:, None, :, None], dc[None, :, None, :]]
    bias = jnp.where(col_ok[:, None, :], bias, NEG_INF)
    kb = k[:, row_idx]
    vb = v[:, row_idx]
    scale = NA_HEAD_DIM ** -0.5
    s_nb = jnp.einsum('brqhd,brikhd->bhrqik', q, kb).astype(jnp.float32) * scale + bias
    s_cx = jnp.einsum('brqhd,bchd->bhrqc', q, kc).astype(jnp.float32) * scale
    n_nb = kh * GRID_W
    s = jnp.concatenate([s_nb.reshape(B, NA_HEADS, rows, GRID_W, n_nb), s_cx], axis=-1)
    prob = jax.nn.softmax(s, axis=-1).astype(v.dtype)
    p_nb = prob[..., :n_nb].reshape(s_nb.shape)
    p_cx = prob[..., n_nb:]
    o = (jnp.einsum('bhrqik,brikhd->brqhd', p_nb, vb)
         + jnp.einsum('bhrqc,bchd->brqhd', p_cx, vc))
    return o.reshape(B, L, MIX_WIDTH)


def context_attention(q, k, v):
    B, C, _ = q.shape
    shp = (B, C, NA_HEADS, NA_HEAD_DIM)
    q, k, v = q.reshape(shp), k.reshape(shp), v.reshape(shp)
    s = jnp.einsum('bqhd,bkhd->bhqk', q, k).astype(jnp.float32) * (NA_HEAD_DIM ** -0.5)
    prob = jax.nn.softmax(s, axis=-1).astype(v.dtype)
    return jnp.einsum('bhqk,bkhd->bqhd', prob, v).reshape(B, C, MIX_WIDTH)


def token_mixer(proj, k_ctx, v_ctx, p, on_grid):
    def cols(off, width):
        return proj[..., off:off + width]
    y_pool = pool_mixer(cols(POOL_OFF, MIX_WIDTH), p['pool_w'], p['pool_scale'])
    y_fnet = fourier_mixer(cols(FNET_OFF, MIX_WIDTH))
    y_hyena = hyena_mixer(cols(HYENA_OFF, 3 * MIX_WIDTH), p)
    q = cols(NA_OFF, MIX_WIDTH)
    if on_grid:
        y_na = neighbourhood_attention(q, cols(NA_OFF + MIX_WIDTH, MIX_WIDTH),
                                       cols(NA_OFF + 2 * MIX_WIDTH, MIX_WIDTH),
                                       k_ctx, v_ctx, p['na_rpb'])
    else:
        y_na = context_attention(q, k_ctx, v_ctx)
    gates = jax.nn.sigmoid(cols(GATE_OFF, N_BRANCH * D_MODEL))
    merged = 0
    for b, y in enumerate((y_pool, y_fnet, y_hyena, y_na)):
        merged = merged + gates[..., b * D_MODEL:(b + 1) * D_MODEL] * (y @ p['w_branch'][b])
    return merged @ p['w_out']


def conv_ffn(h, p):
    u, g = jnp.split(h @ p['ffn_w_up'], 2, axis=-1)
    g = depthwise_conv(g, p['ffn_conv_w'], p['ffn_conv_b'])
    return (jax.nn.silu(g) * u) @ p['ffn_w_down']


def setup_inputs(seed: int = 0) -> dict:
    key = jax.random.key(seed)
    ks = iter(jax.random.split(key, 32))

    def nrm(shape, s):
        return jax.random.normal(next(ks), shape, jnp.float32) * s

    D, F, M, Hd = D_MODEL, FFN_HIDDEN, MIX_WIDTH, HYENA_FILTER_HIDDEN
    return {
        'x': nrm((BATCH, SEQ, D), 1.0),
        'c': nrm((BATCH, D), 1.0),
        'ctx': nrm((BATCH, CTX_LEN, D), 1.0),
        'c_ctx': nrm((D,), 1.0),
        'w_mod': nrm((DEPTH, D, 6 * D), 0.5 * D ** -0.5),
        'b_mod': nrm((DEPTH, 6 * D), 0.01),
        'norm1_g': 1.0 + nrm((DEPTH, D), 0.02),
        'norm2_g': 1.0 + nrm((DEPTH, D), 0.02),
        'w_in': nrm((DEPTH, D, IN_WIDTH), D ** -0.5),
        'pool_w': nrm((DEPTH, POOL_GROUPS, M // POOL_GROUPS, M // POOL_GROUPS), (M // POOL_GROUPS) ** -0.5),
        'pool_scale': 1.0 + nrm((DEPTH, M), 0.02),
        'hyena_conv_w': nrm((DEPTH, SHORT_CONV, 3 * M), SHORT_CONV ** -0.5),
        'hyena_conv_b': nrm((DEPTH, 3 * M), 0.01),
        'hyena_filt_w1': nrm((DEPTH, HYENA_POS_DIM, Hd), HYENA_POS_DIM ** -0.5),
        'hyena_filt_b1': nrm((DEPTH, Hd), 0.01),
        'hyena_filt_w2': nrm((DEPTH, Hd, Hd), Hd ** -0.5),
        'hyena_filt_b2': nrm((DEPTH, Hd), 0.01),
        'hyena_filt_w3': nrm((DEPTH, Hd, HYENA_ORDER * 2 * M), Hd ** -0.5),
        'hyena_filt_b3': nrm((DEPTH, HYENA_ORDER * 2 * M), 0.01),
        'hyena_freq': 1.0 + nrm((DEPTH, Hd), 0.1),
        'hyena_skip': nrm((DEPTH, HYENA_ORDER, M), 0.5),
        'na_rpb': nrm((DEPTH, NA_HEADS, 2 * NA_WIN_H - 1, 2 * NA_WIN_W - 1), 0.1),
        'w_branch': nrm((DEPTH, N_BRANCH, M, D), M ** -0.5),
        'w_out': nrm((DEPTH, D, D), D ** -0.5),
        'ffn_w_up': nrm((DEPTH, D, 2 * F), D ** -0.5),
        'ffn_conv_w': nrm((DEPTH, SHORT_CONV, F), SHORT_CONV ** -0.5),
        'ffn_conv_b': nrm((DEPTH, F), 0.01),
        'ffn_w_down': nrm((DEPTH, F, D), F ** -0.5),
        'final_norm_g': 1.0 + nrm((D,), 0.02),
    }


def reference(x, c, ctx, c_ctx, w_mod, b_mod, norm1_g, norm2_g, w_in, pool_w, pool_scale,
              hyena_conv_w, hyena_conv_b, hyena_filt_w1, hyena_filt_b1, hyena_filt_w2,
              hyena_filt_b2, hyena_filt_w3, hyena_filt_b3, hyena_freq, hyena_skip, na_rpb,
              w_branch, w_out, ffn_w_up, ffn_conv_w, ffn_conv_b, ffn_w_down, final_norm_g):
    xc = ctx
    c_act = jax.nn.silu(c)
    cc_act = jax.nn.silu(c_ctx)[None]
    for l in range(DEPTH):
        last = l == DEPTH - 1
        p = {
            'pool_w': pool_w[l], 'pool_scale': pool_scale[l],
            'hyena_conv_w': hyena_conv_w[l], 'hyena_conv_b': hyena_conv_b[l],
            'hyena_filt_w1': hyena_filt_w1[l], 'hyena_filt_b1': hyena_filt_b1[l],
            'hyena_filt_w2': hyena_filt_w2[l], 'hyena_filt_b2': hyena_filt_b2[l],
            'hyena_filt_w3': hyena_filt_w3[l], 'hyena_filt_b3': hyena_filt_b3[l],
            'hyena_freq': hyena_freq[l], 'hyena_skip': hyena_skip[l], 'na_rpb': na_rpb[l],
            'w_branch': w_branch[l], 'w_out': w_out[l],
            'ffn_w_up': ffn_w_up[l], 'ffn_conv_w': ffn_conv_w[l],
            'ffn_conv_b': ffn_conv_b[l], 'ffn_w_down': ffn_w_down[l],
        }
        mod = (c_act @ w_mod[l] + b_mod[l])[:, None]
        mod_c = (cc_act @ w_mod[l] + b_mod[l])[:, None]
        sh1, sc1, g1, sh2, sc2, g2 = jnp.split(mod, 6, axis=-1)
        sh1c, sc1c, g1c, sh2c, sc2c, g2c = jnp.split(mod_c, 6, axis=-1)
        hc = ada_norm(xc, norm1_g[l], sh1c, sc1c)
        if last:
            kv_c = hc @ w_in[l][:, NA_OFF + MIX_WIDTH:NA_OFF + 3 * MIX_WIDTH]
        else:
            proj_c = hc @ w_in[l]
            kv_c = proj_c[..., NA_OFF + MIX_WIDTH:NA_OFF + 3 * MIX_WIDTH]
        k_c, v_c = jnp.split(kv_c, 2, axis=-1)
        hx = ada_norm(x, norm1_g[l], sh1, sc1)
        proj_x = hx @ w_in[l]
        x = x + g1 * token_mixer(proj_x, k_c, v_c, p, True)
        x = x + g2 * conv_ffn(ada_norm(x, norm2_g[l], sh2, sc2), p)
        if not last:
            xc = xc + g1c * token_mixer(proj_c, k_c, v_c, p, False)
            xc = xc + g2c * conv_ffn(ada_norm(xc, norm2_g[l], sh2c, sc2c), p)
    return rmsnorm(x, final_norm_g)
```

```python
import contextlib
import math
import numpy as np
import ml_dtypes
import concourse.bass as bass
import concourse.mybir as mybir
from concourse.bass_utils import run_bass_kernel_spmd

F32 = mybir.dt.float32
BF16 = mybir.dt.bfloat16
AF = mybir.ActivationFunctionType
ALU = mybir.AluOpType
AX = mybir.AxisListType
NPBF = ml_dtypes.bfloat16

D = 1024
KD = 8
LX = 4096
LC = 256
DEPTH = 2
FF = 2816
NFF = 22
INW = 6144
EPS = 1e-6


class Buf:
    __slots__ = ("name", "lw", "rd")

    def __init__(self, name=""):
        self.name = name
        self.lw = None
        self.rd = []


class Op:
    __slots__ = ("eng", "fn", "deps", "is_dma", "sig", "need", "pos")

    def __init__(self, eng, fn, is_dma):
        self.eng = eng
        self.fn = fn
        self.deps = []
        self.is_dma = is_dma
        self.sig = None
        self.need = False
        self.pos = 0


class Tl:
    __slots__ = ("t", "b", "_tb")

    def __init__(self, t, name=""):
        self.t = t
        self.b = Buf(name)
        self._tb = None

    @property
    def tb(self):
        if self._tb is None:
            self._tb = self.t[:].bitcast(BF16)
        return self._tb


class Ring:
    def __init__(self, items):
        self.items = items
        self.i = 0

    def next(self):
        x = self.items[self.i % len(self.items)]
        self.i += 1
        return x


ENGS = ("pe", "act", "dve", "pool", "sp")
SEM_ROT = 1500
DMA_SLOTS = 8


class Prog:
    def __init__(self, nc):
        self.nc = nc
        self.ops = {e: [] for e in ENGS}
        self.stack = contextlib.ExitStack()
        self.cur = self.stack
        self.dma_hist = {e: [] for e in ENGS}
        self.nuid = 0

    def sb(self, shape, dt, name="t"):
        self.nuid += 1
        t = self.cur.enter_context(self.nc.sbuf_tensor(f"{name}_{self.nuid}", list(shape), dt))
        return Tl(t, name)

    def ring(self, n, shape, dt, name="r"):
        return Ring([self.sb(shape, dt, name) for _ in range(n)])

    def ps(self, shape, dt, name="p"):
        self.nuid += 1
        t = self.cur.enter_context(self.nc.psum_tensor(f"{name}_{self.nuid}", list(shape), dt))
        return Tl(t, name)

    @contextlib.contextmanager
    def phase(self):
        old = self.cur
        with contextlib.ExitStack() as st:
            self.cur = st
            yield
            self.barrier()
        self.cur = old

    def add(self, eng, fn, reads=(), writes=(), dma=False):
        op = Op(eng, fn, dma)
        deps = []
        for b in reads:
            if b.lw is not None:
                deps.append(b.lw)
        for b in writes:
            if b.lw is not None:
                deps.append(b.lw)
            deps.extend(b.rd)
        if dma:
            h = self.dma_hist[eng]
            if len(h) >= DMA_SLOTS:
                deps.append(h[len(h) - DMA_SLOTS])
            h.append(op)
        seen = set()
        for d in deps:
            if eng == "pe" and d.eng == "pe" and d.fn is not None:
                continue
            if id(d) not in seen and d is not op:
                seen.add(id(d))
                op.deps.append(d)
                d.need = True
        for b in reads:
            b.rd.append(op)
        for b in writes:
            b.lw = op
            b.rd = []
        op.pos = len(self.ops[eng])
        self.ops[eng].append(op)
        return op

    def barrier(self):
        lasts = []
        for e in ENGS:
            for op in reversed(self.ops[e]):
                if op.fn is not None and not op.is_dma:
                    lasts.append(op)
                    break
            lasts.extend(self.dma_hist[e][-DMA_SLOTS:])
        for e in ENGS:
            op = Op(e, None, False)
            for d in lasts:
                op.deps.append(d)
                d.need = True
            self.ops[e].append(op)

    def emit(self):
        nc = self.nc
        st = self.stack
        semcache = {}

        def sem(key):
            if key not in semcache:
                semcache[key] = st.enter_context(nc.semaphore("s_" + "_".join(str(k) for k in key)))
            return semcache[key]

        for e in ENGS:
            cnt = 0
            dcnt = 0
            for op in self.ops[e]:
                if op.fn is None:
                    continue
                if op.is_dma:
                    slot = dcnt % DMA_SLOTS
                    u = dcnt // DMA_SLOTS
                    op.sig = (("d", e, slot, u // SEM_ROT), 16 * (u % SEM_ROT + 1), 16)
                    dcnt += 1
                elif op.need:
                    op.sig = (("c", e, cnt // SEM_ROT), cnt % SEM_ROT + 1, 1)
                    cnt += 1
        for e in ENGS:
            for op in self.ops[e]:
                if op.sig is not None:
                    sem(op.sig[0])

        def run_engine(e, eng):
            waited = {}
            for op in self.ops[e]:
                for d in op.deps:
                    if d.sig is None:
                        continue
                    key, val, _ = d.sig
                    if waited.get(key, 0) < val:
                        eng.wait_ge(sem(key), val)
                        waited[key] = val
                if op.fn is None:
                    continue
                ins = op.fn(eng)
                if op.sig is not None:
                    ins.then_inc(sem(op.sig[0]), op.sig[2])

        with nc.Block() as block:
            @block.tensor
            def _(eng):
                run_engine("pe", eng)

            @block.scalar
            def _(eng):
                run_engine("act", eng)

            @block.vector
            def _(eng):
                run_engine("dve", eng)

            @block.gpsimd
            def _(eng):
                run_engine("pool", eng)

            @block.sync
            def _(eng):
                run_engine("sp", eng)

    def dma(self, out, in_, reads=(), writes=(), q="sp"):
        return self.add(q, lambda e: e.dma_start(out=out, in_=in_), reads, writes, dma=True)

    def mm(self, out, lhsT, rhs, start, stop, reads=(), writes=()):
        return self.add("pe", lambda e: e.matmul(out, lhsT, rhs, start=start, stop=stop), reads, writes)

    def tr(self, out, in_, ident, reads=(), writes=()):
        return self.add("pe", lambda e: e.transpose(out, in_, ident), reads, writes)

    def act(self, out, in_, func, reads=(), writes=(), **kw):
        return self.add("act", lambda e: e.activation(out, in_, func, **kw), reads, writes)

    def cp(self, eng, out, in_, reads=(), writes=()):
        if eng == "act":
            return self.add("act", lambda e: e.copy(out, in_), reads, writes)
        return self.add(eng, lambda e: e.tensor_copy(out, in_), reads, writes)

    def tt(self, eng, out, a, b, op, reads=(), writes=()):
        return self.add(eng, lambda e: e.tensor_tensor(out, a, b, op), reads, writes)

    def ts(self, eng, out, a, s1, s2, op0, op1, reads=(), writes=()):
        return self.add(eng, lambda e: e.tensor_scalar(out, a, s1, s2, op0, op1), reads, writes)

    def ts1(self, eng, out, a, s1, op0, reads=(), writes=()):
        return self.add(eng, lambda e: e.tensor_single_scalar(out, a, s1, op0), reads, writes)

    def stt(self, eng, out, a, s, b, op0, op1, reads=(), writes=()):
        eng = "dve"
        return self.add(eng, lambda e: e.scalar_tensor_tensor(out, a, s, b, op0, op1), reads, writes)


class Seq:
    def __init__(self, name, L, which):
        self.name = name
        self.L = L
        self.which = which
        self.BS = min(512, L)
        self.NB = L // self.BS
        self.NT = L // 128


class Builder:
    def __init__(self, debug=False, layers=DEPTH, do_x=True):
        self.debug = debug
        self.layers = layers
        self.do_x = do_x
        self.nc = bass.Bass("TRN2", target_bir_lowering=False)
        self.P = Prog(self.nc)
        self.evac_i = 0

    def din(self, name, shape, dt=F32):
        return self.nc.dram_tensor(name, list(shape), dt, kind="ExternalInput").ap()

    def dscr(self, name, shape, dt):
        kind = "ExternalOutput" if self.debug else "Internal"
        return self.nc.dram_tensor(name, list(shape), dt, kind=kind).ap()

    def psf(self):
        return self.PSF.next()

    def psb(self):
        return self.PSB.next()

    def mm_jobs(self, jobs):
        P = self.P
        n = max(len(j[2]) for j in jobs)
        for k in range(n):
            for (pap, pbuf, steps) in jobs:
                if k < len(steps):
                    lhsT, rhs, rb = steps[k]
                    P.mm(pap, lhsT, rhs, k == 0, k == len(steps) - 1, rb, [pbuf])

    def evac_eng(self):
        self.evac_i += 1
        return "act" if self.evac_i % 2 else "dve"

    def build(self):
        P = self.P
        nc = self.nc
        sx = Seq("x", LX, 0)
        sc = Seq("c", LC, 1)
        seqs = [sc, sx] if self.do_x else [sc]
        I = {}
        I["xT"] = self.din("xT", [D, LX])
        I["cT"] = self.din("cT", [D, LC])
        I["cvec"] = self.din("cvec", [128, 16])
        I["w_mod"] = self.din("w_mod", [DEPTH, D, 6 * D])
        I["w_in"] = self.din("w_in", [DEPTH, D, INW])
        I["w_branch"] = self.din("w_branch", [DEPTH, 4, 256, D])
        I["w_out"] = self.din("w_out", [DEPTH, D, D])
        I["ffn_w_up"] = self.din("ffn_w_up", [DEPTH, D, 2 * FF])
        I["ffn_w_down"] = self.din("ffn_w_down", [DEPTH, FF, D])
        I["pool_w"] = self.din("pool_w", [DEPTH, 4, 64, 64])
        I["vecs"] = self.din("vecs", [128, NVEC])
        I["fw1"] = self.din("fw1", [DEPTH, 33, 64])
        I["fw2"] = self.din("fw2", [DEPTH, 64, 64])
        I["fw3"] = self.din("fw3", [DEPTH, 64, 1024])
        I["bias"] = self.din("bias", [DEPTH, 128, 4, 5, 576])
        I["ident"] = self.din("ident", [128, 128], BF16)
        I["alt"] = self.din("alt", [128, 1], BF16)
        for s in seqs:
            L = s.L
            I["fc_" + s.name] = self.din("fc_" + s.name, [L, L], BF16)
            I["fs_" + s.name] = self.din("fs_" + s.name, [L, L], BF16)
            for k_ in ("mec_", "mes_", "moc_", "mos_"):
                I[k_ + s.name] = self.din(k_ + s.name, [L // 2, L // 2 + 128], BF16)
            for k_ in ("iec_", "ies_", "ioc_", "ios_"):
                I[k_ + s.name] = self.din(k_ + s.name, [L // 2 + 128, L // 2], BF16)
            I["feats_" + s.name] = self.din("feats_" + s.name, [33, L])
            I["dec_" + s.name] = self.din("dec_" + s.name, [256, L])
            I["pcorr_" + s.name] = self.din("pcorr_" + s.name, [128, 2, 16])
            I["chdft_" + s.name] = self.din("chdft_" + s.name, [128, 256], BF16)
        self.I = I
        outT = self.nc.dram_tensor("outT", [D, LX], F32, kind="ExternalOutput").ap()
        S = {}
        for s in seqs:
            n = s.name
            L = s.L
            S["xa_" + n] = self.dscr("xa_" + n, [D, L], F32)
            S["xb_" + n] = self.dscr("xb_" + n, [D, L], F32)
            S["pool_" + n] = self.dscr("pool_" + n, [256, L], F32)
            S["fnet_" + n] = self.dscr("fnet_" + n, [L, 256], BF16)
            S["hy_" + n] = self.dscr("hy_" + n, [768, L], F32)
            S["q_" + n] = self.dscr("q_" + n, [256, L], BF16)
            S["k_" + n] = self.dscr("k_" + n, [256, L], BF16)
            S["v_" + n] = self.dscr("v_" + n, [L + 64, 256], BF16)
            S["gate_" + n] = self.dscr("gate_" + n, [4096, L], BF16)
            S["ybr_" + n] = self.dscr("ybr_" + n, [4, 256, L], BF16)
            S["kf_" + n] = self.dscr("kf_" + n, [2, 2, 4, 128, L // 2 + 128], F32)
            S["x2_" + n] = self.dscr("x2_" + n, [256, L], BF16)
            S["aT_" + n] = self.dscr("aT_" + n, [FF, L], BF16)
        self.S = S

        with P.stack:
            self.PSF = Ring([P.ps([128, 512], F32, "psf") for _ in range(8)])
            self.PSB = self.PSF
            self.ident = P.sb([128, 128], BF16, "ident")
            self.ones = P.sb([128, 128], BF16, "ones")
            self.vecs = P.sb([128, NVEC], F32, "vecs")
            self.modv = P.sb([128, 96], F32, "modv")
            self.avec = P.sb([128, 32], F32, "avec")
            self.cact = P.sb([128, 16], F32, "cact")
            P.dma(self.ident.t[:], I["ident"], writes=[self.ident.b])
            P.dma(self.vecs.t[:], I["vecs"], writes=[self.vecs.b])
            P.add("dve", lambda e: e.memset(self.ones.t[:], 1.0), [], [self.ones.b])
            P.dma(self.cact.t[:], I["cvec"], writes=[self.cact.b])
            P.act(self.cact.t[:], self.cact.t[:], AF.Silu, [self.cact.b], [self.cact.b])

            xin = {"x": I["xT"], "c": I["cT"]}
            for l in range(self.layers):
                last = l == DEPTH - 1
                self.phase_mod(l)
                for s in seqs:
                    n = s.name
                    xa, xb = S["xa_" + n], S["xb_" + n]
                    if s.which == 1 and last:
                        self.phase_proj(l, s, xin[n], groups=("k", "v"))
                        continue
                    self.phase_proj(l, s, xin[n], groups=None)
                    self.phase_filter(l, s)
                    self.phase_pool(l, s)
                    self.phase_fnet(l, s)
                    self.phase_hyena(l, s)
                    self.phase_attn(l, s)
                    self.phase_merge(l, s, xin[n], xa)
                    self.phase_ffn_up(l, s, xa)
                    self.phase_ffn_down(l, s, xa, xb)
                    xin[n] = xb
            if self.do_x and self.layers == DEPTH:
                self.phase_final(sx, xin["x"], outT)
            else:
                with P.phase():
                    pass
        P.emit()
        return self.nc

    def vcol(self, name, l=None, k=0):
        off = VOFF[name] + (0 if l is None else l * VLEN[name]) + k
        return self.vecs.t[:, off:off + 1]

    def phase_mod(self, l):
        P, I = self.P, self.I
        with P.phase():
            wr = P.ring(8, [128, 8, 128], F32, "wmod")
            pss = [self.psf() for _ in range(4)]
            wv = I["w_mod"][l].rearrange("(k p) n -> p k n", p=128)
            for q in range(12):
                jobs = []
                for r in range(4):
                    j = 4 * q + r
                    w = wr.next()
                    P.dma(w.t[:], wv[:, :, j * 128:(j + 1) * 128], writes=[w.b])
                    jobs.append((pss[r].t[:, 2 * q:2 * q + 2], pss[r].b,
                                 [(w.t[:, k, :], self.cact.t[:, 2 * k:2 * k + 2], [w.b, self.cact.b]) for k in range(8)]))
                self.mm_jobs(jobs)
            bm = VOFF["bmod"] + l * 96
            for r in range(4):
                P.tt("dve", self.modv.t[:].rearrange("p (q r w) -> p q r w", r=4, w=2)[:, :, r, :],
                     pss[r].t[:, 0:24].rearrange("p (q w) -> p q w", w=2),
                     self.vecs.t[:, bm:bm + 96].rearrange("p (q r w) -> p q r w", r=4, w=2)[:, :, r, :], ALU.add,
                     [pss[r].b, self.vecs.b], [self.modv.b])
            for i, (gname, scj) in enumerate((("n1g", 1), ("n2g", 4))):
                for k in range(8):
                    j = scj * 8 + k
                    P.ts("dve", self.avec.t[:, i * 16 + 2 * k:i * 16 + 2 * k + 2], self.modv.t[:, 2 * j:2 * j + 2],
                         1.0, self.vcol(gname, l, k), ALU.add, ALU.mult, [self.modv.b, self.vecs.b], [self.avec.b])

    def mcol(self, which_mod, k, w):
        j = which_mod * 8 + k
        return self.modv.t[:, 2 * j + w:2 * j + w + 1]

    def norm_block(self, s, xsrc, blk, a_fn, sh_fn, xr, sqr, rsr, tmpr, out_fn):
        P = self.P
        BS = s.BS
        xt = xr.next()
        P.dma(xt.t[:], xsrc.rearrange("(k p) t -> p k t", p=128)[:, :, blk * BS:(blk + 1) * BS], writes=[xt.b])
        sq = sqr.next()
        P.act(sq.t[:], xt.t[:], AF.Square, [xt.b], [sq.b])
        rs = rsr.next()
        H = BS // 2
        pss = [self.psf(), self.psf()]
        self.mm_jobs([(pss[h].t[:, :H], pss[h].b,
                       [(self.ones.t[:], sq.t[:, k, h * H:(h + 1) * H], [self.ones.b, sq.b]) for k in range(8)]) for h in range(2)])
        for h in range(2):
            P.act(rs.t[:, h * H:(h + 1) * H], pss[h].t[:, :H], AF.Sqrt, [pss[h].b, self.vecs.b], [rs.b],
                  bias=self.vcol("eps"), scale=1.0 / D)
        P.add("dve", lambda e: e.reciprocal(rs.t[:], rs.t[:]), [rs.b], [rs.b])
        for k in range(8):
            tmp = tmpr.next()
            P.stt("dve", tmp.t[:], xt.t[:, k, :], a_fn(k), rs.t[:], ALU.mult, ALU.mult,
                  [xt.b, rs.b, self.avec.b, self.vecs.b], [tmp.b])
            out_fn(k, tmp)

    def phase_proj(self, l, s, xsrc, groups):
        P, I, S = self.P, self.I, self.S
        n, L, BS, NB, w = s.name, s.L, s.BS, s.NB, s.which
        glist = [("pool", 0, 256, "FM", S["pool_" + n], F32, None),
                 ("fnet", 256, 256, "TM", S["fnet_" + n], BF16, None),
                 ("hy", 512, 768, "FM", S["hy_" + n], F32, None),
                 ("q", 1280, 256, "FM", S["q_" + n], BF16, "q"),
                 ("k", 1536, 256, "FM", S["k_" + n], BF16, None),
                 ("v", 1792, 256, "TM", S["v_" + n], BF16, None),
                 ("gate", 2048, 4096, "FM", S["gate_" + n], BF16, "sig")]
        if groups is not None:
            glist = [g for g in glist if g[0] in groups]
        with P.phase():
            hx = P.sb([128, 8, L], BF16, "hx")
            with P.phase():
                xr = P.ring(2, [128, 8, BS], F32, "xr")
                sqr = P.ring(2, [128, 8, BS], BF16, "sq")
                rsr = P.ring(2, [128, BS], F32, "rs")
                tmpr = P.ring(3, [128, BS], F32, "tmp")
                for blk in range(NB):
                    def out_fn(k, tmp, blk=blk):
                        P.act(hx.t[:, k, blk * BS:(blk + 1) * BS], tmp.t[:], AF.Identity, [tmp.b, self.modv.b], [hx.b],
                              bias=self.mcol(0, k, w), scale=1.0)
                    self.norm_block(s, xsrc, blk, lambda k: self.avec.t[:, 2 * k + w:2 * k + w + 1], None,
                                    xr, sqr, rsr, tmpr, out_fn)
            wst = P.ring(2, [128, 8, 256], F32, "wst")
            wbf = P.ring(2, [128, 8, 256], BF16, "wbf")
            ofm32 = P.ring(5, [128, BS], F32, "ofm32")
            ofm16 = P.ring(5, [128, BS], BF16, "ofm16")
            otm = P.ring(5, [128, 256], BF16, "otm")
            wv = I["w_in"][l].rearrange("(k p) n -> p k n", p=128)
            for (gname, c0, ncols, mode, dest, dt, special) in glist:
                for cb in range(ncols // 256):
                    cs = c0 + cb * 256
                    ws = wst.next()
                    P.dma(ws.t[:], wv[:, :, cs:cs + 256], writes=[ws.b])
                    wb = wbf.next()
                    P.cp("pool", wb.t[:], ws.t[:], [ws.b], [wb.b])
                    if mode == "FM":
                        items = [(cc, blk) for cc in range(2) for blk in range(NB)]
                        for i0 in range(0, len(items), 4):
                            grp = items[i0:i0 + 4]
                            pss = [self.psf() for _ in grp]
                            self.mm_jobs([(pss[i].t[:, :BS], pss[i].b,
                                           [(wb.t[:, k, cc * 128:(cc + 1) * 128], hx.t[:, k, blk * BS:(blk + 1) * BS], [wb.b, hx.b])
                                            for k in range(8)]) for i, (cc, blk) in enumerate(grp)])
                            for i, (cc, blk) in enumerate(grp):
                                ps = pss[i]
                                row0 = cb * 256 + cc * 128
                                o = (ofm32 if dt == F32 else ofm16).next()
                                if special == "sig":
                                    P.act(o.t[:], ps.t[:, :BS], AF.Sigmoid, [ps.b], [o.b])
                                elif special == "q":
                                    P.act(o.t[:], ps.t[:, :BS], AF.Copy, [ps.b], [o.b], scale=0.125)
                                else:
                                    P.cp(self.evac_eng(), o.t[:], ps.t[:, :BS], [ps.b], [o.b])
                                P.dma(dest[row0:row0 + 128, blk * BS:(blk + 1) * BS], o.t[:], reads=[o.b], q="pool")
                    else:
                        for t0 in range(0, s.NT, 4):
                            grp = list(range(t0, min(t0 + 4, s.NT)))
                            pss = [self.psf() for _ in grp]
                            self.mm_jobs([(pss[i].t[:, :256], pss[i].b,
                                           [(hx.t[:, k, tt * 128:(tt + 1) * 128], wb.t[:, k, :], [wb.b, hx.b]) for k in range(8)])
                                          for i, tt in enumerate(grp)])
                            for i, tt in enumerate(grp):
                                o = otm.next()
                                P.cp(self.evac_eng(), o.t[:], pss[i].t[:, :256], [pss[i].b], [o.b])
                                P.dma(dest[tt * 128:(tt + 1) * 128, cb * 256:(cb + 1) * 256], o.t[:], reads=[o.b], q="pool")

    def seq_transform(self, s, outs, handler, mring, KT=4):
        P = self.P
        BS, NB, NT = s.BS, s.NB, s.NT
        KT = min(KT, NT)
        for nb in range(NB):
            pss = [self.psf() for _ in outs]
            for g in range(NT // KT):
                loaded = {}
                for (tm, c0, mat) in outs:
                    if id(mat) not in loaded:
                        mt = mring.next()
                        P.dma(mt.t[:, :KT, :BS],
                              mat.rearrange("(g p) n -> p g n", p=128)[:, g * KT:(g + 1) * KT, nb * BS:(nb + 1) * BS],
                              writes=[mt.b])
                        loaded[id(mat)] = mt
                for kk in range(KT):
                    tt = g * KT + kk
                    for oi, (tm, c0, mat) in enumerate(outs):
                        mt = loaded[id(mat)]
                        P.mm(pss[oi].t[:, :BS], tm.t[:, tt, c0:c0 + 128], mt.t[:, kk, :BS], tt == 0, tt == NT - 1,
                             [tm.b, mt.b], [pss[oi].b])
            handler(nb, pss)

    def seq_transform2(self, outs, handler, mring, nt_in, blocks, KT=2):
        P = self.P
        KT = min(KT, nt_in)
        for bi, (b0, bs) in enumerate(blocks):
            pss = [self.psf() for _ in outs]
            for k0 in range(0, nt_in, KT):
                kn = min(KT, nt_in - k0)
                loaded = {}
                for (tm, c0, mat) in outs:
                    if id(mat) not in loaded:
                        mt = mring.next()
                        P.dma(mt.t[:, :kn, :bs], mat.rearrange("(g p) n -> p g n", p=128)[:, k0:k0 + kn, b0:b0 + bs],
                              writes=[mt.b])
                        loaded[id(mat)] = mt
                for kk in range(kn):
                    tt = k0 + kk
                    for oi, (tm, c0, mat) in enumerate(outs):
                        mt = loaded[id(mat)]
                        P.mm(pss[oi].t[:, :bs], tm.t[:, tt, c0:c0 + 128], mt.t[:, kk, :bs], tt == 0, tt == nt_in - 1,
                             [tm.b, mt.b], [pss[oi].b])
            handler(bi, b0, bs, pss)

    def fm_to_tm(self, src_tl, src_fn, nblk, tm, t0, c0):
        P = self.P
        j = 0
        while j < nblk:
            nn = min(16, nblk - j)
            nbk = min(4, nn)
            pbs = [self.psb() for _ in range(nbk)]
            for i in range(nn):
                pb = pbs[i % nbk]
                sl_ = i // nbk
                P.tr(pb.tb[:, sl_ * 128:(sl_ + 1) * 128], src_fn(j + i), self.ident.t[:], [src_tl.b, self.ident.b], [pb.b])
            for bi in range(nbk):
                cnt = len(range(bi, nn, nbk))
                dst = tm.t[:, t0 + j + bi:t0 + j + bi + (cnt - 1) * nbk + 1:nbk, c0:c0 + 128]
                P.cp(self.evac_eng(), dst, pbs[bi].tb[:, :cnt * 128].rearrange("p (a c) -> p a c", c=128), [pbs[bi].b], [tm.b])
            j += nn

    def phase_filter(self, l, s):
        P, I, S = self.P, self.I, self.S
        n, L, BS, NB, NT = s.name, s.L, s.BS, s.NB, s.NT
        N2 = 2 * L
        Lh = L // 2
        FB = Lh + 128
        NTh = Lh // 128
        fblocks = [(b0, min(512, FB - b0)) for b0 in range(0, FB, 512)]
        kf = S["kf_" + n]
        with P.phase():
            w3 = P.sb([64, 1024], F32, "fw3")
            h2 = P.sb([64, L], F32, "h2")
            P.dma(w3.t[:], I["fw3"][l], writes=[w3.b])
            with P.phase():
                ft = P.sb([33, L], F32, "feats")
                w1 = P.sb([33, 64], F32, "fw1")
                w2 = P.sb([64, 64], F32, "fw2")
                h1 = P.sb([64, L], F32, "h1")
                P.dma(ft.t[:], I["feats_" + n], writes=[ft.b])
                P.dma(w1.t[:], I["fw1"][l], writes=[w1.b])
                P.dma(w2.t[:], I["fw2"][l], writes=[w2.b])
                ar = P.ring(2, [64, BS], F32, "farg")
                mr = P.ring(2, [64, BS], F32, "fmask")
                for (src, wgt, kdim, dst, bname) in ((ft, w1, 33, h1, "fb1"), (h1, w2, 64, h2, "fb2")):
                    for blk in range(NB):
                        ps = self.psf()
                        P.mm(ps.t[:64, :BS], wgt.t[:kdim, :], src.t[:kdim, blk * BS:(blk + 1) * BS], True, True,
                             [wgt.b, src.b], [ps.b])
                        a = ar.next()
                        m = mr.next()
                        P.ts("dve", a.t[:], ps.t[:64, :BS], self.vecs.t[:64, VOFF["freq"] + l:VOFF["freq"] + l + 1],
                             self.vecs.t[:64, VOFF[bname] + l:VOFF[bname] + l + 1], ALU.mult, ALU.add,
                             [ps.b, self.vecs.b], [a.b])
                        for _ in range(2):
                            P.add("dve", lambda e, a=a, m=m: e.tensor_single_scalar(m.t[:], a.t[:], float(np.pi), ALU.is_gt),
                                  [a.b], [m.b])
                            P.stt("dve", a.t[:], m.t[:], float(-2 * np.pi), a.t[:], ALU.mult, ALU.add, [a.b, m.b], [a.b])
                            P.add("dve", lambda e, a=a, m=m: e.tensor_single_scalar(m.t[:], a.t[:], float(-np.pi), ALU.is_lt),
                                  [a.b], [m.b])
                            P.stt("dve", a.t[:], m.t[:], float(2 * np.pi), a.t[:], ALU.mult, ALU.add, [a.b, m.b], [a.b])
                        P.act(dst.t[:, blk * BS:(blk + 1) * BS], a.t[:], AF.Sin, [a.b], [dst.b])
            hT = P.sb([128, 2, L], F32, "hT")
            sd = P.sb([128, 2, L], BF16, "hsd")
            dec = P.sb([128, L], F32, "dec")
            tms = [P.sb([128, NTh, 256], BF16, "hftm") for _ in range(4)]
            sums = P.sb([128, 8], F32, "fsums")
            mring = P.ring(8, [128, 2, 512], BF16, "mring")
            kst = P.ring(12, [128, 512], F32, "kst")
            mec, mes, moc, mos = (I[k + n] for k in ("mec_", "mes_", "moc_", "mos_"))
            wsc = 2.0 / N2
            for o in range(2):
                for cc in range(2):
                    P.dma(dec.t[:], I["dec_" + n][cc * 128:(cc + 1) * 128, :], writes=[dec.b])
                    for d in range(2):
                        j = o * 4 + d * 2 + cc
                        for blk in range(NB):
                            ps = self.psf()
                            P.mm(ps.t[:, :BS], w3.t[:, j * 128:(j + 1) * 128], h2.t[:, blk * BS:(blk + 1) * BS], True, True,
                                 [w3.b, h2.b], [ps.b])
                            P.stt("dve", hT.t[:, d, blk * BS:(blk + 1) * BS], ps.t[:, :BS],
                                  self.vecs.t[:, VOFF["fb3"] + l * 8 + j:VOFF["fb3"] + l * 8 + j + 1],
                                  dec.t[:, blk * BS:(blk + 1) * BS], ALU.add, ALU.mult,
                                  [ps.b, dec.b, self.vecs.b], [hT.b])
                        P.add("dve", lambda e, d=d: e.tensor_reduce(sums.t[:, d:d + 1], hT.t[:, d, :], AX.X, ALU.add,
                                                                    apply_absolute_value=True), [hT.b], [sums.b])
                    P.tt("dve", sums.t[:, 2:3], sums.t[:, 0:1], sums.t[:, 1:2], ALU.add, [sums.b], [sums.b])
                    P.ts1("dve", sums.t[:, 2:3], sums.t[:, 2:3], EPS, ALU.add, [sums.b], [sums.b])
                    P.add("dve", lambda e: e.reciprocal(sums.t[:, 2:3], sums.t[:, 2:3]), [sums.b], [sums.b])
                    P.tt("dve", sums.t[:, 4 + cc:5 + cc], hT.t[:, 1, 0:1], sums.t[:, 2:3], ALU.mult, [hT.b, sums.b], [sums.b])
                    for d in range(2):
                        P.ts1("dve", hT.t[:, d, :], hT.t[:, d, :], sums.t[:, 2:3], ALU.mult, [hT.b, sums.b], [hT.b])
                    P.tt("pool", sd.t[:, 0, :], hT.t[:, 0, :], hT.t[:, 1, :], ALU.add, [hT.b], [sd.b])
                    P.tt("pool", sd.t[:, 1, :], hT.t[:, 0, :], hT.t[:, 1, :], ALU.subtract, [hT.b], [sd.b])
                    for q in range(2):
                        for e_ in range(2):
                            self.fm_to_tm(sd, lambda jj, q=q, e_=e_: sd.t[:, q, 256 * jj + e_:256 * (jj + 1):2], NTh,
                                          tms[2 * q + e_], 0, cc * 128)
                for cc in range(2):
                    outs = [(tms[0], cc * 128, mec), (tms[1], cc * 128, moc), (tms[2], cc * 128, mes), (tms[3], cc * 128, mos)]

                    def handler(bi, b0, bs, pss, o=o, cc=cc):
                        ec, oc, es, os_ = pss
                        hb0 = sums.t[:, 4 + cc:5 + cc]
                        toc, tos = kst.next(), kst.next()
                        P.cp("act", toc.t[:, :bs], oc.t[:, :bs], [oc.b], [toc.b])
                        P.cp("act", tos.t[:, :bs], os_.t[:, :bs], [os_.b], [tos.b])
                        ks = [kst.next() for _ in range(4)]
                        P.stt("dve", ks[0].t[:, :bs], ec.t[:, :bs], hb0, toc.t[:, :bs], ALU.subtract, ALU.add,
                              [ec.b, sums.b, toc.b], [ks[0].b])
                        P.tt("dve", ks[1].t[:, :bs], es.t[:, :bs], tos.t[:, :bs], ALU.add, [es.b, tos.b], [ks[1].b])
                        P.stt("dve", ks[2].t[:, :bs], ec.t[:, :bs], hb0, toc.t[:, :bs], ALU.subtract, ALU.subtract,
                              [ec.b, sums.b, toc.b], [ks[2].b])
                        P.tt("dve", ks[3].t[:, :bs], tos.t[:, :bs], es.t[:, :bs], ALU.subtract, [es.b, tos.b], [ks[3].b])
                        for q in range(4):
                            P.ts1("pool", ks[q].t[:, :bs], ks[q].t[:, :bs], wsc, ALU.mult, [ks[q].b], [ks[q].b])
                            if b0 == 0:
                                P.ts1("pool", ks[q].t[:, 0:1], ks[q].t[:, 0:1], 0.5, ALU.mult, [ks[q].b], [ks[q].b])
                            P.dma(kf[o, cc, q, :, b0:b0 + bs], ks[q].t[:, :bs], reads=[ks[q].b], q="pool")

                    self.seq_transform2(outs, handler, mring, NTh, fblocks)

    def phase_pool(self, l, s):
        P, I, S = self.P, self.I, self.S
        n, L, BS, NB = s.name, s.L, s.BS, s.NB
        LP = L + 32
        with P.phase():
            u = P.sb([128, 2, LP], F32, "pu")
            A = P.sb([128, 2, LP], F32, "pA")
            B = P.sb([128, 2, LP], F32, "pB")
            r = P.sb([128, 2, L], BF16, "pr")
            corr = P.sb([128, 2, 16], F32, "pcorr")
            pwf = P.sb([128, 2, 128], F32, "pwf")
            pwb = P.sb([128, 2, 128], BF16, "pwb")
            tmp8 = P.sb([128, 2, 8], F32, "ptmp")
            ost = P.ring(3, [128, BS], BF16, "post")
            P.add("pool", lambda e: e.memset(u.t[:], 0.0), [], [u.b])
            P.add("pool", lambda e: e.memset(pwf.t[:], 0.0), [], [pwf.b])
            P.dma(u.t[:, :, 16:16 + L], S["pool_" + n].rearrange("(c p) t -> p c t", p=128), reads=[], writes=[u.b])
            P.dma(corr.t[:], I["pcorr_" + n], writes=[corr.b])
            for g in range(4):
                hp = (g % 2) * 64
                P.dma(pwf.t[hp:hp + 64, g // 2, hp:hp + 64], I["pool_w"][l, g], writes=[pwf.b])
            P.cp("dve", pwb.t[:], pwf.t[:], [pwf.b], [pwb.b])
            for g in range(4):
                hp, c = (g % 2) * 64, g // 2
                sl = slice(hp, hp + 64)
                eng = "dve" if g % 2 == 0 else "pool"
                P.tt(eng, A.t[sl, c, 1:LP], u.t[sl, c, 0:LP - 1], u.t[sl, c, 1:LP], ALU.add, [u.b], [A.b])
                cur = A
                if g >= 1:
                    P.tt(eng, B.t[sl, c, 2:LP - 1], A.t[sl, c, 1:LP - 2], A.t[sl, c, 3:LP], ALU.add, [A.b], [B.b])
                    cur = B
                if g >= 2:
                    P.tt(eng, A.t[sl, c, 4:LP - 3], B.t[sl, c, 2:LP - 5], B.t[sl, c, 6:LP - 1], ALU.add, [B.b], [A.b])
                    cur = A
                if g >= 3:
                    P.tt(eng, B.t[sl, c, 8:LP - 7], A.t[sl, c, 4:LP - 11], A.t[sl, c, 12:LP - 3], ALU.add, [A.b], [B.b])
                    cur = B
                wdt = (2, 4, 8, 16)[g]
                P.stt(eng, r.t[sl, c, :], cur.t[sl, c, 16:16 + L], 1.0 / wdt, u.t[sl, c, 16:16 + L], ALU.mult, ALU.subtract,
                      [cur.b, u.b], [r.b])
                for (c0, k0) in ((0, 0), (L - 8, 8)):
                    P.tt(eng, tmp8.t[sl, c, :], cur.t[sl, c, 16 + c0:16 + c0 + 8], corr.t[sl, c, k0:k0 + 8], ALU.mult,
                         [cur.b, corr.b], [tmp8.b])
                    P.tt(eng, r.t[sl, c, c0:c0 + 8], tmp8.t[sl, c, :], u.t[sl, c, 16 + c0:16 + c0 + 8], ALU.subtract,
                         [tmp8.b, u.b, r.b], [r.b])
            for c in range(2):
                for blk in range(NB):
                    ps = self.psf()
                    P.mm(ps.t[:, :BS], pwb.t[:, c, :], r.t[:, c, blk * BS:(blk + 1) * BS], True, True, [pwb.b, r.b], [ps.b])
                    o = ost.next()
                    P.act(o.t[:], ps.t[:, :BS], AF.Copy, [ps.b, self.vecs.b], [o.b], scale=self.vcol("pscale", l, c))
                    P.dma(S["ybr_" + n][0, c * 128:(c + 1) * 128, blk * BS:(blk + 1) * BS], o.t[:], reads=[o.b], q="pool")

    def phase_fnet(self, l, s):
        P, I, S = self.P, self.I, self.S
        n, L, BS, NB, NT = s.name, s.L, s.BS, s.NB, s.NT
        with P.phase():
            utm = P.sb([128, NT, 256], BF16, "futm")
            cd = P.sb([128, 256], BF16, "chdft")
            P.dma(utm.t[:], S["fnet_" + n].rearrange("(g p) c -> p g c", p=128), writes=[utm.b])
            P.dma(cd.t[:], I["chdft_" + n], writes=[cd.b])
            mring = P.ring(4, [128, 4, BS], BF16, "mring")
            ab = P.ring(8, [128, BS], BF16, "fab")
            ost = P.ring(3, [128, BS], BF16, "fost")
            fc, fs = I["fc_" + n], I["fs_" + n]
            outs = [(utm, 0, fc), (utm, 0, fs), (utm, 128, fc), (utm, 128, fs)]

            def handler(nb, pss):
                tiles = []
                for c in range(2):
                    a_c, a_s = ab.next(), ab.next()
                    P.cp("act", a_c.t[:], pss[2 * c].t[:, :BS], [pss[2 * c].b], [a_c.b])
                    P.cp("dve", a_s.t[:], pss[2 * c + 1].t[:, :BS], [pss[2 * c + 1].b], [a_s.b])
                    tiles.append((a_c, a_s))
                ps2 = [self.psf(), self.psf()]
                self.mm_jobs([(ps2[c].t[:, :BS], ps2[c].b,
                               [(cd.t[:, 0:128], tiles[c][0].t[:], [cd.b, tiles[c][0].b]),
                                (cd.t[:, 128:256], tiles[c][1].t[:], [cd.b, tiles[c][1].b])]) for c in range(2)])
                for c in range(2):
                    o = ost.next()
                    P.cp(self.evac_eng(), o.t[:], ps2[c].t[:, :BS], [ps2[c].b], [o.b])
                    P.dma(S["ybr_" + n][1, c * 128:(c + 1) * 128, nb * BS:(nb + 1) * BS], o.t[:], reads=[o.b], q="pool")

            self.seq_transform(s, outs, handler, mring)

    def phase_hyena(self, l, s):
        P, I, S = self.P, self.I, self.S
        n, L, BS, NB, NT = s.name, s.L, s.BS, s.NB, s.NT
        with P.phase():
            v = P.sb([128, 2, L], F32, "hv")
            xg = P.sb([128, 2, L], BF16, "hxg")
            x2st = P.ring(2, [128, L], BF16, "hx2st")
            with P.phase():
                ur = P.ring(2, [128, L + 2], F32, "hur")
                tr_ = P.ring(2, [128, L], F32, "htr")
                uv = S["hy_" + n]
                for c in range(6):
                    ut = ur.next()
                    P.add("pool", lambda e, ut=ut: e.memset(ut.t[:, 0:1], 0.0), [], [ut.b])
                    P.add("pool", lambda e, ut=ut: e.memset(ut.t[:, L + 1:L + 2], 0.0), [], [ut.b])
                    P.dma(ut.t[:, 1:L + 1], uv[c * 128:(c + 1) * 128, :], writes=[ut.b])
                    t1 = tr_.next()
                    eng = "dve" if c % 2 == 0 else "pool"
                    P.ts(eng, t1.t[:], ut.t[:, 0:L], self.vcol("hcw", l, 0 * 6 + c), self.vcol("hcb", l, c), ALU.mult, ALU.add,
                         [ut.b, self.vecs.b], [t1.b])
                    P.stt(eng, t1.t[:], ut.t[:, 1:L + 1], self.vcol("hcw", l, 1 * 6 + c), t1.t[:], ALU.mult, ALU.add,
                          [ut.b, t1.b, self.vecs.b], [t1.b])
                    if c < 2:
                        dst, dstb, x2t = v.t[:, c, :], v.b, None
                    elif c < 4:
                        dst, dstb, x2t = xg.t[:, c - 2, :], xg.b, None
                    else:
                        x2t = x2st.next()
                        dst, dstb = x2t.t[:], x2t.b
                    P.stt(eng, dst, ut.t[:, 2:L + 2], self.vcol("hcw", l, 2 * 6 + c), t1.t[:], ALU.mult, ALU.add,
                          [ut.b, t1.b, self.vecs.b], [dstb])
                    if x2t is not None:
                        P.dma(S["x2_" + n][(c - 4) * 128:(c - 3) * 128, :], x2t.t[:], reads=[x2t.b], q="pool")
            Lh = L // 2
            FB = Lh + 128
            NTh = Lh // 128
            NFT = FB // 128
            fblocks = [(b0, min(512, FB - b0)) for b0 in range(0, FB, 512)]
            tblocks = [(b0, min(512, Lh - b0)) for b0 in range(0, Lh, 512)]
            mec, mes, moc, mos = (I[k + n] for k in ("mec_", "mes_", "moc_", "mos_"))
            iec, ies, ioc, ios = (I[k + n] for k in ("iec_", "ies_", "ioc_", "ios_"))
            kfv = S["kf_" + n]
            for o in range(2):
                with P.phase():
                    zt = [P.sb([128, NTh, 256], BF16, "zteo") for _ in range(2)]
                    with P.phase():
                        zbf = P.sb([128, 2, L], BF16, "zbf")
                        for c in range(2):
                            P.cp("act" if c == 0 else "dve", zbf.t[:, c, :], v.t[:, c, :], [v.b], [zbf.b])
                            for e_ in range(2):
                                self.fm_to_tm(zbf, lambda jj, c=c, e_=e_: zbf.t[:, c, 256 * jj + e_:256 * (jj + 1):2], NTh,
                                              zt[e_], 0, c * 128)
                        if o == 1:
                            P.dma(xg.t[:], S["x2_" + n].rearrange("(c p) t -> p c t", p=128), writes=[xg.b])
                    pq = [P.sb([128, NFT, 256], BF16, "pqtm") for _ in range(4)]
                    mring = P.ring(8, [128, 2, 512], BF16, "mring")
                    kr_ = P.ring(2, [128, 4, 512], F32, "kfr")
                    t4 = P.ring(12, [128, 512], F32, "ht4")
                    zb = P.ring(8, [128, 512], BF16, "hzb")
                    yb = P.ring(8, [128, 512], BF16, "hyb")
                    ost = P.ring(3, [128, 1024], BF16, "host")
                    for c in range(2):
                        outs = [(zt[0], c * 128, mec), (zt[0], c * 128, mes), (zt[1], c * 128, moc), (zt[1], c * 128, mos)]

                        def handler(bi, b0, bs, pss, o=o, c=c):
                            er, ei, or_, oi = pss
                            kt = kr_.next()
                            P.dma(kt.t[:, :, :bs], kfv[o, c].rearrange("q p t -> p q t")[:, :, b0:b0 + bs], writes=[kt.b])
                            tor, toi = t4.next(), t4.next()
                            P.cp("act", tor.t[:, :bs], or_.t[:, :bs], [or_.b], [tor.b])
                            P.cp("act", toi.t[:, :bs], oi.t[:, :bs], [oi.b], [toi.b])
                            zrl, zil, zrh, zih = zb.next(), zb.next(), zb.next(), zb.next()
                            P.tt("dve", zrl.t[:, :bs], er.t[:, :bs], tor.t[:, :bs], ALU.add, [er.b, tor.b], [zrl.b])
                            P.tt("dve", zil.t[:, :bs], ei.t[:, :bs], toi.t[:, :bs], ALU.add, [ei.b, toi.b], [zil.b])
                            P.tt("dve", zrh.t[:, :bs], er.t[:, :bs], tor.t[:, :bs], ALU.subtract, [er.b, tor.b], [zrh.b])
                            P.tt("dve", zih.t[:, :bs], toi.t[:, :bs], ei.t[:, :bs], ALU.subtract, [ei.b, toi.b], [zih.b])
                            ys = []
                            for hi_, (zr, zi) in enumerate(((zrl, zil), (zrh, zih))):
                                krr, kii = kt.t[:, 2 * hi_, :bs], kt.t[:, 2 * hi_ + 1, :bs]
                                a1, a2, a3, a4 = t4.next(), t4.next(), t4.next(), t4.next()
                                P.tt("dve", a1.t[:, :bs], zr.t[:, :bs], krr, ALU.mult, [zr.b, kt.b], [a1.b])
                                P.tt("pool", a2.t[:, :bs], zi.t[:, :bs], kii, ALU.mult, [zi.b, kt.b], [a2.b])
                                P.tt("dve", a3.t[:, :bs], zr.t[:, :bs], kii, ALU.mult, [zr.b, kt.b], [a3.b])
                                P.tt("pool", a4.t[:, :bs], zi.t[:, :bs], krr, ALU.mult, [zi.b, kt.b], [a4.b])
                                P.tt("dve", a1.t[:, :bs], a1.t[:, :bs], a2.t[:, :bs], ALU.subtract, [a1.b, a2.b], [a1.b])
                                P.tt("pool", a3.t[:, :bs], a3.t[:, :bs], a4.t[:, :bs], ALU.add, [a3.b, a4.b], [a3.b])
                                ys.append((a1, a3))
                            (yrl, yil), (yrh, yih) = ys
                            outs_ = [yb.next() for _ in range(4)]
                            P.tt("dve", outs_[0].t[:, :bs], yrl.t[:, :bs], yrh.t[:, :bs], ALU.add, [yrl.b, yrh.b], [outs_[0].b])
                            P.tt("pool", outs_[1].t[:, :bs], yil.t[:, :bs], yih.t[:, :bs], ALU.subtract, [yil.b, yih.b], [outs_[1].b])
                            P.tt("dve", outs_[2].t[:, :bs], yrl.t[:, :bs], yrh.t[:, :bs], ALU.subtract, [yrl.b, yrh.b], [outs_[2].b])
                            P.tt("pool", outs_[3].t[:, :bs], yil.t[:, :bs], yih.t[:, :bs], ALU.add, [yil.b, yih.b], [outs_[3].b])
                            nj = bs // 128
                            for q in range(4):
                                self.fm_to_tm(outs_[q], lambda jj, t=outs_[q]: t.t[:, jj * 128:(jj + 1) * 128], nj, pq[q],
                                              b0 // 128, c * 128)

                        self.seq_transform2(outs, handler, mring, NTh, fblocks)
                    for c in range(2):
                        outs2 = [(pq[0], c * 128, iec), (pq[1], c * 128, ies), (pq[2], c * 128, ioc), (pq[3], c * 128, ios)]

                        def handler2(bi, b0, bs, pss, o=o, c=c):
                            ot = ost.next() if o == 1 else None
                            for e_ in range(2):
                                pa, pb_ = pss[2 * e_], pss[2 * e_ + 1]
                                tb_ = t4.next()
                                P.cp("act", tb_.t[:, :bs], pb_.t[:, :bs], [pb_.b], [tb_.b])
                                tm = t4.next()
                                vsl = v.t[:, c, 2 * b0 + e_:2 * (b0 + bs):2]
                                P.stt("dve", tm.t[:, :bs], vsl, self.vcol("hskip", l, o * 2 + c), pa.t[:, :bs],
                                      ALU.mult, ALU.add, [v.b, pa.b, self.vecs.b], [tm.b])
                                P.tt("pool", tm.t[:, :bs], tm.t[:, :bs], tb_.t[:, :bs], ALU.add, [tm.b, tb_.b], [tm.b])
                                gsl = xg.t[:, c, 2 * b0 + e_:2 * (b0 + bs):2]
                                if o == 0:
                                    P.tt("dve", vsl, tm.t[:, :bs], gsl, ALU.mult, [tm.b, xg.b, v.b], [v.b])
                                else:
                                    P.tt("dve", ot.t[:, e_:2 * bs:2], tm.t[:, :bs], gsl, ALU.mult, [tm.b, xg.b, ot.b], [ot.b])
                            if o == 1:
                                P.dma(S["ybr_" + n][2, c * 128:(c + 1) * 128, 2 * b0:2 * (b0 + bs)], ot.t[:, :2 * bs],
                                      reads=[ot.b], q="pool")

                        self.seq_transform2(outs2, handler2, mring, NFT, tblocks)

    def phase_attn(self, l, s):
        P, I, S = self.P, self.I, self.S
        n, L = s.name, s.L
        grid = s.which == 0
        with P.phase():
            qT = P.sb([128, 2, L], BF16, "qT")
            kcT = P.sb([128, 2, LC], BF16, "kcT")
            vc = P.sb([128, 2, 256], BF16, "vc")
            yT = P.sb([128, 2, L], BF16, "yT")
            P.dma(qT.t[:], S["q_" + n].rearrange("(c p) t -> p c t", p=128), writes=[qT.b])
            P.dma(kcT.t[:], S["k_c"].rearrange("(c p) t -> p c t", p=128), writes=[kcT.b])
            P.dma(vc.t[:], S["v_c"][0:LC, :].rearrange("(g p) c -> p g c", p=128), writes=[vc.b])
            if grid:
                NT = s.NT
                kT = P.sb([128, 2, L], BF16, "kT")
                ve = P.sb([128, NT, 256], BF16, "ve")
                bias = P.sb([128, 4, 5, 576], F32, "bias")
                P.dma(kT.t[:], S["k_" + n].rearrange("(c p) t -> p c t", p=128), writes=[kT.b])
                P.dma(ve.t[:], S["v_" + n][0:L, :].rearrange("(g p) c -> p g c", p=128), writes=[ve.b])
                P.dma(bias.t[:], I["bias"][l], writes=[bias.b])
            sr = P.ring(6, [128, 832], F32, "as")
            pr = P.ring(9, [128, 832], BF16, "ap")
            ptr_ = P.ring(8, [128, 7, 128], BF16, "apT")
            st = P.ring(12, [128, 4], F32, "astat")
            orow = P.ring(3, [128, 256], BF16, "aorow")
            npair = L // 128

            def stage_a(pi):
                r = 2 * pi
                if grid:
                    r0a = min(max(r - 4, 0), 56)
                    r0b = min(max(r - 3, 0), 56)
                    nine = r0b != r0a
                    pat = {0: 1, 2: 2, 60: 3, 62: 4}.get(r, 0)
                    nnb = 576 if nine else 512
                else:
                    r0a, nine, nnb = 0, False, 0
                nk = nnb + 256
                heads = []
                for h in range(4):
                    hp, hc = (h % 2) * 64, h // 2
                    hs = slice(hp, hp + 64)
                    sb_ = sr.next()
                    stt_ = st.next()
                    qa = qT.t[hs, hc, r * 64:r * 64 + 128]
                    p2 = self.psf()
                    if grid:
                        p1 = self.psf()
                        P.mm(p1.t[:, :512], qa, kT.t[hs, hc, r0a * 64:r0a * 64 + 512], True, True, [qT.b, kT.b], [p1.b])
                        P.mm(p2.t[:, 64:320], qa, kcT.t[hs, hc, :], True, True, [qT.b, kcT.b], [p2.b])
                        if nine:
                            P.mm(p2.t[:, 0:64], qa, kT.t[hs, hc, r0a * 64 + 512:r0a * 64 + 576], True, True, [qT.b, kT.b], [p2.b])
                        P.tt("dve", sb_.t[:, 0:512], p1.t[:, :512], bias.t[:, h, pat, 0:512], ALU.add, [p1.b, bias.b], [sb_.b])
                        if nine:
                            P.tt("dve", sb_.t[:, 512:576], p2.t[:, 0:64], bias.t[:, h, pat, 512:576], ALU.add, [p2.b, bias.b], [sb_.b])
                    else:
                        P.mm(p2.t[:, 64:320], qa, kcT.t[hs, hc, :], True, True, [qT.b, kcT.b], [p2.b])
                    P.cp("act", sb_.t[:, nnb:nnb + 256], p2.t[:, 64:320], [p2.b], [sb_.b])
                    P.add("dve", lambda e, sb_=sb_, stt_=stt_, nk=nk: e.reduce_max(stt_.t[:, 0:1], sb_.t[:, :nk], AX.X),
                          [sb_.b], [stt_.b])
                    P.ts1("dve", stt_.t[:, 1:2], stt_.t[:, 0:1], -1.0, ALU.mult, [stt_.b], [stt_.b])
                    P.add("pool", lambda e, stt_=stt_: e.memset(stt_.t[:, 2:3], 0.0), [stt_.b], [stt_.b])
                    pb_ = pr.next()
                    P.act(pb_.t[:, :nk], sb_.t[:, :nk], AF.Exp, [sb_.b, stt_.b], [pb_.b, stt_.b],
                          bias=stt_.t[:, 1:2], scale=1.0, accum_out=stt_.t[:, 2:3])
                    P.add("dve", lambda e, stt_=stt_: e.reciprocal(stt_.t[:, 3:4], stt_.t[:, 2:3]), [stt_.b], [stt_.b])
                    heads.append((pb_, stt_))
                return (r, r0a, nine, nnb, heads)

            def stage_b(state):
                r, r0a, nine, nnb, heads = state
                chunks = []
                if grid:
                    for j in range(4):
                        chunks.append((j * 128, 128, "nb", j))
                    if nine:
                        chunks.append((512, 64, "nb", 4))
                chunks.append((nnb, 128, "cx", 0))
                chunks.append((nnb + 128, 128, "cx", 1))
                nj = len(chunks)
                ot = orow.next()
                pbks = [self.psb() for _ in range(4)]
                for j, (c0, ncol, kind, idx) in enumerate(chunks):
                    for h in range(4):
                        pb_ = heads[h][0]
                        P.tr(pbks[h].tb[:ncol, j * 128:(j + 1) * 128], pb_.t[:, c0:c0 + ncol], self.ident.t[:],
                             [pb_.b, self.ident.b], [pbks[h].b])
                pTs = []
                for h in range(4):
                    pT = ptr_.next()
                    eng = "dve" if h % 2 == 0 else "act"
                    if nine:
                        P.cp(eng, pT.t[:, 0:4, :], pbks[h].tb[:, 0:512].rearrange("p (a c) -> p a c", c=128), [pbks[h].b], [pT.b])
                        P.cp(eng, pT.t[:64, 4, :], pbks[h].tb[:64, 512:640], [pbks[h].b], [pT.b])
                        P.cp(eng, pT.t[:, 5:7, :], pbks[h].tb[:, 640:896].rearrange("p (a c) -> p a c", c=128), [pbks[h].b], [pT.b])
                    else:
                        P.cp(eng, pT.t[:, :nj, :], pbks[h].tb[:, :nj * 128].rearrange("p (a c) -> p a c", c=128),
                             [pbks[h].b], [pT.b])
                    pTs.append(pT)
                pos = [self.psf() for _ in range(4)]
                jobs = []
                for h in range(4):
                    steps = []
                    for j, (c0, ncol, kind, idx) in enumerate(chunks):
                        if kind == "nb":
                            vt, vb = ve.t[:ncol, r0a // 2 + idx, h * 64:(h + 1) * 64], ve.b
                        else:
                            vt, vb = vc.t[:, idx, h * 64:(h + 1) * 64], vc.b
                        steps.append((pTs[h].t[:ncol, j, :], vt, [pTs[h].b, vb]))
                    jobs.append((pos[h].t[:, :64], pos[h].b, steps))
                self.mm_jobs(jobs)
                for h in range(4):
                    stt_ = heads[h][1]
                    P.act(ot.t[:, h * 64:(h + 1) * 64], pos[h].t[:, :64], AF.Copy, [pos[h].b, stt_.b], [ot.b], scale=stt_.t[:, 3:4])
                pbk2 = [self.psb(), self.psb()]
                for c in range(2):
                    P.tr(pbk2[c].tb[:, 0:128], ot.t[:, c * 128:(c + 1) * 128], self.ident.t[:],
                         [ot.b, self.ident.b], [pbk2[c].b])
                    P.cp("dve" if c == 0 else "act", yT.t[:, c, r * 64:r * 64 + 128], pbk2[c].tb[:, :128], [pbk2[c].b], [yT.b])

            prev = stage_a(0)
            for pi in range(1, npair):
                cur = stage_a(pi)
                stage_b(prev)
                prev = cur
            stage_b(prev)
            P.dma(S["ybr_" + n][3].rearrange("(c p) t -> p c t", p=128), yT.t[:], reads=[yT.b], q="pool")

    def phase_merge(self, l, s, xsrc, xdst):
        P, I, S = self.P, self.I, self.S
        n, L, BS, NB, w = s.name, s.L, s.BS, s.NB, s.which
        with P.phase():
            wbr = P.sb([128, 8, 1024], BF16, "wbr")
            wo = P.sb([128, 8, 1024], BF16, "wo")
            wst = P.ring(2, [128, 2, 1024], F32, "mwst")
            wbv = I["w_branch"][l].rearrange("b (k p) n -> p (b k) n", p=128)
            wov = I["w_out"][l].rearrange("(k p) n -> p k n", p=128)
            for i in range(4):
                ws = wst.next()
                P.dma(ws.t[:], wbv[:, 2 * i:2 * i + 2, :], writes=[ws.b])
                P.cp("pool", wbr.t[:, 2 * i:2 * i + 2, :], ws.t[:], [ws.b], [wbr.b])
            for i in range(4):
                ws = wst.next()
                P.dma(ws.t[:], wov[:, 2 * i:2 * i + 2, :], writes=[ws.b])
                P.cp("pool", wo.t[:, 2 * i:2 * i + 2, :], ws.t[:], [ws.b], [wo.b])
            ybr = P.ring(2, [128, 8, BS], BF16, "mybr")
            gr = P.ring(3, [128, 4, BS], BF16, "mgate")
            tb = P.ring(8, [128, BS], F32, "mtb")
            mg = P.ring(2, [128, 8, BS], BF16, "mmg")
            xr = P.ring(5, [128, BS], F32, "mxr")
            xo = P.ring(5, [128, BS], F32, "mxo")
            yv = S["ybr_" + n].rearrange("b (k p) t -> p (b k) t", p=128)
            gv = S["gate_" + n].rearrange("(b c p) t -> p b c t", p=128, c=8)
            for blk in range(NB):
                sl = slice(blk * BS, (blk + 1) * BS)
                yt = ybr.next()
                P.dma(yt.t[:], yv[:, :, sl], writes=[yt.b])
                m = mg.next()
                for dc in range(8):
                    g = gr.next()
                    P.dma(g.t[:], gv[:, :, dc, sl], writes=[g.b])
                    ts_ = []
                    pss = [self.psf() for _ in range(4)]
                    self.mm_jobs([(pss[b].t[:, :BS], pss[b].b,
                                   [(wbr.t[:, 2 * b + kc, dc * 128:(dc + 1) * 128], yt.t[:, 2 * b + kc, :], [wbr.b, yt.b])
                                    for kc in range(2)]) for b in range(4)])
                    for b in range(4):
                        t = tb.next()
                        P.tt("dve", t.t[:], pss[b].t[:, :BS], g.t[:, b, :], ALU.mult, [pss[b].b, g.b], [t.b])
                        ts_.append(t)
                    P.tt("pool", ts_[0].t[:], ts_[0].t[:], ts_[1].t[:], ALU.add, [ts_[0].b, ts_[1].b], [ts_[0].b])
                    P.tt("pool", ts_[2].t[:], ts_[2].t[:], ts_[3].t[:], ALU.add, [ts_[2].b, ts_[3].b], [ts_[2].b])
                    P.tt("pool", m.t[:, dc, :], ts_[0].t[:], ts_[2].t[:], ALU.add, [ts_[0].b, ts_[2].b], [m.b])
                for d0 in (0, 4):
                    pss = [self.psf() for _ in range(4)]
                    self.mm_jobs([(pss[i].t[:, :BS], pss[i].b,
                                   [(wo.t[:, k, (d0 + i) * 128:(d0 + i + 1) * 128], m.t[:, k, :], [wo.b, m.b]) for k in range(8)])
                                  for i in range(4)])
                    for i in range(4):
                        dc = d0 + i
                        xt = xr.next()
                        P.dma(xt.t[:], xsrc[dc * 128:(dc + 1) * 128, sl], writes=[xt.b])
                        o = xo.next()
                        P.stt("dve", o.t[:], pss[i].t[:, :BS], self.mcol(2, dc, w), xt.t[:], ALU.mult, ALU.add,
                              [pss[i].b, xt.b, self.modv.b], [o.b])
                        P.dma(xdst[dc * 128:(dc + 1) * 128, sl], o.t[:], reads=[o.b], q="pool")

    def phase_ffn_up(self, l, s, xsrc):
        P, I, S = self.P, self.I, self.S
        n, L, BS, NB, w = s.name, s.L, s.BS, s.NB, s.which
        with P.phase():
            hx = P.sb([128, 8, L], BF16, "hx2")
            with P.phase():
                xr = P.ring(2, [128, 8, BS], F32, "xr")
                sqr = P.ring(2, [128, 8, BS], BF16, "sq")
                rsr = P.ring(2, [128, BS], F32, "rs")
                tmpr = P.ring(3, [128, BS], F32, "tmp")
                for blk in range(NB):
                    def out_fn(k, tmp, blk=blk):
                        P.act(hx.t[:, k, blk * BS:(blk + 1) * BS], tmp.t[:], AF.Identity, [tmp.b, self.modv.b], [hx.b],
                              bias=self.mcol(3, k, w), scale=1.0)
                    self.norm_block(s, xsrc, blk, lambda k: self.avec.t[:, 16 + 2 * k + w:16 + 2 * k + w + 1], None,
                                    xr, sqr, rsr, tmpr, out_fn)
            wst = P.ring(2, [128, 8, 256], F32, "fwst")
            wbf = P.ring(3, [128, 8, 256], BF16, "fwbf")
            ub = P.ring(2, [128, L], BF16, "fub")
            gb = P.ring(2, [128, L + 2], F32, "fgb")
            t1r = P.ring(1, [128, L], F32, "ft1")
            ar = P.ring(2, [128, L], BF16, "far")
            wv = I["ffn_w_up"][l].rearrange("(k p) n -> p k n", p=128)

            def load_w(j):
                ws = wst.next()
                P.dma(ws.t[:, :, 0:128], wv[:, :, j * 128:(j + 1) * 128], writes=[ws.b])
                P.dma(ws.t[:, :, 128:256], wv[:, :, FF + j * 128:FF + (j + 1) * 128], writes=[ws.b])
                wb_ = wbf.next()
                P.cp("pool", wb_.t[:], ws.t[:], [ws.b], [wb_.b])
                return wb_

            wq = [load_w(0), load_w(1)]
            for j in range(NFF):
                wb = wq[j]
                u = ub.next()
                g = gb.next()
                P.add("pool", lambda e, g=g: e.memset(g.t[:, 0:1], 0.0), [], [g.b])
                P.add("pool", lambda e, g=g: e.memset(g.t[:, L + 1:L + 2], 0.0), [], [g.b])
                for b0 in range(0, NB, 2):
                    blks = list(range(b0, min(b0 + 2, NB)))
                    pss = [(self.psf(), self.psf()) for _ in blks]
                    jobs = []
                    for i, blk in enumerate(blks):
                        sl = slice(blk * BS, (blk + 1) * BS)
                        jobs.append((pss[i][0].t[:, :BS], pss[i][0].b, [(wb.t[:, k, 0:128], hx.t[:, k, sl], [wb.b, hx.b]) for k in range(8)]))
                        jobs.append((pss[i][1].t[:, :BS], pss[i][1].b, [(wb.t[:, k, 128:256], hx.t[:, k, sl], [wb.b, hx.b]) for k in range(8)]))
                    self.mm_jobs(jobs)
                    for i, blk in enumerate(blks):
                        sl = slice(blk * BS, (blk + 1) * BS)
                        pu, pg = pss[i]
                        P.cp("dve", u.t[:, sl], pu.t[:, :BS], [pu.b], [u.b])
                        P.cp("act", g.t[:, 1 + blk * BS:1 + (blk + 1) * BS], pg.t[:, :BS], [pg.b], [g.b])
                if j + 2 < NFF:
                    wq.append(load_w(j + 2))
                t1 = t1r.next()
                eng = "pool"
                P.ts(eng, t1.t[:], g.t[:, 0:L], self.vcol("fcw", l, 0 * NFF + j), self.vcol("fcb", l, j), ALU.mult, ALU.add,
                     [g.b, self.vecs.b], [t1.b])
                P.stt(eng, t1.t[:], g.t[:, 1:L + 1], self.vcol("fcw", l, 1 * NFF + j), t1.t[:], ALU.mult, ALU.add,
                      [g.b, t1.b, self.vecs.b], [t1.b])
                P.stt("dve", t1.t[:], g.t[:, 2:L + 2], self.vcol("fcw", l, 2 * NFF + j), t1.t[:], ALU.mult, ALU.add,
                      [g.b, t1.b, self.vecs.b], [t1.b])
                P.act(t1.t[:], t1.t[:], AF.Silu, [t1.b], [t1.b])
                a = ar.next()
                P.tt("dve", a.t[:], t1.t[:], u.t[:], ALU.mult, [t1.b, u.b], [a.b])
                P.dma(S["aT_" + n][j * 128:(j + 1) * 128, :], a.t[:], reads=[a.b], q="pool")

    def phase_ffn_down(self, l, s, xsrc, xdst):
        P, I, S = self.P, self.I, self.S
        n, L, BS, NB, w = s.name, s.L, s.BS, s.NB, s.which
        with P.phase():
            wd = P.sb([128, NFF, 1024], BF16, "wd")
            wst = P.ring(2, [128, 2, 1024], F32, "dwst")
            wv = I["ffn_w_down"][l].rearrange("(k p) n -> p k n", p=128)
            for i in range(NFF // 2):
                ws = wst.next()
                P.dma(ws.t[:], wv[:, 2 * i:2 * i + 2, :], writes=[ws.b])
                P.cp("pool", wd.t[:, 2 * i:2 * i + 2, :], ws.t[:], [ws.b], [wd.b])
            ar = P.ring(2, [128, NFF, BS], BF16, "dar")
            xr = P.ring(5, [128, BS], F32, "dxr")
            xo = P.ring(5, [128, BS], F32, "dxo")
            av = S["aT_" + n].rearrange("(k p) t -> p k t", p=128)
            for blk in range(NB):
                sl = slice(blk * BS, (blk + 1) * BS)
                a = ar.next()
                P.dma(a.t[:], av[:, :, sl], writes=[a.b])
                for d0 in (0, 4):
                    pss = [self.psf() for _ in range(4)]
                    self.mm_jobs([(pss[i].t[:, :BS], pss[i].b,
                                   [(wd.t[:, k, (d0 + i) * 128:(d0 + i + 1) * 128], a.t[:, k, :], [wd.b, a.b]) for k in range(NFF)])
                                  for i in range(4)])
                    for i in range(4):
                        dc = d0 + i
                        xt = xr.next()
                        P.dma(xt.t[:], xsrc[dc * 128:(dc + 1) * 128, sl], writes=[xt.b])
                        o = xo.next()
                        P.stt("dve", o.t[:], pss[i].t[:, :BS], self.mcol(5, dc, w), xt.t[:], ALU.mult, ALU.add,
                              [pss[i].b, xt.b, self.modv.b], [o.b])
                        P.dma(xdst[dc * 128:(dc + 1) * 128, sl], o.t[:], reads=[o.b], q="pool")

    def phase_final(self, s, xsrc, outT):
        P = self.P
        BS, NB = s.BS, s.NB
        with P.phase():
            xr = P.ring(2, [128, 8, BS], F32, "xr")
            sqr = P.ring(2, [128, 8, BS], BF16, "sq")
            rsr = P.ring(2, [128, BS], F32, "rs")
            tmpr = P.ring(4, [128, BS], F32, "tmp")
            for blk in range(NB):
                def out_fn(k, tmp, blk=blk):
                    P.dma(outT[k * 128:(k + 1) * 128, blk * BS:(blk + 1) * BS], tmp.t[:], reads=[tmp.b], q="pool")
                self.norm_block(s, xsrc, blk, lambda k: self.vcol("fng", None, k), None, xr, sqr, rsr, tmpr, out_fn)


VSPEC = [("eps", 1, 1), ("n1g", 8, DEPTH), ("n2g", 8, DEPTH), ("bmod", 96, DEPTH), ("pscale", 2, DEPTH),
         ("hcw", 18, DEPTH), ("hcb", 6, DEPTH), ("freq", 1, DEPTH), ("fb1", 1, DEPTH), ("fb2", 1, DEPTH),
         ("fb3", 8, DEPTH), ("hskip", 4, DEPTH), ("fcw", 3 * NFF, DEPTH), ("fcb", NFF, DEPTH), ("fng", 8, 1)]
VOFF, VLEN = {}, {}
_o = 0
for _n, _ln, _rep in VSPEC:
    VOFF[_n] = _o
    VLEN[_n] = _ln
    _o += _ln * _rep
NVEC = _o


def fm(v):
    v = np.asarray(v, np.float32)
    return np.ascontiguousarray(v.reshape(-1, 128).T)


def pad64(v):
    o = np.zeros((128, 1), np.float32)
    o[:64, 0] = v
    return o


_CONST = {}


def consts():
    if _CONST:
        return _CONST
    C = {}
    C["ident"] = np.eye(128, dtype=np.float32).astype(NPBF)
    for name, L in (("x", LX), ("c", LC)):
        t = np.arange(L, dtype=np.int64)
        tk = (t[:, None] * t[None, :])
        ang = 2.0 * np.pi * (tk % L).astype(np.float64) / L
        C["fc_" + name] = np.cos(ang).astype(np.float32).astype(NPBF)
        C["fs_" + name] = np.sin(ang).astype(np.float32).astype(NPBF)
        N2 = 2 * L
        Lh, FB = L // 2, L // 2 + 128
        tau = np.arange(Lh, dtype=np.int64)[:, None]
        ff = np.arange(FB, dtype=np.int64)[None, :]
        valid = (ff <= Lh).astype(np.float64)
        for par, nm in ((0, "e"), (1, "o")):
            ang2 = 2.0 * np.pi * (((2 * tau + par) * ff) % N2).astype(np.float64) / N2
            mc = np.cos(ang2) * valid
            ms = -np.sin(ang2) * valid
            C["m%sc_%s" % (nm, name)] = mc.astype(np.float32).astype(NPBF)
            C["m%ss_%s" % (nm, name)] = ms.astype(np.float32).astype(NPBF)
            C["i%sc_%s" % (nm, name)] = np.ascontiguousarray(mc.T).astype(np.float32).astype(NPBF)
            C["i%ss_%s" % (nm, name)] = np.ascontiguousarray(ms.T).astype(np.float32).astype(NPBF)
        tt = np.linspace(0.0, 1.0, L, dtype=np.float32)[:, None]
        bands = np.linspace(1e-4, 15, 16, dtype=np.float32)[None, :]
        angf = (np.float32(2.0 * math.pi / L) * np.arange(L, dtype=np.float32)[:, None]) * bands
        feats = np.concatenate([tt, np.cos(angf), -np.sin(angf)], axis=-1).astype(np.float32)
        C["feats_" + name] = np.ascontiguousarray(feats.T)
        deltas = np.linspace(math.log(1e-2) / 1.5, math.log(1e-2) / 0.3, 256, dtype=np.float32)
        dec = np.exp(-tt * np.abs(deltas)[None, :]).astype(np.float32) + np.float32(0.05)
        C["dec_" + name] = np.ascontiguousarray(dec.T)
        corr = np.zeros((128, 2, 16), np.float32)
        for g, wd in enumerate((2, 4, 8, 16)):
            pos = np.concatenate([np.arange(8), np.arange(L - 8, L)])
            lo = np.clip(pos - wd // 2, 0, L)
            hi = np.clip(pos - wd // 2 + wd, 0, L)
            hp = (g % 2) * 64
            corr[hp:hp + 64, g // 2, :] = (1.0 / (hi - lo).astype(np.float32))[None, :]
        C["pcorr_" + name] = corr
    c = np.arange(64)
    angc = 2.0 * np.pi * ((c[:, None] * c[None, :]) % 64) / 64.0
    cc = np.zeros((128, 128))
    ss = np.zeros((128, 128))
    for g in range(2):
        cc[g * 64:(g + 1) * 64, g * 64:(g + 1) * 64] = np.cos(angc)
        ss[g * 64:(g + 1) * 64, g * 64:(g + 1) * 64] = -np.sin(angc)
    for name, L in (("x", LX), ("c", LC)):
        sc_ = 1.0 / math.sqrt(64.0 * L)
        C["chdft_" + name] = np.concatenate([cc * sc_, ss * sc_], axis=1).astype(np.float32).astype(NPBF)
    _CONST.update(C)
    return _CONST


def host_inputs(inp, b, seqs=("c", "x")):
    C = consts()
    m = {}
    m["xT"] = np.ascontiguousarray(inp["x"][b].T)
    m["cT"] = np.ascontiguousarray(inp["ctx"][b].T)
    cv = np.zeros((128, 16), np.float32)
    cf, ccf = fm(inp["c"][b]), fm(inp["c_ctx"])
    cv[:, 0::2] = cf
    cv[:, 1::2] = ccf
    m["cvec"] = cv
    for k in ("w_mod", "w_in", "w_branch", "w_out", "ffn_w_up", "ffn_w_down", "pool_w"):
        m[k] = np.ascontiguousarray(inp[k], dtype=np.float32)
    m["fw1"] = np.ascontiguousarray(inp["hyena_filt_w1"], dtype=np.float32)
    m["fw2"] = np.ascontiguousarray(inp["hyena_filt_w2"], dtype=np.float32)
    m["fw3"] = np.ascontiguousarray(inp["hyena_filt_w3"], dtype=np.float32)
    V = np.zeros((128, NVEC), np.float32)

    def put(name, l, arr):
        o = VOFF[name] + (0 if l is None else l * VLEN[name])
        V[:, o:o + arr.shape[1]] = arr

    put("eps", None, np.full((128, 1), EPS, np.float32))
    for l in range(DEPTH):
        put("n1g", l, fm(inp["norm1_g"][l]))
        put("n2g", l, fm(inp["norm2_g"][l]))
        put("bmod", l, np.repeat(fm(inp["b_mod"][l]), 2, axis=1))
        put("pscale", l, fm(inp["pool_scale"][l]))
        put("hcw", l, np.concatenate([fm(inp["hyena_conv_w"][l][j]) for j in range(3)], axis=1))
        put("hcb", l, fm(inp["hyena_conv_b"][l]))
        put("freq", l, pad64(inp["hyena_freq"][l]))
        put("fb1", l, pad64(inp["hyena_filt_b1"][l]))
        put("fb2", l, pad64(inp["hyena_filt_b2"][l]))
        put("fb3", l, fm(inp["hyena_filt_b3"][l]))
        put("hskip", l, np.concatenate([fm(inp["hyena_skip"][l][o]) for o in range(2)], axis=1))
        put("fcw", l, np.concatenate([fm(inp["ffn_conv_w"][l][j]) for j in range(3)], axis=1))
        put("fcb", l, fm(inp["ffn_conv_b"][l]))
    put("fng", None, fm(inp["final_norm_g"]))
    m["vecs"] = V
    rpb = np.asarray(inp["na_rpb"], np.float32)
    col = np.arange(64)
    c0 = np.clip(col - 8, 0, 48)
    col_ok = (col[None, :] >= c0[:, None]) & (col[None, :] < c0[:, None] + 16)
    dc = np.clip(col[None, :] - col[:, None], -15, 15) + 15
    bias = np.full((DEPTH, 128, 4, 5, 9, 64), np.float32(-1e30), np.float32)
    for pat, r in enumerate((4, 0, 2, 60, 62)):
        U = min(max(r - 4, 0), 56)
        for half, rr in enumerate((r, r + 1)):
            r0 = min(max(rr - 4, 0), 56)
            for i in range(8):
                ui = r0 + i - U
                dr = r0 + i - rr + 7
                g = rpb[:, :, dr][:, :, dc]
                g = np.where(col_ok[None, None], g, np.float32(-1e30))
                bias[:, half * 64:(half + 1) * 64, :, pat, ui, :] = np.transpose(g, (0, 2, 1, 3))
    m["bias"] = bias.reshape(DEPTH, 128, 4, 5, 576)
    m["ident"] = C["ident"]
    m["alt"] = np.where(np.arange(128) % 2 == 0, 1.0, -1.0).astype(np.float32).reshape(128, 1).astype(NPBF)
    for s in seqs:
        for k in ("fc_", "fs_", "mec_", "mes_", "moc_", "mos_", "iec_", "ies_", "ioc_", "ios_", "feats_", "dec_", "pcorr_", "chdft_"):
            m[k + s] = C[k + s]
    return m


def finish_consts(m, seqs):
    return m


_NC = {}


def kernel(**inputs):
    inp = {k: np.asarray(v) for k, v in inputs.items()}
    if "nc" not in _NC:
        _NC["nc"] = Builder().build()
    nc = _NC["nc"]
    in_maps = [host_inputs(inp, b) for b in range(8)]
    res = run_bass_kernel_spmd(nc, in_maps, core_ids=list(range(8)))
    out = np.stack([np.ascontiguousarray(r["outT"].T) for r in res.results], axis=0)
    return out.astype(np.float32)
```

```python
import contextlib
import math
import numpy as np
import ml_dtypes
import concourse.bass as bass
import concourse.mybir as mybir
from concourse.bass_utils import run_bass_kernel_spmd

F32 = mybir.dt.float32
BF16 = mybir.dt.bfloat16
AF = mybir.ActivationFunctionType
ALU = mybir.AluOpType
AX = mybir.AxisListType
NPBF = ml_dtypes.bfloat16

D = 1024
KD = 8
LX = 4096
LC = 256
DEPTH = 2
FF = 2816
NFF = 22
INW = 6144
EPS = 1e-6


class Buf:
    __slots__ = ("name", "lw", "rd")

    def __init__(self, name=""):
        self.name = name
        self.lw = None
        self.rd = []


class Op:
    __slots__ = ("eng", "fn", "deps", "is_dma", "sig", "need", "pos")

    def __init__(self, eng, fn, is_dma):
        self.eng = eng
        self.fn = fn
        self.deps = []
        self.is_dma = is_dma
        self.sig = None
        self.need = False
        self.pos = 0


class Tl:
    __slots__ = ("t", "b", "_tb")

    def __init__(self, t, name=""):
        self.t = t
        self.b = Buf(name)
        self._tb = None

    @property
    def tb(self):
        if self._tb is None:
            self._tb = self.t[:].bitcast(BF16)
        return self._tb


class Ring:
    def __init__(self, items):
        self.items = items
        self.i = 0

    def next(self):
        x = self.items[self.i % len(self.items)]
        self.i += 1
        return x


ENGS = ("pe", "act", "dve", "pool", "sp")
SEM_ROT = 1500
DMA_SLOTS = 8


class Prog:
    def __init__(self, nc):
        self.nc = nc
        self.ops = {e: [] for e in ENGS}
        self.stack = contextlib.ExitStack()
        self.cur = self.stack
        self.dma_hist = {e: [] for e in ENGS}
        self.nuid = 0

    def sb(self, shape, dt, name="t"):
        self.nuid += 1
        t = self.cur.enter_context(self.nc.sbuf_tensor(f"{name}_{self.nuid}", list(shape), dt))
        return Tl(t, name)

    def ring(self, n, shape, dt, name="r"):
        return Ring([self.sb(shape, dt, name) for _ in range(n)])

    def ps(self, shape, dt, name="p"):
        self.nuid += 1
        t = self.cur.enter_context(self.nc.psum_tensor(f"{name}_{self.nuid}", list(shape), dt))
        return Tl(t, name)

    @contextlib.contextmanager
    def phase(self):
        old = self.cur
        with contextlib.ExitStack() as st:
            self.cur = st
            yield
            self.barrier()
        self.cur = old

    def add(self, eng, fn, reads=(), writes=(), dma=False):
        op = Op(eng, fn, dma)
        deps = []
        for b in reads:
            if b.lw is not None:
                deps.append(b.lw)
        for b in writes:
            if b.lw is not None:
                deps.append(b.lw)
            deps.extend(b.rd)
        if dma:
            h = self.dma_hist[eng]
            if len(h) >= DMA_SLOTS:
                deps.append(h[len(h) - DMA_SLOTS])
            h.append(op)
        seen = set()
        for d in deps:
            if eng == "pe" and d.eng == "pe" and d.fn is not None:
                continue
            if id(d) not in seen and d is not op:
                seen.add(id(d))
                op.deps.append(d)
                d.need = True
        for b in reads:
            b.rd.append(op)
        for b in writes:
            b.lw = op
            b.rd = []
        op.pos = len(self.ops[eng])
        self.ops[eng].append(op)
        return op

    def barrier(self):
        lasts = []
        for e in ENGS:
            for op in reversed(self.ops[e]):
                if op.fn is not None and not op.is_dma:
                    lasts.append(op)
                    break
            lasts.extend(self.dma_hist[e][-DMA_SLOTS:])
        for e in ENGS:
            op = Op(e, None, False)
            for d in lasts:
                op.deps.append(d)
                d.need = True
            self.ops[e].append(op)

    def emit(self):
        nc = self.nc
        st = self.stack
        semcache = {}

        def sem(key):
            if key not in semcache:
                semcache[key] = st.enter_context(nc.semaphore("s_" + "_".join(str(k) for k in key)))
            return semcache[key]

        for e in ENGS:
            cnt = 0
            dcnt = 0
            for op in self.ops[e]:
                if op.fn is None:
                    continue
                if op.is_dma:
                    slot = dcnt % DMA_SLOTS
                    u = dcnt // DMA_SLOTS
                    op.sig = (("d", e, slot, u // SEM_ROT), 16 * (u % SEM_ROT + 1), 16)
                    dcnt += 1
                elif op.need:
                    op.sig = (("c", e, cnt // SEM_ROT), cnt % SEM_ROT + 1, 1)
                    cnt += 1
        for e in ENGS:
            for op in self.ops[e]:
                if op.sig is not None:
                    sem(op.sig[0])

        def run_engine(e, eng):
            waited = {}
            for op in self.ops[e]:
                for d in op.deps:
                    if d.sig is None:
                        continue
                    key, val, _ = d.sig
                    if waited.get(key, 0) < val:
                        eng.wait_ge(sem(key), val)
                        waited[key] = val
                if op.fn is None:
                    continue
                ins = op.fn(eng)
                if op.sig is not None:
                    ins.then_inc(sem(op.sig[0]), op.sig[2])

        with nc.Block() as block:
            @block.tensor
            def _(eng):
                run_engine("pe", eng)

            @block.scalar
            def _(eng):
                run_engine("act", eng)

            @block.vector
            def _(eng):
                run_engine("dve", eng)

            @block.gpsimd
            def _(eng):
                run_engine("pool", eng)

            @block.sync
            def _(eng):
                run_engine("sp", eng)

    def dma(self, out, in_, reads=(), writes=(), q="sp"):
        return self.add(q, lambda e: e.dma_start(out=out, in_=in_), reads, writes, dma=True)

    def mm(self, out, lhsT, rhs, start, stop, reads=(), writes=()):
        return self.add("pe", lambda e: e.matmul(out, lhsT, rhs, start=start, stop=stop), reads, writes)

    def tr(self, out, in_, ident, reads=(), writes=()):
        return self.add("pe", lambda e: e.transpose(out, in_, ident), reads, writes)

    def act(self, out, in_, func, reads=(), writes=(), **kw):
        return self.add("act", lambda e: e.activation(out, in_, func, **kw), reads, writes)

    def cp(self, eng, out, in_, reads=(), writes=()):
        if eng == "act":
            return self.add("act", lambda e: e.copy(out, in_), reads, writes)
        return self.add(eng, lambda e: e.tensor_copy(out, in_), reads, writes)

    def tt(self, eng, out, a, b, op, reads=(), writes=()):
        return self.add(eng, lambda e: e.tensor_tensor(out, a, b, op), reads, writes)

    def ts(self, eng, out, a, s1, s2, op0, op1, reads=(), writes=()):
        return self.add(eng, lambda e: e.tensor_scalar(out, a, s1, s2, op0, op1), reads, writes)

    def ts1(self, eng, out, a, s1, op0, reads=(), writes=()):
        return self.add(eng, lambda e: e.tensor_single_scalar(out, a, s1, op0), reads, writes)

    def stt(self, eng, out, a, s, b, op0, op1, reads=(), writes=()):
        eng = "dve"
        return self.add(eng, lambda e: e.scalar_tensor_tensor(out, a, s, b, op0, op1), reads, writes)


class Seq:
    def __init__(self, name, L, which):
        self.name = name
        self.L = L
        self.which = which
        self.BS = min(512, L)
        self.NB = L // self.BS
        self.NT = L // 128


class Builder:
    def __init__(self, debug=False, layers=DEPTH, do_x=True):
        self.debug = debug
        self.layers = layers
        self.do_x = do_x
        self.nc = bass.Bass("TRN2", target_bir_lowering=False)
        self.P = Prog(self.nc)
        self.evac_i = 0

    def din(self, name, shape, dt=F32):
        return self.nc.dram_tensor(name, list(shape), dt, kind="ExternalInput").ap()

    def dscr(self, name, shape, dt):
        kind = "ExternalOutput" if self.debug else "Internal"
        return self.nc.dram_tensor(name, list(shape), dt, kind=kind).ap()

    def psf(self):
        held = getattr(self, "held", ())
        while True:
            t = self.PSF.next()
            if id(t) not in held:
                return t

    def psb(self):
        return self.psf()

    def mm_jobs(self, jobs):
        P = self.P
        n = max(len(j[2]) for j in jobs)
        for k in range(n):
            for (pap, pbuf, steps) in jobs:
                if k < len(steps):
                    lhsT, rhs, rb = steps[k]
                    P.mm(pap, lhsT, rhs, k == 0, k == len(steps) - 1, rb, [pbuf])

    def evac_eng(self):
        self.evac_i += 1
        return "act" if self.evac_i % 2 else "dve"

    def build(self):
        P = self.P
        nc = self.nc
        sx = Seq("x", LX, 0)
        sc = Seq("c", LC, 1)
        seqs = [sc, sx] if self.do_x else [sc]
        I = {}
        I["xT"] = self.din("xT", [D, LX])
        I["cT"] = self.din("cT", [D, LC])
        I["cvec"] = self.din("cvec", [128, 16])
        I["w_mod"] = self.din("w_mod", [DEPTH, D, 6 * D])
        I["w_in"] = self.din("w_in", [DEPTH, D, INW])
        I["w_branch"] = self.din("w_branch", [DEPTH, 4, 256, D])
        I["w_out"] = self.din("w_out", [DEPTH, D, D])
        I["ffn_w_up"] = self.din("ffn_w_up", [DEPTH, D, 2 * FF])
        I["ffn_w_down"] = self.din("ffn_w_down", [DEPTH, FF, D])
        I["pool_w"] = self.din("pool_w", [DEPTH, 4, 64, 64])
        I["vecs"] = self.din("vecs", [128, NVEC])
        I["fw1"] = self.din("fw1", [DEPTH, 33, 64])
        I["fw2"] = self.din("fw2", [DEPTH, 64, 64])
        I["fw3"] = self.din("fw3", [DEPTH, 64, 1024])
        I["bias"] = self.din("bias", [DEPTH, 128, 4, 5, 576])
        I["ident"] = self.din("ident", [128, 128], BF16)
        I["alt"] = self.din("alt", [128, 1], BF16)
        for s in seqs:
            L = s.L
            I["fc_" + s.name] = self.din("fc_" + s.name, [L, L], BF16)
            I["fs_" + s.name] = self.din("fs_" + s.name, [L, L], BF16)
            for k_ in ("mec_", "mes_", "moc_", "mos_"):
                I[k_ + s.name] = self.din(k_ + s.name, [L // 2, L // 2 + 128], BF16)
            for k_ in ("iec_", "ies_", "ioc_", "ios_"):
                I[k_ + s.name] = self.din(k_ + s.name, [L // 2 + 128, L // 2], BF16)
            I["feats_" + s.name] = self.din("feats_" + s.name, [33, L])
            I["dec_" + s.name] = self.din("dec_" + s.name, [256, L])
            I["pcorr_" + s.name] = self.din("pcorr_" + s.name, [128, 2, 16])
            I["chdft_" + s.name] = self.din("chdft_" + s.name, [128, 256], BF16)
        self.I = I
        outT = self.nc.dram_tensor("outT", [D, LX], F32, kind="ExternalOutput").ap()
        S = {}
        for s in seqs:
            n = s.name
            L = s.L
            S["xa_" + n] = self.dscr("xa_" + n, [D, L], F32)
            S["xb_" + n] = self.dscr("xb_" + n, [D, L], F32)
            S["pool_" + n] = self.dscr("pool_" + n, [256, L], F32)
            S["fnet_" + n] = self.dscr("fnet_" + n, [L, 256], BF16)
            S["hy_" + n] = self.dscr("hy_" + n, [768, L], F32)
            S["q_" + n] = self.dscr("q_" + n, [256, L], BF16)
            S["k_" + n] = self.dscr("k_" + n, [256, L], BF16)
            S["v_" + n] = self.dscr("v_" + n, [L + 64, 256], BF16)
            S["gate_" + n] = self.dscr("gate_" + n, [4096, L], BF16)
            S["ybr_" + n] = self.dscr("ybr_" + n, [4, 256, L], BF16)
            S["kf_" + n] = self.dscr("kf_" + n, [2, 2, 4, 128, L // 2 + 128], F32)
            S["x2_" + n] = self.dscr("x2_" + n, [256, L], BF16)
            S["aT_" + n] = self.dscr("aT_" + n, [FF, L], BF16)
        self.S = S

        with P.stack:
            self.PSF = Ring([P.ps([128, 512], F32, "psf") for _ in range(8)])
            self.PSB = self.PSF
            self.ident = P.sb([128, 128], BF16, "ident")
            self.ones = P.sb([128, 128], BF16, "ones")
            self.vecs = P.sb([128, NVEC], F32, "vecs")
            self.modv = P.sb([128, 96], F32, "modv")
            self.avec = P.sb([128, 32], F32, "avec")
            self.cact = P.sb([128, 16], F32, "cact")
            P.dma(self.ident.t[:], I["ident"], writes=[self.ident.b])
            P.dma(self.vecs.t[:], I["vecs"], writes=[self.vecs.b])
            P.add("dve", lambda e: e.memset(self.ones.t[:], 1.0), [], [self.ones.b])
            P.dma(self.cact.t[:], I["cvec"], writes=[self.cact.b])
            P.act(self.cact.t[:], self.cact.t[:], AF.Silu, [self.cact.b], [self.cact.b])

            xin = {"x": I["xT"], "c": I["cT"]}
            for l in range(self.layers):
                last = l == DEPTH - 1
                self.phase_mod(l)
                for s in seqs:
                    n = s.name
                    xa, xb = S["xa_" + n], S["xb_" + n]
                    if s.which == 1 and last:
                        self.phase_proj(l, s, xin[n], groups=("k", "v"))
                        continue
                    self.phase_proj(l, s, xin[n], groups=None)
                    self.phase_filter(l, s)
                    self.phase_pool(l, s)
                    self.phase_fnet(l, s)
                    self.phase_hyena(l, s)
                    self.phase_attn(l, s)
                    self.phase_merge(l, s, xin[n], xa)
                    self.phase_ffn_up(l, s, xa)
                    self.phase_ffn_down(l, s, xa, xb)
                    xin[n] = xb
            if self.do_x and self.layers == DEPTH:
                self.phase_final(sx, xin["x"], outT)
            else:
                with P.phase():
                    pass
        P.emit()
        return self.nc

    def vcol(self, name, l=None, k=0):
        off = VOFF[name] + (0 if l is None else l * VLEN[name]) + k
        return self.vecs.t[:, off:off + 1]

    def phase_mod(self, l):
        P, I = self.P, self.I
        with P.phase():
            wr = P.ring(8, [128, 8, 128], F32, "wmod")
            pss = [self.psf() for _ in range(4)]
            wv = I["w_mod"][l].rearrange("(k p) n -> p k n", p=128)
            for q in range(12):
                jobs = []
                for r in range(4):
                    j = 4 * q + r
                    w = wr.next()
                    P.dma(w.t[:], wv[:, :, j * 128:(j + 1) * 128], writes=[w.b])
                    jobs.append((pss[r].t[:, 2 * q:2 * q + 2], pss[r].b,
                                 [(w.t[:, k, :], self.cact.t[:, 2 * k:2 * k + 2], [w.b, self.cact.b]) for k in range(8)]))
                self.mm_jobs(jobs)
            bm = VOFF["bmod"] + l * 96
            for r in range(4):
                P.tt("dve", self.modv.t[:].rearrange("p (q r w) -> p q r w", r=4, w=2)[:, :, r, :],
                     pss[r].t[:, 0:24].rearrange("p (q w) -> p q w", w=2),
                     self.vecs.t[:, bm:bm + 96].rearrange("p (q r w) -> p q r w", r=4, w=2)[:, :, r, :], ALU.add,
                     [pss[r].b, self.vecs.b], [self.modv.b])
            for i, (gname, scj) in enumerate((("n1g", 1), ("n2g", 4))):
                for k in range(8):
                    j = scj * 8 + k
                    P.ts("dve", self.avec.t[:, i * 16 + 2 * k:i * 16 + 2 * k + 2], self.modv.t[:, 2 * j:2 * j + 2],
                         1.0, self.vcol(gname, l, k), ALU.add, ALU.mult, [self.modv.b, self.vecs.b], [self.avec.b])

    def mcol(self, which_mod, k, w):
        j = which_mod * 8 + k
        return self.modv.t[:, 2 * j + w:2 * j + w + 1]

    def norm_block(self, s, xsrc, blk, a_fn, sh_fn, xr, sqr, rsr, tmpr, out_fn):
        P = self.P
        BS = s.BS
        xt = xr.next()
        P.dma(xt.t[:], xsrc.rearrange("(k p) t -> p k t", p=128)[:, :, blk * BS:(blk + 1) * BS], writes=[xt.b])
        sq = sqr.next()
        P.act(sq.t[:], xt.t[:], AF.Square, [xt.b], [sq.b])
        rs = rsr.next()
        H = BS // 2
        pss = [self.psf(), self.psf()]
        self.mm_jobs([(pss[h].t[:, :H], pss[h].b,
                       [(self.ones.t[:], sq.t[:, k, h * H:(h + 1) * H], [self.ones.b, sq.b]) for k in range(8)]) for h in range(2)])
        for h in range(2):
            P.act(rs.t[:, h * H:(h + 1) * H], pss[h].t[:, :H], AF.Sqrt, [pss[h].b, self.vecs.b], [rs.b],
                  bias=self.vcol("eps"), scale=1.0 / D)
        P.add("dve", lambda e: e.reciprocal(rs.t[:], rs.t[:]), [rs.b], [rs.b])
        for k in range(8):
            tmp = tmpr.next()
            P.stt("dve", tmp.t[:], xt.t[:, k, :], a_fn(k), rs.t[:], ALU.mult, ALU.mult,
                  [xt.b, rs.b, self.avec.b, self.vecs.b], [tmp.b])
            out_fn(k, tmp)

    def phase_proj(self, l, s, xsrc, groups):
        P, I, S = self.P, self.I, self.S
        n, L, BS, NB, w = s.name, s.L, s.BS, s.NB, s.which
        glist = [("pool", 0, 256, "FM", S["pool_" + n], F32, None),
                 ("fnet", 256, 256, "TM", S["fnet_" + n], BF16, None),
                 ("hy", 512, 768, "FM", S["hy_" + n], F32, None),
                 ("q", 1280, 256, "FM", S["q_" + n], BF16, "q"),
                 ("k", 1536, 256, "FM", S["k_" + n], BF16, None),
                 ("v", 1792, 256, "TM", S["v_" + n], BF16, None),
                 ("gate", 2048, 4096, "FM", S["gate_" + n], BF16, "sig")]
        if groups is not None:
            glist = [g for g in glist if g[0] in groups]
        with P.phase():
            hx = P.sb([128, 8, L], BF16, "hx")
            with P.phase():
                xr = P.ring(2, [128, 8, BS], F32, "xr")
                sqr = P.ring(2, [128, 8, BS], BF16, "sq")
                rsr = P.ring(2, [128, BS], F32, "rs")
                tmpr = P.ring(3, [128, BS], F32, "tmp")
                for blk in range(NB):
                    def out_fn(k, tmp, blk=blk):
                        P.act(hx.t[:, k, blk * BS:(blk + 1) * BS], tmp.t[:], AF.Identity, [tmp.b, self.modv.b], [hx.b],
                              bias=self.mcol(0, k, w), scale=1.0)
                    self.norm_block(s, xsrc, blk, lambda k: self.avec.t[:, 2 * k + w:2 * k + w + 1], None,
                                    xr, sqr, rsr, tmpr, out_fn)
            wst = P.ring(2, [128, 8, 256], F32, "wst")
            wbf = P.ring(2, [128, 8, 256], BF16, "wbf")
            ofm32 = P.ring(5, [128, BS], F32, "ofm32")
            ofm16 = P.ring(5, [128, BS], BF16, "ofm16")
            otm = P.ring(5, [128, 256], BF16, "otm")
            wv = I["w_in"][l].rearrange("(k p) n -> p k n", p=128)
            for (gname, c0, ncols, mode, dest, dt, special) in glist:
                for cb in range(ncols // 256):
                    cs = c0 + cb * 256
                    ws = wst.next()
                    P.dma(ws.t[:], wv[:, :, cs:cs + 256], writes=[ws.b])
                    wb = wbf.next()
                    P.cp("pool", wb.t[:], ws.t[:], [ws.b], [wb.b])
                    if mode == "FM":
                        items = [(cc, blk) for cc in range(2) for blk in range(NB)]
                        for i0 in range(0, len(items), 4):
                            grp = items[i0:i0 + 4]
                            pss = [self.psf() for _ in grp]
                            self.mm_jobs([(pss[i].t[:, :BS], pss[i].b,
                                           [(wb.t[:, k, cc * 128:(cc + 1) * 128], hx.t[:, k, blk * BS:(blk + 1) * BS], [wb.b, hx.b])
                                            for k in range(8)]) for i, (cc, blk) in enumerate(grp)])
                            for i, (cc, blk) in enumerate(grp):
                                ps = pss[i]
                                row0 = cb * 256 + cc * 128
                                o = (ofm32 if dt == F32 else ofm16).next()
                                if special == "sig":
                                    P.act(o.t[:], ps.t[:, :BS], AF.Sigmoid, [ps.b], [o.b])
                                elif special == "q":
                                    P.act(o.t[:], ps.t[:, :BS], AF.Copy, [ps.b], [o.b], scale=0.125)
                                else:
                                    P.cp(self.evac_eng(), o.t[:], ps.t[:, :BS], [ps.b], [o.b])
                                P.dma(dest[row0:row0 + 128, blk * BS:(blk + 1) * BS], o.t[:], reads=[o.b], q="pool")
                    else:
                        for t0 in range(0, s.NT, 4):
                            grp = list(range(t0, min(t0 + 4, s.NT)))
                            pss = [self.psf() for _ in grp]
                            self.mm_jobs([(pss[i].t[:, :256], pss[i].b,
                                           [(hx.t[:, k, tt * 128:(tt + 1) * 128], wb.t[:, k, :], [wb.b, hx.b]) for k in range(8)])
                                          for i, tt in enumerate(grp)])
                            for i, tt in enumerate(grp):
                                o = otm.next()
                                P.cp(self.evac_eng(), o.t[:], pss[i].t[:, :256], [pss[i].b], [o.b])
                                P.dma(dest[tt * 128:(tt + 1) * 128, cb * 256:(cb + 1) * 256], o.t[:], reads=[o.b], q="pool")

    def seq_transform(self, s, outs, handler, mring, KT=4):
        P = self.P
        BS, NB, NT = s.BS, s.NB, s.NT
        KT = min(KT, NT)
        for nb in range(NB):
            pss = [self.psf() for _ in outs]
            for g in range(NT // KT):
                loaded = {}
                for (tm, c0, mat) in outs:
                    if id(mat) not in loaded:
                        mt = mring.next()
                        P.dma(mt.t[:, :KT, :BS],
                              mat.rearrange("(g p) n -> p g n", p=128)[:, g * KT:(g + 1) * KT, nb * BS:(nb + 1) * BS],
                              writes=[mt.b])
                        loaded[id(mat)] = mt
                for kk in range(KT):
                    tt = g * KT + kk
                    for oi, (tm, c0, mat) in enumerate(outs):
                        mt = loaded[id(mat)]
                        P.mm(pss[oi].t[:, :BS], tm.t[:, tt, c0:c0 + 128], mt.t[:, kk, :BS], tt == 0, tt == NT - 1,
                             [tm.b, mt.b], [pss[oi].b])
            handler(nb, pss)

    def seq_transform2(self, outs, handler, mring, nt_in, blocks, KT=2):
        P = self.P
        KT = min(KT, nt_in)
        pending = None
        for bi, (b0, bs) in enumerate(blocks):
            pss = [self.psf() for _ in outs]
            for k0 in range(0, nt_in, KT):
                kn = min(KT, nt_in - k0)
                loaded = {}
                for (tm, c0, mat) in outs:
                    if id(mat) not in loaded:
                        mt = mring.next()
                        P.dma(mt.t[:, :kn, :bs], mat.rearrange("(g p) n -> p g n", p=128)[:, k0:k0 + kn, b0:b0 + bs],
                              writes=[mt.b])
                        loaded[id(mat)] = mt
                for kk in range(kn):
                    tt = k0 + kk
                    for oi, (tm, c0, mat) in enumerate(outs):
                        mt = loaded[id(mat)]
                        P.mm(pss[oi].t[:, :bs], tm.t[:, tt, c0:c0 + 128], mt.t[:, kk, :bs], tt == 0, tt == nt_in - 1,
                             [tm.b, mt.b], [pss[oi].b])
            self.held = set(id(p) for p in pss)
            if pending is not None:
                pending()
                pending = None
            self.held = set()
            pending = handler(bi, b0, bs, pss)
        if pending is not None:
            pending()

    def fm_to_tm(self, src_tl, src_fn, nblk, tm, t0, c0):
        P = self.P
        j = 0
        while j < nblk:
            nn = min(16, nblk - j)
            nbk = min(4, nn)
            pbs = [self.psb() for _ in range(nbk)]
            for i in range(nn):
                pb = pbs[i % nbk]
                sl_ = i // nbk
                P.tr(pb.tb[:, sl_ * 128:(sl_ + 1) * 128], src_fn(j + i), self.ident.t[:], [src_tl.b, self.ident.b], [pb.b])
            for bi in range(nbk):
                cnt = len(range(bi, nn, nbk))
                dst = tm.t[:, t0 + j + bi:t0 + j + bi + (cnt - 1) * nbk + 1:nbk, c0:c0 + 128]
                P.cp(self.evac_eng(), dst, pbs[bi].tb[:, :cnt * 128].rearrange("p (a c) -> p a c", c=128), [pbs[bi].b], [tm.b])
            j += nn

    def phase_filter(self, l, s):
        P, I, S = self.P, self.I, self.S
        n, L, BS, NB, NT = s.name, s.L, s.BS, s.NB, s.NT
        N2 = 2 * L
        Lh = L // 2
        FB = Lh + 128
        NTh = Lh // 128
        fblocks = [(b0, min(512, FB - b0)) for b0 in range(0, FB, 512)]
        kf = S["kf_" + n]
        with P.phase():
            w3 = P.sb([64, 1024], F32, "fw3")
            h2 = P.sb([64, L], F32, "h2")
            P.dma(w3.t[:], I["fw3"][l], writes=[w3.b])
            with P.phase():
                ft = P.sb([33, L], F32, "feats")
                w1 = P.sb([33, 64], F32, "fw1")
                w2 = P.sb([64, 64], F32, "fw2")
                h1 = P.sb([64, L], F32, "h1")
                P.dma(ft.t[:], I["feats_" + n], writes=[ft.b])
                P.dma(w1.t[:], I["fw1"][l], writes=[w1.b])
                P.dma(w2.t[:], I["fw2"][l], writes=[w2.b])
                ar = P.ring(2, [64, BS], F32, "farg")
                mr = P.ring(2, [64, BS], F32, "fmask")
                for (src, wgt, kdim, dst, bname) in ((ft, w1, 33, h1, "fb1"), (h1, w2, 64, h2, "fb2")):
                    for blk in range(NB):
                        ps = self.psf()
                        P.mm(ps.t[:64, :BS], wgt.t[:kdim, :], src.t[:kdim, blk * BS:(blk + 1) * BS], True, True,
                             [wgt.b, src.b], [ps.b])
                        a = ar.next()
                        m = mr.next()
                        P.ts("dve", a.t[:], ps.t[:64, :BS], self.vecs.t[:64, VOFF["freq"] + l:VOFF["freq"] + l + 1],
                             self.vecs.t[:64, VOFF[bname] + l:VOFF[bname] + l + 1], ALU.mult, ALU.add,
                             [ps.b, self.vecs.b], [a.b])
                        for _ in range(2):
                            P.add("dve", lambda e, a=a, m=m: e.tensor_single_scalar(m.t[:], a.t[:], float(np.pi), ALU.is_gt),
                                  [a.b], [m.b])
                            P.stt("dve", a.t[:], m.t[:], float(-2 * np.pi), a.t[:], ALU.mult, ALU.add, [a.b, m.b], [a.b])
                            P.add("dve", lambda e, a=a, m=m: e.tensor_single_scalar(m.t[:], a.t[:], float(-np.pi), ALU.is_lt),
                                  [a.b], [m.b])
                            P.stt("dve", a.t[:], m.t[:], float(2 * np.pi), a.t[:], ALU.mult, ALU.add, [a.b, m.b], [a.b])
                        P.act(dst.t[:, blk * BS:(blk + 1) * BS], a.t[:], AF.Sin, [a.b], [dst.b])
            hT = P.sb([128, 2, L], F32, "hT")
            sd = P.sb([128, 2, L], BF16, "hsd")
            dec = P.sb([128, L], F32, "dec")
            tms = [P.sb([128, NTh, 256], BF16, "hftm") for _ in range(4)]
            sums = P.sb([128, 8], F32, "fsums")
            mring = P.ring(8, [128, 2, 512], BF16, "mring")
            kst = P.ring(12, [128, 512], F32, "kst")
            mec, mes, moc, mos = (I[k + n] for k in ("mec_", "mes_", "moc_", "mos_"))
            wsc = 2.0 / N2
            for o in range(2):
                for cc in range(2):
                    P.dma(dec.t[:], I["dec_" + n][cc * 128:(cc + 1) * 128, :], writes=[dec.b])
                    for d in range(2):
                        j = o * 4 + d * 2 + cc
                        for blk in range(NB):
                            ps = self.psf()
                            P.mm(ps.t[:, :BS], w3.t[:, j * 128:(j + 1) * 128], h2.t[:, blk * BS:(blk + 1) * BS], True, True,
                                 [w3.b, h2.b], [ps.b])
                            P.stt("dve", hT.t[:, d, blk * BS:(blk + 1) * BS], ps.t[:, :BS],
                                  self.vecs.t[:, VOFF["fb3"] + l * 8 + j:VOFF["fb3"] + l * 8 + j + 1],
                                  dec.t[:, blk * BS:(blk + 1) * BS], ALU.add, ALU.mult,
                                  [ps.b, dec.b, self.vecs.b], [hT.b])
                        P.add("dve", lambda e, d=d: e.tensor_reduce(sums.t[:, d:d + 1], hT.t[:, d, :], AX.X, ALU.add,
                                                                    apply_absolute_value=True), [hT.b], [sums.b])
                    P.tt("dve", sums.t[:, 2:3], sums.t[:, 0:1], sums.t[:, 1:2], ALU.add, [sums.b], [sums.b])
                    P.ts1("dve", sums.t[:, 2:3], sums.t[:, 2:3], EPS, ALU.add, [sums.b], [sums.b])
                    P.add("dve", lambda e: e.reciprocal(sums.t[:, 2:3], sums.t[:, 2:3]), [sums.b], [sums.b])
                    P.tt("dve", sums.t[:, 4 + cc:5 + cc], hT.t[:, 1, 0:1], sums.t[:, 2:3], ALU.mult, [hT.b, sums.b], [sums.b])
                    for d in range(2):
                        P.ts1("dve", hT.t[:, d, :], hT.t[:, d, :], sums.t[:, 2:3], ALU.mult, [hT.b, sums.b], [hT.b])
                    P.tt("pool", sd.t[:, 0, :], hT.t[:, 0, :], hT.t[:, 1, :], ALU.add, [hT.b], [sd.b])
                    P.tt("pool", sd.t[:, 1, :], hT.t[:, 0, :], hT.t[:, 1, :], ALU.subtract, [hT.b], [sd.b])
                    for q in range(2):
                        for e_ in range(2):
                            self.fm_to_tm(sd, lambda jj, q=q, e_=e_: sd.t[:, q, 256 * jj + e_:256 * (jj + 1):2], NTh,
                                          tms[2 * q + e_], 0, cc * 128)
                for cc in range(2):
                    outs = [(tms[0], cc * 128, mec), (tms[1], cc * 128, moc), (tms[2], cc * 128, mes), (tms[3], cc * 128, mos)]

                    def handler(bi, b0, bs, pss, o=o, cc=cc):
                        ec, oc, es, os_ = pss
                        hb0 = sums.t[:, 4 + cc:5 + cc]
                        toc, tos = kst.next(), kst.next()
                        P.cp("act", toc.t[:, :bs], oc.t[:, :bs], [oc.b], [toc.b])
                        P.cp("act", tos.t[:, :bs], os_.t[:, :bs], [os_.b], [tos.b])
                        ks = [kst.next() for _ in range(4)]
                        P.stt("dve", ks[0].t[:, :bs], ec.t[:, :bs], hb0, toc.t[:, :bs], ALU.subtract, ALU.add,
                              [ec.b, sums.b, toc.b], [ks[0].b])
                        P.tt("dve", ks[1].t[:, :bs], es.t[:, :bs], tos.t[:, :bs], ALU.add, [es.b, tos.b], [ks[1].b])
                        P.stt("dve", ks[2].t[:, :bs], ec.t[:, :bs], hb0, toc.t[:, :bs], ALU.subtract, ALU.subtract,
                              [ec.b, sums.b, toc.b], [ks[2].b])
                        P.tt("dve", ks[3].t[:, :bs], tos.t[:, :bs], es.t[:, :bs], ALU.subtract, [es.b, tos.b], [ks[3].b])
                        for q in range(4):
                            P.ts1("pool", ks[q].t[:, :bs], ks[q].t[:, :bs], wsc, ALU.mult, [ks[q].b], [ks[q].b])
                            if b0 == 0:
                                P.ts1("pool", ks[q].t[:, 0:1], ks[q].t[:, 0:1], 0.5, ALU.mult, [ks[q].b], [ks[q].b])
                            P.dma(kf[o, cc, q, :, b0:b0 + bs], ks[q].t[:, :bs], reads=[ks[q].b], q="pool")

                    self.seq_transform2(outs, handler, mring, NTh, fblocks)

    def phase_pool(self, l, s):
        P, I, S = self.P, self.I, self.S
        n, L, BS, NB = s.name, s.L, s.BS, s.NB
        LP = L + 32
        with P.phase():
            u = P.sb([128, 2, LP], F32, "pu")
            A = P.sb([128, 2, LP], F32, "pA")
            B = P.sb([128, 2, LP], F32, "pB")
            r = P.sb([128, 2, L], BF16, "pr")
            corr = P.sb([128, 2, 16], F32, "pcorr")
            pwf = P.sb([128, 2, 128], F32, "pwf")
            pwb = P.sb([128, 2, 128], BF16, "pwb")
            tmp8 = P.sb([128, 2, 8], F32, "ptmp")
            ost = P.ring(3, [128, BS], BF16, "post")
            P.add("pool", lambda e: e.memset(u.t[:], 0.0), [], [u.b])
            P.add("pool", lambda e: e.memset(pwf.t[:], 0.0), [], [pwf.b])
            P.dma(u.t[:, :, 16:16 + L], S["pool_" + n].rearrange("(c p) t -> p c t", p=128), reads=[], writes=[u.b])
            P.dma(corr.t[:], I["pcorr_" + n], writes=[corr.b])
            for g in range(4):
                hp = (g % 2) * 64
                P.dma(pwf.t[hp:hp + 64, g // 2, hp:hp + 64], I["pool_w"][l, g], writes=[pwf.b])
            P.cp("dve", pwb.t[:], pwf.t[:], [pwf.b], [pwb.b])
            for g in range(4):
                hp, c = (g % 2) * 64, g // 2
                sl = slice(hp, hp + 64)
                eng = "dve" if g % 2 == 0 else "pool"
                P.tt(eng, A.t[sl, c, 1:LP], u.t[sl, c, 0:LP - 1], u.t[sl, c, 1:LP], ALU.add, [u.b], [A.b])
                cur = A
                if g >= 1:
                    P.tt(eng, B.t[sl, c, 2:LP - 1], A.t[sl, c, 1:LP - 2], A.t[sl, c, 3:LP], ALU.add, [A.b], [B.b])
                    cur = B
                if g >= 2:
                    P.tt(eng, A.t[sl, c, 4:LP - 3], B.t[sl, c, 2:LP - 5], B.t[sl, c, 6:LP - 1], ALU.add, [B.b], [A.b])
                    cur = A
                if g >= 3:
                    P.tt(eng, B.t[sl, c, 8:LP - 7], A.t[sl, c, 4:LP - 11], A.t[sl, c, 12:LP - 3], ALU.add, [A.b], [B.b])
                    cur = B
                wdt = (2, 4, 8, 16)[g]
                P.stt(eng, r.t[sl, c, :], cur.t[sl, c, 16:16 + L], 1.0 / wdt, u.t[sl, c, 16:16 + L], ALU.mult, ALU.subtract,
                      [cur.b, u.b], [r.b])
                for (c0, k0) in ((0, 0), (L - 8, 8)):
                    P.tt(eng, tmp8.t[sl, c, :], cur.t[sl, c, 16 + c0:16 + c0 + 8], corr.t[sl, c, k0:k0 + 8], ALU.mult,
                         [cur.b, corr.b], [tmp8.b])
                    P.tt(eng, r.t[sl, c, c0:c0 + 8], tmp8.t[sl, c, :], u.t[sl, c, 16 + c0:16 + c0 + 8], ALU.subtract,
                         [tmp8.b, u.b, r.b], [r.b])
            for c in range(2):
                for blk in range(NB):
                    ps = self.psf()
                    P.mm(ps.t[:, :BS], pwb.t[:, c, :], r.t[:, c, blk * BS:(blk + 1) * BS], True, True, [pwb.b, r.b], [ps.b])
                    o = ost.next()
                    P.act(o.t[:], ps.t[:, :BS], AF.Copy, [ps.b, self.vecs.b], [o.b], scale=self.vcol("pscale", l, c))
                    P.dma(S["ybr_" + n][0, c * 128:(c + 1) * 128, blk * BS:(blk + 1) * BS], o.t[:], reads=[o.b], q="pool")

    def phase_fnet(self, l, s):
        P, I, S = self.P, self.I, self.S
        n, L, BS, NB, NT = s.name, s.L, s.BS, s.NB, s.NT
        with P.phase():
            utm = P.sb([128, NT, 256], BF16, "futm")
            cd = P.sb([128, 256], BF16, "chdft")
            P.dma(utm.t[:], S["fnet_" + n].rearrange("(g p) c -> p g c", p=128), writes=[utm.b])
            P.dma(cd.t[:], I["chdft_" + n], writes=[cd.b])
            mring = P.ring(4, [128, 4, BS], BF16, "mring")
            ab = P.ring(8, [128, BS], BF16, "fab")
            ost = P.ring(3, [128, BS], BF16, "fost")
            fc, fs = I["fc_" + n], I["fs_" + n]
            outs = [(utm, 0, fc), (utm, 0, fs), (utm, 128, fc), (utm, 128, fs)]

            def handler(nb, pss):
                tiles = []
                for c in range(2):
                    a_c, a_s = ab.next(), ab.next()
                    P.cp("act", a_c.t[:], pss[2 * c].t[:, :BS], [pss[2 * c].b], [a_c.b])
                    P.cp("dve", a_s.t[:], pss[2 * c + 1].t[:, :BS], [pss[2 * c + 1].b], [a_s.b])
                    tiles.append((a_c, a_s))
                ps2 = [self.psf(), self.psf()]
                self.mm_jobs([(ps2[c].t[:, :BS], ps2[c].b,
                               [(cd.t[:, 0:128], tiles[c][0].t[:], [cd.b, tiles[c][0].b]),
                                (cd.t[:, 128:256], tiles[c][1].t[:], [cd.b, tiles[c][1].b])]) for c in range(2)])
                for c in range(2):
                    o = ost.next()
                    P.cp(self.evac_eng(), o.t[:], ps2[c].t[:, :BS], [ps2[c].b], [o.b])
                    P.dma(S["ybr_" + n][1, c * 128:(c + 1) * 128, nb * BS:(nb + 1) * BS], o.t[:], reads=[o.b], q="pool")

            self.seq_transform(s, outs, handler, mring)

    def phase_hyena(self, l, s):
        P, I, S = self.P, self.I, self.S
        n, L, BS, NB, NT = s.name, s.L, s.BS, s.NB, s.NT
        with P.phase():
            v = P.sb([128, 2, L], F32, "hv")
            xg = P.sb([128, 2, L], BF16, "hxg")
            x2st = P.ring(2, [128, L], BF16, "hx2st")
            with P.phase():
                ur = P.ring(2, [128, L + 2], F32, "hur")
                tr_ = P.ring(2, [128, L], F32, "htr")
                uv = S["hy_" + n]
                for c in range(6):
                    ut = ur.next()
                    P.add("pool", lambda e, ut=ut: e.memset(ut.t[:, 0:1], 0.0), [], [ut.b])
                    P.add("pool", lambda e, ut=ut: e.memset(ut.t[:, L + 1:L + 2], 0.0), [], [ut.b])
                    P.dma(ut.t[:, 1:L + 1], uv[c * 128:(c + 1) * 128, :], writes=[ut.b])
                    t1 = tr_.next()
                    eng = "dve" if c % 2 == 0 else "pool"
                    P.ts(eng, t1.t[:], ut.t[:, 0:L], self.vcol("hcw", l, 0 * 6 + c), self.vcol("hcb", l, c), ALU.mult, ALU.add,
                         [ut.b, self.vecs.b], [t1.b])
                    P.stt(eng, t1.t[:], ut.t[:, 1:L + 1], self.vcol("hcw", l, 1 * 6 + c), t1.t[:], ALU.mult, ALU.add,
                          [ut.b, t1.b, self.vecs.b], [t1.b])
                    if c < 2:
                        dst, dstb, x2t = v.t[:, c, :], v.b, None
                    elif c < 4:
                        dst, dstb, x2t = xg.t[:, c - 2, :], xg.b, None
                    else:
                        x2t = x2st.next()
                        dst, dstb = x2t.t[:], x2t.b
                    P.stt(eng, dst, ut.t[:, 2:L + 2], self.vcol("hcw", l, 2 * 6 + c), t1.t[:], ALU.mult, ALU.add,
                          [ut.b, t1.b, self.vecs.b], [dstb])
                    if x2t is not None:
                        P.dma(S["x2_" + n][(c - 4) * 128:(c - 3) * 128, :], x2t.t[:], reads=[x2t.b], q="pool")
            Lh = L // 2
            FB = Lh + 128
            NTh = Lh // 128
            NFT = FB // 128
            fblocks = [(b0, min(512, FB - b0)) for b0 in range(0, FB, 512)]
            tblocks = [(b0, min(512, Lh - b0)) for b0 in range(0, Lh, 512)]
            mec, mes, moc, mos = (I[k + n] for k in ("mec_", "mes_", "moc_", "mos_"))
            iec, ies, ioc, ios = (I[k + n] for k in ("iec_", "ies_", "ioc_", "ios_"))
            kfv = S["kf_" + n]
            for o in range(2):
                with P.phase():
                    zt = [P.sb([128, NTh, 256], BF16, "zteo") for _ in range(2)]
                    with P.phase():
                        zbf = P.sb([128, 2, L], BF16, "zbf")
                        for c in range(2):
                            P.cp("act" if c == 0 else "dve", zbf.t[:, c, :], v.t[:, c, :], [v.b], [zbf.b])
                            for e_ in range(2):
                                self.fm_to_tm(zbf, lambda jj, c=c, e_=e_: zbf.t[:, c, 256 * jj + e_:256 * (jj + 1):2], NTh,
                                              zt[e_], 0, c * 128)
                        if o == 1:
                            P.dma(xg.t[:], S["x2_" + n].rearrange("(c p) t -> p c t", p=128), writes=[xg.b])
                    pq = [P.sb([128, NFT, 256], BF16, "pqtm") for _ in range(4)]
                    mring = P.ring(8, [128, 2, 512], BF16, "mring")
                    kr_ = P.ring(2, [128, 4, 512], F32, "kfr")
                    t4 = P.ring(12, [128, 512], F32, "ht4")
                    zb = P.ring(8, [128, 512], BF16, "hzb")
                    yb = P.ring(12, [128, 512], BF16, "hyb")
                    ost = P.ring(3, [128, 1024], BF16, "host")
                    for c in range(2):
                        outs = [(zt[0], c * 128, mec), (zt[0], c * 128, mes), (zt[1], c * 128, moc), (zt[1], c * 128, mos)]

                        def handler(bi, b0, bs, pss, o=o, c=c):
                            er, ei, or_, oi = pss
                            kt = kr_.next()
                            P.dma(kt.t[:, :, :bs], kfv[o, c].rearrange("q p t -> p q t")[:, :, b0:b0 + bs], writes=[kt.b])
                            tor, toi = t4.next(), t4.next()
                            P.cp("act", tor.t[:, :bs], or_.t[:, :bs], [or_.b], [tor.b])
                            P.cp("act", toi.t[:, :bs], oi.t[:, :bs], [oi.b], [toi.b])
                            zrl, zil, zrh, zih = zb.next(), zb.next(), zb.next(), zb.next()
                            P.tt("dve", zrl.t[:, :bs], er.t[:, :bs], tor.t[:, :bs], ALU.add, [er.b, tor.b], [zrl.b])
                            P.tt("dve", zil.t[:, :bs], ei.t[:, :bs], toi.t[:, :bs], ALU.add, [ei.b, toi.b], [zil.b])
                            P.tt("dve", zrh.t[:, :bs], er.t[:, :bs], tor.t[:, :bs], ALU.subtract, [er.b, tor.b], [zrh.b])
                            P.tt("dve", zih.t[:, :bs], toi.t[:, :bs], ei.t[:, :bs], ALU.subtract, [ei.b, toi.b], [zih.b])
                            ys = []
                            for hi_, (zr, zi) in enumerate(((zrl, zil), (zrh, zih))):
                                krr, kii = kt.t[:, 2 * hi_, :bs], kt.t[:, 2 * hi_ + 1, :bs]
                                a1, a2, a3, a4 = t4.next(), t4.next(), t4.next(), t4.next()
                                P.tt("dve", a1.t[:, :bs], zr.t[:, :bs], krr, ALU.mult, [zr.b, kt.b], [a1.b])
                                P.tt("pool", a2.t[:, :bs], zi.t[:, :bs], kii, ALU.mult, [zi.b, kt.b], [a2.b])
                                P.tt("dve", a3.t[:, :bs], zr.t[:, :bs], kii, ALU.mult, [zr.b, kt.b], [a3.b])
                                P.tt("pool", a4.t[:, :bs], zi.t[:, :bs], krr, ALU.mult, [zi.b, kt.b], [a4.b])
                                P.tt("dve", a1.t[:, :bs], a1.t[:, :bs], a2.t[:, :bs], ALU.subtract, [a1.b, a2.b], [a1.b])
                                P.tt("pool", a3.t[:, :bs], a3.t[:, :bs], a4.t[:, :bs], ALU.add, [a3.b, a4.b], [a3.b])
                                ys.append((a1, a3))
                            (yrl, yil), (yrh, yih) = ys
                            outs_ = [yb.next() for _ in range(4)]
                            P.tt("dve", outs_[0].t[:, :bs], yrl.t[:, :bs], yrh.t[:, :bs], ALU.add, [yrl.b, yrh.b], [outs_[0].b])
                            P.tt("pool", outs_[1].t[:, :bs], yil.t[:, :bs], yih.t[:, :bs], ALU.subtract, [yil.b, yih.b], [outs_[1].b])
                            P.tt("dve", outs_[2].t[:, :bs], yrl.t[:, :bs], yrh.t[:, :bs], ALU.subtract, [yrl.b, yrh.b], [outs_[2].b])
                            P.tt("pool", outs_[3].t[:, :bs], yil.t[:, :bs], yih.t[:, :bs], ALU.add, [yil.b, yih.b], [outs_[3].b])
                            nj = bs // 128

                            def part2(outs_=outs_, nj=nj, b0=b0, c=c):
                                for q in range(4):
                                    self.fm_to_tm(outs_[q], lambda jj, t=outs_[q]: t.t[:, jj * 128:(jj + 1) * 128], nj, pq[q],
                                                  b0 // 128, c * 128)
                            return part2

                        self.seq_transform2(outs, handler, mring, NTh, fblocks)
                    for c in range(2):
                        outs2 = [(pq[0], c * 128, iec), (pq[1], c * 128, ies), (pq[2], c * 128, ioc), (pq[3], c * 128, ios)]

                        def handler2(bi, b0, bs, pss, o=o, c=c):
                            ot = ost.next() if o == 1 else None
                            for e_ in range(2):
                                pa, pb_ = pss[2 * e_], pss[2 * e_ + 1]
                                tb_ = t4.next()
                                P.cp("act", tb_.t[:, :bs], pb_.t[:, :bs], [pb_.b], [tb_.b])
                                tm = t4.next()
                                vsl = v.t[:, c, 2 * b0 + e_:2 * (b0 + bs):2]
                                P.stt("dve", tm.t[:, :bs], vsl, self.vcol("hskip", l, o * 2 + c), pa.t[:, :bs],
                                      ALU.mult, ALU.add, [v.b, pa.b, self.vecs.b], [tm.b])
                                P.tt("pool", tm.t[:, :bs], tm.t[:, :bs], tb_.t[:, :bs], ALU.add, [tm.b, tb_.b], [tm.b])
                                gsl = xg.t[:, c, 2 * b0 + e_:2 * (b0 + bs):2]
                                if o == 0:
                                    P.tt("dve", vsl, tm.t[:, :bs], gsl, ALU.mult, [tm.b, xg.b, v.b], [v.b])
                                else:
                                    P.tt("dve", ot.t[:, e_:2 * bs:2], tm.t[:, :bs], gsl, ALU.mult, [tm.b, xg.b, ot.b], [ot.b])
                            if o == 1:
                                P.dma(S["ybr_" + n][2, c * 128:(c + 1) * 128, 2 * b0:2 * (b0 + bs)], ot.t[:, :2 * bs],
                                      reads=[ot.b], q="pool")

                        self.seq_transform2(outs2, handler2, mring, NFT, tblocks)

    def phase_attn(self, l, s):
        P, I, S = self.P, self.I, self.S
        n, L = s.name, s.L
        grid = s.which == 0
        with P.phase():
            qT = P.sb([128, 2, L], BF16, "qT")
            kcT = P.sb([128, 2, LC], BF16, "kcT")
            vc = P.sb([128, 2, 256], BF16, "vc")
            yT = P.sb([128, 2, L], BF16, "yT")
            P.dma(qT.t[:], S["q_" + n].rearrange("(c p) t -> p c t", p=128), writes=[qT.b])
            P.dma(kcT.t[:], S["k_c"].rearrange("(c p) t -> p c t", p=128), writes=[kcT.b])
            P.dma(vc.t[:], S["v_c"][0:LC, :].rearrange("(g p) c -> p g c", p=128), writes=[vc.b])
            if grid:
                NT = s.NT
                kT = P.sb([128, 2, L], BF16, "kT")
                ve = P.sb([128, NT, 256], BF16, "ve")
                bias = P.sb([128, 4, 5, 576], F32, "bias")
                P.dma(kT.t[:], S["k_" + n].rearrange("(c p) t -> p c t", p=128), writes=[kT.b])
                P.dma(ve.t[:], S["v_" + n][0:L, :].rearrange("(g p) c -> p g c", p=128), writes=[ve.b])
                P.dma(bias.t[:], I["bias"][l], writes=[bias.b])
            sr = P.ring(6, [128, 832], F32, "as")
            pr = P.ring(9, [128, 832], BF16, "ap")
            ptr_ = P.ring(8, [128, 7, 128], BF16, "apT")
            st = P.ring(12, [128, 4], F32, "astat")
            orow = P.ring(3, [128, 256], BF16, "aorow")
            npair = L // 128

            def stage_a(pi):
                r = 2 * pi
                if grid:
                    r0a = min(max(r - 4, 0), 56)
                    r0b = min(max(r - 3, 0), 56)
                    nine = r0b != r0a
                    pat = {0: 1, 2: 2, 60: 3, 62: 4}.get(r, 0)
                    nnb = 576 if nine else 512
                else:
                    r0a, nine, nnb = 0, False, 0
                nk = nnb + 256
                heads = []
                for h in range(4):
                    hp, hc = (h % 2) * 64, h // 2
                    hs = slice(hp, hp + 64)
                    sb_ = sr.next()
                    stt_ = st.next()
                    qa = qT.t[hs, hc, r * 64:r * 64 + 128]
                    p2 = self.psf()
                    if grid:
                        p1 = self.psf()
                        P.mm(p1.t[:, :512], qa, kT.t[hs, hc, r0a * 64:r0a * 64 + 512], True, True, [qT.b, kT.b], [p1.b])
                        P.mm(p2.t[:, 64:320], qa, kcT.t[hs, hc, :], True, True, [qT.b, kcT.b], [p2.b])
                        if nine:
                            P.mm(p2.t[:, 0:64], qa, kT.t[hs, hc, r0a * 64 + 512:r0a * 64 + 576], True, True, [qT.b, kT.b], [p2.b])
                        P.tt("dve", sb_.t[:, 0:512], p1.t[:, :512], bias.t[:, h, pat, 0:512], ALU.add, [p1.b, bias.b], [sb_.b])
                        if nine:
                            P.tt("dve", sb_.t[:, 512:576], p2.t[:, 0:64], bias.t[:, h, pat, 512:576], ALU.add, [p2.b, bias.b], [sb_.b])
                    else:
                        P.mm(p2.t[:, 64:320], qa, kcT.t[hs, hc, :], True, True, [qT.b, kcT.b], [p2.b])
                    P.cp("act", sb_.t[:, nnb:nnb + 256], p2.t[:, 64:320], [p2.b], [sb_.b])
                    P.add("dve", lambda e, sb_=sb_, stt_=stt_, nk=nk: e.reduce_max(stt_.t[:, 0:1], sb_.t[:, :nk], AX.X),
                          [sb_.b], [stt_.b])
                    P.ts1("dve", stt_.t[:, 1:2], stt_.t[:, 0:1], -1.0, ALU.mult, [stt_.b], [stt_.b])
                    P.add("pool", lambda e, stt_=stt_: e.memset(stt_.t[:, 2:3], 0.0), [stt_.b], [stt_.b])
                    pb_ = pr.next()
                    P.act(pb_.t[:, :nk], sb_.t[:, :nk], AF.Exp, [sb_.b, stt_.b], [pb_.b, stt_.b],
                          bias=stt_.t[:, 1:2], scale=1.0, accum_out=stt_.t[:, 2:3])
                    P.add("dve", lambda e, stt_=stt_: e.reciprocal(stt_.t[:, 3:4], stt_.t[:, 2:3]), [stt_.b], [stt_.b])
                    heads.append((pb_, stt_))
                return (r, r0a, nine, nnb, heads)

            def stage_b(state):
                r, r0a, nine, nnb, heads = state
                chunks = []
                if grid:
                    for j in range(4):
                        chunks.append((j * 128, 128, "nb", j))
                    if nine:
                        chunks.append((512, 64, "nb", 4))
                chunks.append((nnb, 128, "cx", 0))
                chunks.append((nnb + 128, 128, "cx", 1))
                nj = len(chunks)
                ot = orow.next()
                pbks = [self.psb() for _ in range(4)]
                for j, (c0, ncol, kind, idx) in enumerate(chunks):
                    for h in range(4):
                        pb_ = heads[h][0]
                        P.tr(pbks[h].tb[:ncol, j * 128:(j + 1) * 128], pb_.t[:, c0:c0 + ncol], self.ident.t[:],
                             [pb_.b, self.ident.b], [pbks[h].b])
                pTs = []
                for h in range(4):
                    pT = ptr_.next()
                    eng = "dve" if h % 2 == 0 else "act"
                    if nine:
                        P.cp(eng, pT.t[:, 0:4, :], pbks[h].tb[:, 0:512].rearrange("p (a c) -> p a c", c=128), [pbks[h].b], [pT.b])
                        P.cp(eng, pT.t[:64, 4, :], pbks[h].tb[:64, 512:640], [pbks[h].b], [pT.b])
                        P.cp(eng, pT.t[:, 5:7, :], pbks[h].tb[:, 640:896].rearrange("p (a c) -> p a c", c=128), [pbks[h].b], [pT.b])
                    else:
                        P.cp(eng, pT.t[:, :nj, :], pbks[h].tb[:, :nj * 128].rearrange("p (a c) -> p a c", c=128),
                             [pbks[h].b], [pT.b])
                    pTs.append(pT)
                pos = [self.psf() for _ in range(4)]
                jobs = []
                for h in range(4):
                    steps = []
                    for j, (c0, ncol, kind, idx) in enumerate(chunks):
                        if kind == "nb":
                            vt, vb = ve.t[:ncol, r0a // 2 + idx, h * 64:(h + 1) * 64], ve.b
                        else:
                            vt, vb = vc.t[:, idx, h * 64:(h + 1) * 64], vc.b
                        steps.append((pTs[h].t[:ncol, j, :], vt, [pTs[h].b, vb]))
                    jobs.append((pos[h].t[:, :64], pos[h].b, steps))
                self.mm_jobs(jobs)
                for h in range(4):
                    stt_ = heads[h][1]
                    P.act(ot.t[:, h * 64:(h + 1) * 64], pos[h].t[:, :64], AF.Copy, [pos[h].b, stt_.b], [ot.b], scale=stt_.t[:, 3:4])
                pbk2 = [self.psb(), self.psb()]
                for c in range(2):
                    P.tr(pbk2[c].tb[:, 0:128], ot.t[:, c * 128:(c + 1) * 128], self.ident.t[:],
                         [ot.b, self.ident.b], [pbk2[c].b])
                    P.cp("dve" if c == 0 else "act", yT.t[:, c, r * 64:r * 64 + 128], pbk2[c].tb[:, :128], [pbk2[c].b], [yT.b])

            prev = stage_a(0)
            for pi in range(1, npair):
                cur = stage_a(pi)
                stage_b(prev)
                prev = cur
            stage_b(prev)
            P.dma(S["ybr_" + n][3].rearrange("(c p) t -> p c t", p=128), yT.t[:], reads=[yT.b], q="pool")

    def phase_merge(self, l, s, xsrc, xdst):
        P, I, S = self.P, self.I, self.S
        n, L, BS, NB, w = s.name, s.L, s.BS, s.NB, s.which
        with P.phase():
            wbr = P.sb([128, 8, 1024], BF16, "wbr")
            wo = P.sb([128, 8, 1024], BF16, "wo")
            wst = P.ring(2, [128, 2, 1024], F32, "mwst")
            wbv = I["w_branch"][l].rearrange("b (k p) n -> p (b k) n", p=128)
            wov = I["w_out"][l].rearrange("(k p) n -> p k n", p=128)
            for i in range(4):
                ws = wst.next()
                P.dma(ws.t[:], wbv[:, 2 * i:2 * i + 2, :], writes=[ws.b])
                P.cp("pool", wbr.t[:, 2 * i:2 * i + 2, :], ws.t[:], [ws.b], [wbr.b])
            for i in range(4):
                ws = wst.next()
                P.dma(ws.t[:], wov[:, 2 * i:2 * i + 2, :], writes=[ws.b])
                P.cp("pool", wo.t[:, 2 * i:2 * i + 2, :], ws.t[:], [ws.b], [wo.b])
            ybr = P.ring(2, [128, 8, BS], BF16, "mybr")
            gr = P.ring(3, [128, 4, BS], BF16, "mgate")
            tb = P.ring(8, [128, BS], F32, "mtb")
            mg = P.ring(2, [128, 8, BS], BF16, "mmg")
            xr = P.ring(5, [128, BS], F32, "mxr")
            xo = P.ring(5, [128, BS], F32, "mxo")
            yv = S["ybr_" + n].rearrange("b (k p) t -> p (b k) t", p=128)
            gv = S["gate_" + n].rearrange("(b c p) t -> p b c t", p=128, c=8)
            for blk in range(NB):
                sl = slice(blk * BS, (blk + 1) * BS)
                yt = ybr.next()
                P.dma(yt.t[:], yv[:, :, sl], writes=[yt.b])
                m = mg.next()
                for dc in range(8):
                    g = gr.next()
                    P.dma(g.t[:], gv[:, :, dc, sl], writes=[g.b])
                    ts_ = []
                    pss = [self.psf() for _ in range(4)]
                    self.mm_jobs([(pss[b].t[:, :BS], pss[b].b,
                                   [(wbr.t[:, 2 * b + kc, dc * 128:(dc + 1) * 128], yt.t[:, 2 * b + kc, :], [wbr.b, yt.b])
                                    for kc in range(2)]) for b in range(4)])
                    for b in range(4):
                        t = tb.next()
                        P.tt("dve", t.t[:], pss[b].t[:, :BS], g.t[:, b, :], ALU.mult, [pss[b].b, g.b], [t.b])
                        ts_.append(t)
                    P.tt("pool", ts_[0].t[:], ts_[0].t[:], ts_[1].t[:], ALU.add, [ts_[0].b, ts_[1].b], [ts_[0].b])
                    P.tt("pool", ts_[2].t[:], ts_[2].t[:], ts_[3].t[:], ALU.add, [ts_[2].b, ts_[3].b], [ts_[2].b])
                    P.tt("pool", m.t[:, dc, :], ts_[0].t[:], ts_[2].t[:], ALU.add, [ts_[0].b, ts_[2].b], [m.b])
                for d0 in (0, 4):
                    pss = [self.psf() for _ in range(4)]
                    self.mm_jobs([(pss[i].t[:, :BS], pss[i].b,
                                   [(wo.t[:, k, (d0 + i) * 128:(d0 + i + 1) * 128], m.t[:, k, :], [wo.b, m.b]) for k in range(8)])
                                  for i in range(4)])
                    for i in range(4):
                        dc = d0 + i
                        xt = xr.next()
                        P.dma(xt.t[:], xsrc[dc * 128:(dc + 1) * 128, sl], writes=[xt.b])
                        o = xo.next()
                        P.stt("dve", o.t[:], pss[i].t[:, :BS], self.mcol(2, dc, w), xt.t[:], ALU.mult, ALU.add,
                              [pss[i].b, xt.b, self.modv.b], [o.b])
                        P.dma(xdst[dc * 128:(dc + 1) * 128, sl], o.t[:], reads=[o.b], q="pool")

    def phase_ffn_up(self, l, s, xsrc):
        P, I, S = self.P, self.I, self.S
        n, L, BS, NB, w = s.name, s.L, s.BS, s.NB, s.which
        with P.phase():
            hx = P.sb([128, 8, L], BF16, "hx2")
            with P.phase():
                xr = P.ring(2, [128, 8, BS], F32, "xr")
                sqr = P.ring(2, [128, 8, BS], BF16, "sq")
                rsr = P.ring(2, [128, BS], F32, "rs")
                tmpr = P.ring(3, [128, BS], F32, "tmp")
                for blk in range(NB):
                    def out_fn(k, tmp, blk=blk):
                        P.act(hx.t[:, k, blk * BS:(blk + 1) * BS], tmp.t[:], AF.Identity, [tmp.b, self.modv.b], [hx.b],
                              bias=self.mcol(3, k, w), scale=1.0)
                    self.norm_block(s, xsrc, blk, lambda k: self.avec.t[:, 16 + 2 * k + w:16 + 2 * k + w + 1], None,
                                    xr, sqr, rsr, tmpr, out_fn)
            wst = P.ring(2, [128, 8, 256], F32, "fwst")
            wbf = P.ring(3, [128, 8, 256], BF16, "fwbf")
            ub = P.ring(2, [128, L], BF16, "fub")
            gb = P.ring(2, [128, L + 2], F32, "fgb")
            t1r = P.ring(1, [128, L], F32, "ft1")
            ar = P.ring(2, [128, L], BF16, "far")
            wv = I["ffn_w_up"][l].rearrange("(k p) n -> p k n", p=128)

            def load_w(j):
                ws = wst.next()
                P.dma(ws.t[:, :, 0:128], wv[:, :, j * 128:(j + 1) * 128], writes=[ws.b])
                P.dma(ws.t[:, :, 128:256], wv[:, :, FF + j * 128:FF + (j + 1) * 128], writes=[ws.b])
                wb_ = wbf.next()
                P.cp("pool", wb_.t[:], ws.t[:], [ws.b], [wb_.b])
                return wb_

            wq = [load_w(0), load_w(1)]
            for j in range(NFF):
                wb = wq[j]
                u = ub.next()
                g = gb.next()
                P.add("pool", lambda e, g=g: e.memset(g.t[:, 0:1], 0.0), [], [g.b])
                P.add("pool", lambda e, g=g: e.memset(g.t[:, L + 1:L + 2], 0.0), [], [g.b])
                for b0 in range(0, NB, 2):
                    blks = list(range(b0, min(b0 + 2, NB)))
                    pss = [(self.psf(), self.psf()) for _ in blks]
                    jobs = []
                    for i, blk in enumerate(blks):
                        sl = slice(blk * BS, (blk + 1) * BS)
                        jobs.append((pss[i][0].t[:, :BS], pss[i][0].b, [(wb.t[:, k, 0:128], hx.t[:, k, sl], [wb.b, hx.b]) for k in range(8)]))
                        jobs.append((pss[i][1].t[:, :BS], pss[i][1].b, [(wb.t[:, k, 128:256], hx.t[:, k, sl], [wb.b, hx.b]) for k in range(8)]))
                    self.mm_jobs(jobs)
                    for i, blk in enumerate(blks):
                        sl = slice(blk * BS, (blk + 1) * BS)
                        pu, pg = pss[i]
                        P.cp("dve", u.t[:, sl], pu.t[:, :BS], [pu.b], [u.b])
                        P.cp("act", g.t[:, 1 + blk * BS:1 + (blk + 1) * BS], pg.t[:, :BS], [pg.b], [g.b])
                if j + 2 < NFF:
                    wq.append(load_w(j + 2))
                t1 = t1r.next()
                eng = "pool"
                P.ts(eng, t1.t[:], g.t[:, 0:L], self.vcol("fcw", l, 0 * NFF + j), self.vcol("fcb", l, j), ALU.mult, ALU.add,
                     [g.b, self.vecs.b], [t1.b])
                P.stt(eng, t1.t[:], g.t[:, 1:L + 1], self.vcol("fcw", l, 1 * NFF + j), t1.t[:], ALU.mult, ALU.add,
                      [g.b, t1.b, self.vecs.b], [t1.b])
                P.stt("dve", t1.t[:], g.t[:, 2:L + 2], self.vcol("fcw", l, 2 * NFF + j), t1.t[:], ALU.mult, ALU.add,
                      [g.b, t1.b, self.vecs.b], [t1.b])
                P.act(t1.t[:], t1.t[:], AF.Silu, [t1.b], [t1.b])
                a = ar.next()
                P.tt("dve", a.t[:], t1.t[:], u.t[:], ALU.mult, [t1.b, u.b], [a.b])
                P.dma(S["aT_" + n][j * 128:(j + 1) * 128, :], a.t[:], reads=[a.b], q="pool")

    def phase_ffn_down(self, l, s, xsrc, xdst):
        P, I, S = self.P, self.I, self.S
        n, L, BS, NB, w = s.name, s.L, s.BS, s.NB, s.which
        with P.phase():
            wd = P.sb([128, NFF, 1024], BF16, "wd")
            wst = P.ring(2, [128, 2, 1024], F32, "dwst")
            wv = I["ffn_w_down"][l].rearrange("(k p) n -> p k n", p=128)
            for i in range(NFF // 2):
                ws = wst.next()
                P.dma(ws.t[:], wv[:, 2 * i:2 * i + 2, :], writes=[ws.b])
                P.cp("pool", wd.t[:, 2 * i:2 * i + 2, :], ws.t[:], [ws.b], [wd.b])
            ar = P.ring(2, [128, NFF, BS], BF16, "dar")
            xr = P.ring(5, [128, BS], F32, "dxr")
            xo = P.ring(5, [128, BS], F32, "dxo")
            av = S["aT_" + n].rearrange("(k p) t -> p k t", p=128)
            for blk in range(NB):
                sl = slice(blk * BS, (blk + 1) * BS)
                a = ar.next()
                P.dma(a.t[:], av[:, :, sl], writes=[a.b])
                for d0 in (0, 4):
                    pss = [self.psf() for _ in range(4)]
                    self.mm_jobs([(pss[i].t[:, :BS], pss[i].b,
                                   [(wd.t[:, k, (d0 + i) * 128:(d0 + i + 1) * 128], a.t[:, k, :], [wd.b, a.b]) for k in range(NFF)])
                                  for i in range(4)])
                    for i in range(4):
                        dc = d0 + i
                        xt = xr.next()
                        P.dma(xt.t[:], xsrc[dc * 128:(dc + 1) * 128, sl], writes=[xt.b])
                        o = xo.next()
                        P.stt("dve", o.t[:], pss[i].t[:, :BS], self.mcol(5, dc, w), xt.t[:], ALU.mult, ALU.add,
                              [pss[i].b, xt.b, self.modv.b], [o.b])
                        P.dma(xdst[dc * 128:(dc + 1) * 128, sl], o.t[:], reads=[o.b], q="pool")

    def phase_final(self, s, xsrc, outT):
        P = self.P
        BS, NB = s.BS, s.NB
        with P.phase():
            xr = P.ring(2, [128, 8, BS], F32, "xr")
            sqr = P.ring(2, [128, 8, BS], BF16, "sq")
            rsr = P.ring(2, [128, BS], F32, "rs")
            tmpr = P.ring(4, [128, BS], F32, "tmp")
            for blk in range(NB):
                def out_fn(k, tmp, blk=blk):
                    P.dma(outT[k * 128:(k + 1) * 128, blk * BS:(blk + 1) * BS], tmp.t[:], reads=[tmp.b], q="pool")
                self.norm_block(s, xsrc, blk, lambda k: self.vcol("fng", None, k), None, xr, sqr, rsr, tmpr, out_fn)


VSPEC = [("eps", 1, 1), ("n1g", 8, DEPTH), ("n2g", 8, DEPTH), ("bmod", 96, DEPTH), ("pscale", 2, DEPTH),
         ("hcw", 18, DEPTH), ("hcb", 6, DEPTH), ("freq", 1, DEPTH), ("fb1", 1, DEPTH), ("fb2", 1, DEPTH),
         ("fb3", 8, DEPTH), ("hskip", 4, DEPTH), ("fcw", 3 * NFF, DEPTH), ("fcb", NFF, DEPTH), ("fng", 8, 1)]
VOFF, VLEN = {}, {}
_o = 0
for _n, _ln, _rep in VSPEC:
    VOFF[_n] = _o
    VLEN[_n] = _ln
    _o += _ln * _rep
NVEC = _o


def fm(v):
    v = np.asarray(v, np.float32)
    return np.ascontiguousarray(v.reshape(-1, 128).T)


def pad64(v):
    o = np.zeros((128, 1), np.float32)
    o[:64, 0] = v
    return o


_CONST = {}


def consts():
    if _CONST:
        return _CONST
    C = {}
    C["ident"] = np.eye(128, dtype=np.float32).astype(NPBF)
    for name, L in (("x", LX), ("c", LC)):
        t = np.arange(L, dtype=np.int64)
        tk = (t[:, None] * t[None, :])
        ang = 2.0 * np.pi * (tk % L).astype(np.float64) / L
        C["fc_" + name] = np.cos(ang).astype(np.float32).astype(NPBF)
        C["fs_" + name] = np.sin(ang).astype(np.float32).astype(NPBF)
        N2 = 2 * L
        Lh, FB = L // 2, L // 2 + 128
        tau = np.arange(Lh, dtype=np.int64)[:, None]
        ff = np.arange(FB, dtype=np.int64)[None, :]
        valid = (ff <= Lh).astype(np.float64)
        for par, nm in ((0, "e"), (1, "o")):
            ang2 = 2.0 * np.pi * (((2 * tau + par) * ff) % N2).astype(np.float64) / N2
            mc = np.cos(ang2) * valid
            ms = -np.sin(ang2) * valid
            C["m%sc_%s" % (nm, name)] = mc.astype(np.float32).astype(NPBF)
            C["m%ss_%s" % (nm, name)] = ms.astype(np.float32).astype(NPBF)
            C["i%sc_%s" % (nm, name)] = np.ascontiguousarray(mc.T).astype(np.float32).astype(NPBF)
            C["i%ss_%s" % (nm, name)] = np.ascontiguousarray(ms.T).astype(np.float32).astype(NPBF)
        tt = np.linspace(0.0, 1.0, L, dtype=np.float32)[:, None]
        bands = np.linspace(1e-4, 15, 16, dtype=np.float32)[None, :]
        angf = (np.float32(2.0 * math.pi / L) * np.arange(L, dtype=np.float32)[:, None]) * bands
        feats = np.concatenate([tt, np.cos(angf), -np.sin(angf)], axis=-1).astype(np.float32)
        C["feats_" + name] = np.ascontiguousarray(feats.T)
        deltas = np.linspace(math.log(1e-2) / 1.5, math.log(1e-2) / 0.3, 256, dtype=np.float32)
        dec = np.exp(-tt * np.abs(deltas)[None, :]).astype(np.float32) + np.float32(0.05)
        C["dec_" + name] = np.ascontiguousarray(dec.T)
        corr = np.zeros((128, 2, 16), np.float32)
        for g, wd in enumerate((2, 4, 8, 16)):
            pos = np.concatenate([np.arange(8), np.arange(L - 8, L)])
            lo = np.clip(pos - wd // 2, 0, L)
            hi = np.clip(pos - wd // 2 + wd, 0, L)
            hp = (g % 2) * 64
            corr[hp:hp + 64, g // 2, :] = (1.0 / (hi - lo).astype(np.float32))[None, :]
        C["pcorr_" + name] = corr
    c = np.arange(64)
    angc = 2.0 * np.pi * ((c[:, None] * c[None, :]) % 64) / 64.0
    cc = np.zeros((128, 128))
    ss = np.zeros((128, 128))
    for g in range(2):
        cc[g * 64:(g + 1) * 64, g * 64:(g + 1) * 64] = np.cos(angc)
        ss[g * 64:(g + 1) * 64, g * 64:(g + 1) * 64] = -np.sin(angc)
    for name, L in (("x", LX), ("c", LC)):
        sc_ = 1.0 / math.sqrt(64.0 * L)
        C["chdft_" + name] = np.concatenate([cc * sc_, ss * sc_], axis=1).astype(np.float32).astype(NPBF)
    _CONST.update(C)
    return _CONST


def host_inputs(inp, b, seqs=("c", "x")):
    C = consts()
    m = {}
    m["xT"] = np.ascontiguousarray(inp["x"][b].T)
    m["cT"] = np.ascontiguousarray(inp["ctx"][b].T)
    cv = np.zeros((128, 16), np.float32)
    cf, ccf = fm(inp["c"][b]), fm(inp["c_ctx"])
    cv[:, 0::2] = cf
    cv[:, 1::2] = ccf
    m["cvec"] = cv
    for k in ("w_mod", "w_in", "w_branch", "w_out", "ffn_w_up", "ffn_w_down", "pool_w"):
        m[k] = np.ascontiguousarray(inp[k], dtype=np.float32)
    m["fw1"] = np.ascontiguousarray(inp["hyena_filt_w1"], dtype=np.float32)
    m["fw2"] = np.ascontiguousarray(inp["hyena_filt_w2"], dtype=np.float32)
    m["fw3"] = np.ascontiguousarray(inp["hyena_filt_w3"], dtype=np.float32)
    V = np.zeros((128, NVEC), np.float32)

    def put(name, l, arr):
        o = VOFF[name] + (0 if l is None else l * VLEN[name])
        V[:, o:o + arr.shape[1]] = arr

    put("eps", None, np.full((128, 1), EPS, np.float32))
    for l in range(DEPTH):
        put("n1g", l, fm(inp["norm1_g"][l]))
        put("n2g", l, fm(inp["norm2_g"][l]))
        put("bmod", l, np.repeat(fm(inp["b_mod"][l]), 2, axis=1))
        put("pscale", l, fm(inp["pool_scale"][l]))
        put("hcw", l, np.concatenate([fm(inp["hyena_conv_w"][l][j]) for j in range(3)], axis=1))
        put("hcb", l, fm(inp["hyena_conv_b"][l]))
        put("freq", l, pad64(inp["hyena_freq"][l]))
        put("fb1", l, pad64(inp["hyena_filt_b1"][l]))
        put("fb2", l, pad64(inp["hyena_filt_b2"][l]))
        put("fb3", l, fm(inp["hyena_filt_b3"][l]))
        put("hskip", l, np.concatenate([fm(inp["hyena_skip"][l][o]) for o in range(2)], axis=1))
        put("fcw", l, np.concatenate([fm(inp["ffn_conv_w"][l][j]) for j in range(3)], axis=1))
        put("fcb", l, fm(inp["ffn_conv_b"][l]))
    put("fng", None, fm(inp["final_norm_g"]))
    m["vecs"] = V
    rpb = np.asarray(inp["na_rpb"], np.float32)
    col = np.arange(64)
    c0 = np.clip(col - 8, 0, 48)
    col_ok = (col[None, :] >= c0[:, None]) & (col[None, :] < c0[:, None] + 16)
    dc = np.clip(col[None, :] - col[:, None], -15, 15) + 15
    bias = np.full((DEPTH, 128, 4, 5, 9, 64), np.float32(-1e30), np.float32)
    for pat, r in enumerate((4, 0, 2, 60, 62)):
        U = min(max(r - 4, 0), 56)
        for half, rr in enumerate((r, r + 1)):
            r0 = min(max(rr - 4, 0), 56)
            for i in range(8):
                ui = r0 + i - U
                dr = r0 + i - rr + 7
                g = rpb[:, :, dr][:, :, dc]
                g = np.where(col_ok[None, None], g, np.float32(-1e30))
                bias[:, half * 64:(half + 1) * 64, :, pat, ui, :] = np.transpose(g, (0, 2, 1, 3))
    m["bias"] = bias.reshape(DEPTH, 128, 4, 5, 576)
    m["ident"] = C["ident"]
    m["alt"] = np.where(np.arange(128) % 2 == 0, 1.0, -1.0).astype(np.float32).reshape(128, 1).astype(NPBF)
    for s in seqs:
        for k in ("fc_", "fs_", "mec_", "mes_", "moc_", "mos_", "iec_", "ies_", "ioc_", "ios_", "feats_", "dec_", "pcorr_", "chdft_"):
            m[k + s] = C[k + s]
    return m


def finish_consts(m, seqs):
    return m


_NC = {}


def kernel(**inputs):
    inp = {k: np.asarray(v) for k, v in inputs.items()}
    if "nc" not in _NC:
        _NC["nc"] = Builder().build()
    nc = _NC["nc"]
    in_maps = [host_inputs(inp, b) for b in range(8)]
    res = run_bass_kernel_spmd(nc, in_maps, core_ids=list(range(8)))
    out = np.stack([np.ascontiguousarray(r["outT"].T) for r in res.results], axis=0)
    return out.astype(np.float32)
```

```python
import contextlib
import math
import numpy as np
import ml_dtypes
import concourse.bass as bass
import concourse.mybir as mybir
from concourse.bass_utils import run_bass_kernel_spmd

F32 = mybir.dt.float32
BF16 = mybir.dt.bfloat16
AF = mybir.ActivationFunctionType
ALU = mybir.AluOpType
AX = mybir.AxisListType
NPBF = ml_dtypes.bfloat16

D = 1024
KD = 8
LX = 4096
LC = 256
DEPTH = 2
FF = 2816
NFF = 22
INW = 6144
EPS = 1e-6


class Buf:
    __slots__ = ("name", "lw", "rd")

    def __init__(self, name=""):
        self.name = name
        self.lw = None
        self.rd = []


class Op:
    __slots__ = ("eng", "fn", "deps", "is_dma", "sig", "need", "pos")

    def __init__(self, eng, fn, is_dma):
        self.eng = eng
        self.fn = fn
        self.deps = []
        self.is_dma = is_dma
        self.sig = None
        self.need = False
        self.pos = 0


class Tl:
    __slots__ = ("t", "b", "_tb")

    def __init__(self, t, name=""):
        self.t = t
        self.b = Buf(name)
        self._tb = None

    @property
    def tb(self):
        if self._tb is None:
            self._tb = self.t[:].bitcast(BF16)
        return self._tb


class Ring:
    def __init__(self, items):
        self.items = items
        self.i = 0

    def next(self):
        x = self.items[self.i % len(self.items)]
        self.i += 1
        return x


ENGS = ("pe", "act", "dve", "pool", "sp")
SEM_ROT = 1500
DMA_SLOTS = 8


class Prog:
    def __init__(self, nc):
        self.nc = nc
        self.ops = {e: [] for e in ENGS}
        self.stack = contextlib.ExitStack()
        self.cur = self.stack
        self.dma_hist = {e: [] for e in ENGS}
        self.nuid = 0

    def sb(self, shape, dt, name="t"):
        self.nuid += 1
        t = self.cur.enter_context(self.nc.sbuf_tensor(f"{name}_{self.nuid}", list(shape), dt))
        return Tl(t, name)

    def ring(self, n, shape, dt, name="r"):
        return Ring([self.sb(shape, dt, name) for _ in range(n)])

    def ps(self, shape, dt, name="p"):
        self.nuid += 1
        t = self.cur.enter_context(self.nc.psum_tensor(f"{name}_{self.nuid}", list(shape), dt))
        return Tl(t, name)

    @contextlib.contextmanager
    def phase(self):
        old = self.cur
        with contextlib.ExitStack() as st:
            self.cur = st
            yield
            self.barrier()
        self.cur = old

    def add(self, eng, fn, reads=(), writes=(), dma=False):
        op = Op(eng, fn, dma)
        deps = []
        for b in reads:
            if b.lw is not None:
                deps.append(b.lw)
        for b in writes:
            if b.lw is not None:
                deps.append(b.lw)
            deps.extend(b.rd)
        if dma:
            h = self.dma_hist[eng]
            if len(h) >= DMA_SLOTS:
                deps.append(h[len(h) - DMA_SLOTS])
            h.append(op)
        seen = set()
        for d in deps:
            if eng == "pe" and d.eng == "pe" and d.fn is not None:
                continue
            if id(d) not in seen and d is not op:
                seen.add(id(d))
                op.deps.append(d)
                d.need = True
        for b in reads:
            b.rd.append(op)
        for b in writes:
            b.lw = op
            b.rd = []
        op.pos = len(self.ops[eng])
        self.ops[eng].append(op)
        return op

    def barrier(self):
        lasts = []
        for e in ENGS:
            for op in reversed(self.ops[e]):
                if op.fn is not None and not op.is_dma:
                    lasts.append(op)
                    break
            lasts.extend(self.dma_hist[e][-DMA_SLOTS:])
        for e in ENGS:
            op = Op(e, None, False)
            for d in lasts:
                op.deps.append(d)
                d.need = True
            self.ops[e].append(op)

    def emit(self):
        nc = self.nc
        st = self.stack
        semcache = {}

        def sem(key):
            if key not in semcache:
                semcache[key] = st.enter_context(nc.semaphore("s_" + "_".join(str(k) for k in key)))
            return semcache[key]

        for e in ENGS:
            cnt = 0
            dcnt = 0
            for op in self.ops[e]:
                if op.fn is None:
                    continue
                if op.is_dma:
                    slot = dcnt % DMA_SLOTS
                    u = dcnt // DMA_SLOTS
                    op.sig = (("d", e, slot, u // SEM_ROT), 16 * (u % SEM_ROT + 1), 16)
                    dcnt += 1
                elif op.need:
                    op.sig = (("c", e, cnt // SEM_ROT), cnt % SEM_ROT + 1, 1)
                    cnt += 1
        for e in ENGS:
            for op in self.ops[e]:
                if op.sig is not None:
                    sem(op.sig[0])

        def run_engine(e, eng):
            waited = {}
            for op in self.ops[e]:
                for d in op.deps:
                    if d.sig is None:
                        continue
                    key, val, _ = d.sig
                    if waited.get(key, 0) < val:
                        eng.wait_ge(sem(key), val)
                        waited[key] = val
                if op.fn is None:
                    continue
                ins = op.fn(eng)
                if op.sig is not None:
                    ins.then_inc(sem(op.sig[0]), op.sig[2])

        with nc.Block() as block:
            @block.tensor
            def _(eng):
                run_engine("pe", eng)

            @block.scalar
            def _(eng):
                run_engine("act", eng)

            @block.vector
            def _(eng):
                run_engine("dve", eng)

            @block.gpsimd
            def _(eng):
                run_engine("pool", eng)

            @block.sync
            def _(eng):
                run_engine("sp", eng)

    def dma(self, out, in_, reads=(), writes=(), q="sp"):
        return self.add(q, lambda e: e.dma_start(out=out, in_=in_), reads, writes, dma=True)

    def mm(self, out, lhsT, rhs, start, stop, reads=(), writes=()):
        return self.add("pe", lambda e: e.matmul(out, lhsT, rhs, start=start, stop=stop), reads, writes)

    def tr(self, out, in_, ident, reads=(), writes=()):
        return self.add("pe", lambda e: e.transpose(out, in_, ident), reads, writes)

    def act(self, out, in_, func, reads=(), writes=(), **kw):
        return self.add("act", lambda e: e.activation(out, in_, func, **kw), reads, writes)

    def cp(self, eng, out, in_, reads=(), writes=()):
        if eng == "act":
            return self.add("act", lambda e: e.copy(out, in_), reads, writes)
        return self.add(eng, lambda e: e.tensor_copy(out, in_), reads, writes)

    def tt(self, eng, out, a, b, op, reads=(), writes=()):
        return self.add(eng, lambda e: e.tensor_tensor(out, a, b, op), reads, writes)

    def ts(self, eng, out, a, s1, s2, op0, op1, reads=(), writes=()):
        return self.add(eng, lambda e: e.tensor_scalar(out, a, s1, s2, op0, op1), reads, writes)

    def ts1(self, eng, out, a, s1, op0, reads=(), writes=()):
        return self.add(eng, lambda e: e.tensor_single_scalar(out, a, s1, op0), reads, writes)

    def stt(self, eng, out, a, s, b, op0, op1, reads=(), writes=()):
        eng = "dve"
        return self.add(eng, lambda e: e.scalar_tensor_tensor(out, a, s, b, op0, op1), reads, writes)


class Seq:
    def __init__(self, name, L, which):
        self.name = name
        self.L = L
        self.which = which
        self.BS = min(512, L)
        self.NB = L // self.BS
        self.NT = L // 128


class Builder:
    def __init__(self, debug=False, layers=DEPTH, do_x=True):
        self.debug = debug
        self.layers = layers
        self.do_x = do_x
        self.nc = bass.Bass("TRN2", target_bir_lowering=False)
        self.P = Prog(self.nc)
        self.evac_i = 0

    def din(self, name, shape, dt=F32):
        return self.nc.dram_tensor(name, list(shape), dt, kind="ExternalInput").ap()

    def dscr(self, name, shape, dt):
        kind = "ExternalOutput" if self.debug else "Internal"
        return self.nc.dram_tensor(name, list(shape), dt, kind=kind).ap()

    def psf(self):
        held = getattr(self, "held", ())
        while True:
            t = self.PSF.next()
            if id(t) not in held:
                return t

    def psb(self):
        return self.psf()

    def mm_jobs(self, jobs):
        P = self.P
        n = max(len(j[2]) for j in jobs)
        for k in range(n):
            for (pap, pbuf, steps) in jobs:
                if k < len(steps):
                    lhsT, rhs, rb = steps[k]
                    P.mm(pap, lhsT, rhs, k == 0, k == len(steps) - 1, rb, [pbuf])

    def evac_eng(self):
        self.evac_i += 1
        return "act" if self.evac_i % 2 else "dve"

    def build(self):
        P = self.P
        nc = self.nc
        sx = Seq("x", LX, 0)
        sc = Seq("c", LC, 1)
        seqs = [sc, sx] if self.do_x else [sc]
        I = {}
        I["xT"] = self.din("xT", [D, LX])
        I["cT"] = self.din("cT", [D, LC])
        I["cvec"] = self.din("cvec", [128, 16])
        I["w_mod"] = self.din("w_mod", [DEPTH, D, 6 * D])
        I["w_in"] = self.din("w_in", [DEPTH, D, INW])
        I["w_branch"] = self.din("w_branch", [DEPTH, 4, 256, D])
        I["w_out"] = self.din("w_out", [DEPTH, D, D])
        I["ffn_w_up"] = self.din("ffn_w_up", [DEPTH, D, 2 * FF])
        I["ffn_w_down"] = self.din("ffn_w_down", [DEPTH, FF, D])
        I["pool_w"] = self.din("pool_w", [DEPTH, 4, 64, 64])
        I["vecs"] = self.din("vecs", [128, NVEC])
        I["fw1"] = self.din("fw1", [DEPTH, 33, 64])
        I["fw2"] = self.din("fw2", [DEPTH, 64, 64])
        I["fw3"] = self.din("fw3", [DEPTH, 64, 1024])
        I["bias"] = self.din("bias", [DEPTH, 128, 4, 5, 576])
        I["ident"] = self.din("ident", [128, 128], BF16)
        I["alt"] = self.din("alt", [128, 1], BF16)
        for s in seqs:
            L = s.L
            I["fc_" + s.name] = self.din("fc_" + s.name, [L, L], BF16)
            I["fs_" + s.name] = self.din("fs_" + s.name, [L, L], BF16)
            for k_ in ("mec_", "mes_", "moc_", "mos_"):
                I[k_ + s.name] = self.din(k_ + s.name, [L // 2, L // 2 + 128], BF16)
            for k_ in ("iec_", "ies_", "ioc_", "ios_"):
                I[k_ + s.name] = self.din(k_ + s.name, [L // 2 + 128, L // 2], BF16)
            I["feats_" + s.name] = self.din("feats_" + s.name, [33, L])
            I["dec_" + s.name] = self.din("dec_" + s.name, [256, L])
            I["pcorr_" + s.name] = self.din("pcorr_" + s.name, [128, 2, 16])
            I["chdft_" + s.name] = self.din("chdft_" + s.name, [128, 256], BF16)
        self.I = I
        outT = self.nc.dram_tensor("outT", [D, LX], F32, kind="ExternalOutput").ap()
        S = {}
        for s in seqs:
            n = s.name
            L = s.L
            S["xa_" + n] = self.dscr("xa_" + n, [D, L], F32)
            S["xb_" + n] = self.dscr("xb_" + n, [D, L], F32)
            S["pool_" + n] = self.dscr("pool_" + n, [256, L], F32)
            S["fnet_" + n] = self.dscr("fnet_" + n, [L, 256], BF16)
            S["hy_" + n] = self.dscr("hy_" + n, [768, L], F32)
            S["q_" + n] = self.dscr("q_" + n, [256, L], BF16)
            S["k_" + n] = self.dscr("k_" + n, [256, L], BF16)
            S["v_" + n] = self.dscr("v_" + n, [L + 64, 256], BF16)
            S["gate_" + n] = self.dscr("gate_" + n, [4096, L], BF16)
            S["ybr_" + n] = self.dscr("ybr_" + n, [4, 256, L], BF16)
            S["kf_" + n] = self.dscr("kf_" + n, [2, 2, 4, 128, L // 2 + 128], F32)
            S["x2_" + n] = self.dscr("x2_" + n, [256, L], BF16)
            S["aT_" + n] = self.dscr("aT_" + n, [FF, L], BF16)
        self.S = S

        with P.stack:
            self.PSF = Ring([P.ps([128, 512], F32, "psf") for _ in range(8)])
            self.PSB = self.PSF
            self.ident = P.sb([128, 128], BF16, "ident")
            self.ones = P.sb([128, 128], BF16, "ones")
            self.vecs = P.sb([128, NVEC], F32, "vecs")
            self.modv = P.sb([128, 96], F32, "modv")
            self.avec = P.sb([128, 32], F32, "avec")
            self.cact = P.sb([128, 16], F32, "cact")
            P.dma(self.ident.t[:], I["ident"], writes=[self.ident.b])
            P.dma(self.vecs.t[:], I["vecs"], writes=[self.vecs.b])
            P.add("dve", lambda e: e.memset(self.ones.t[:], 1.0), [], [self.ones.b])
            P.dma(self.cact.t[:], I["cvec"], writes=[self.cact.b])
            P.act(self.cact.t[:], self.cact.t[:], AF.Silu, [self.cact.b], [self.cact.b])

            xin = {"x": I["xT"], "c": I["cT"]}
            for l in range(self.layers):
                last = l == DEPTH - 1
                self.phase_mod(l)
                for s in seqs:
                    n = s.name
                    xa, xb = S["xa_" + n], S["xb_" + n]
                    if s.which == 1 and last:
                        self.phase_proj(l, s, xin[n], groups=("k", "v"))
                        continue
                    self.phase_proj(l, s, xin[n], groups=None)
                    self.phase_filter(l, s)
                    self.phase_fnet(l, s)
                    self.phase_hyena(l, s)
                    self.phase_attn(l, s)
                    self.phase_merge(l, s, xin[n], xa)
                    self.phase_ffn_up(l, s, xa)
                    self.phase_ffn_down(l, s, xa, xb)
                    xin[n] = xb
            if self.do_x and self.layers == DEPTH:
                self.phase_final(sx, xin["x"], outT)
            else:
                with P.phase():
                    pass
        P.emit()
        return self.nc

    def vcol(self, name, l=None, k=0):
        off = VOFF[name] + (0 if l is None else l * VLEN[name]) + k
        return self.vecs.t[:, off:off + 1]

    def phase_mod(self, l):
        P, I = self.P, self.I
        with P.phase():
            wr = P.ring(3, [128, 8, 512], F32, "wmod")
            pss = [self.psf() for _ in range(4)]
            wv = I["w_mod"][l].rearrange("(k p) n -> p k n", p=128)
            for q in range(12):
                w = wr.next()
                P.dma(w.t[:], wv[:, :, q * 512:(q + 1) * 512], writes=[w.b])
                jobs = []
                for r in range(4):
                    jobs.append((pss[r].t[:, 2 * q:2 * q + 2], pss[r].b,
                                 [(w.t[:, k, r * 128:(r + 1) * 128], self.cact.t[:, 2 * k:2 * k + 2], [w.b, self.cact.b])
                                  for k in range(8)]))
                self.mm_jobs(jobs)
            bm = VOFF["bmod"] + l * 96
            for r in range(4):
                P.tt("dve", self.modv.t[:].rearrange("p (q r w) -> p q r w", r=4, w=2)[:, :, r, :],
                     pss[r].t[:, 0:24].rearrange("p (q w) -> p q w", w=2),
                     self.vecs.t[:, bm:bm + 96].rearrange("p (q r w) -> p q r w", r=4, w=2)[:, :, r, :], ALU.add,
                     [pss[r].b, self.vecs.b], [self.modv.b])
            for i, (gname, scj) in enumerate((("n1g", 1), ("n2g", 4))):
                for k in range(8):
                    j = scj * 8 + k
                    P.ts("dve", self.avec.t[:, i * 16 + 2 * k:i * 16 + 2 * k + 2], self.modv.t[:, 2 * j:2 * j + 2],
                         1.0, self.vcol(gname, l, k), ALU.add, ALU.mult, [self.modv.b, self.vecs.b], [self.avec.b])

    def mcol(self, which_mod, k, w):
        j = which_mod * 8 + k
        return self.modv.t[:, 2 * j + w:2 * j + w + 1]

    def norm_block(self, s, xsrc, blk, a_fn, sh_fn, xr, sqr, rsr, tmpr, out_fn):
        P = self.P
        BS = s.BS
        xt = xr.next()
        P.dma(xt.t[:], xsrc.rearrange("(k p) t -> p k t", p=128)[:, :, blk * BS:(blk + 1) * BS], writes=[xt.b])
        sq = sqr.next()
        P.act(sq.t[:], xt.t[:], AF.Square, [xt.b], [sq.b])
        rs = rsr.next()
        H = BS // 2
        pss = [self.psf(), self.psf()]
        self.mm_jobs([(pss[h].t[:, :H], pss[h].b,
                       [(self.ones.t[:], sq.t[:, k, h * H:(h + 1) * H], [self.ones.b, sq.b]) for k in range(8)]) for h in range(2)])
        for h in range(2):
            P.act(rs.t[:, h * H:(h + 1) * H], pss[h].t[:, :H], AF.Sqrt, [pss[h].b, self.vecs.b], [rs.b],
                  bias=self.vcol("eps"), scale=1.0 / D)
        P.add("dve", lambda e: e.reciprocal(rs.t[:], rs.t[:]), [rs.b], [rs.b])
        for k in range(8):
            tmp = tmpr.next()
            P.stt("dve", tmp.t[:], xt.t[:, k, :], a_fn(k), rs.t[:], ALU.mult, ALU.mult,
                  [xt.b, rs.b, self.avec.b, self.vecs.b], [tmp.b])
            out_fn(k, tmp)

    def phase_proj(self, l, s, xsrc, groups):
        P, I, S = self.P, self.I, self.S
        n, L, BS, NB, w = s.name, s.L, s.BS, s.NB, s.which
        glist = [("pool", 0, 256, "FM", S["pool_" + n], F32, None),
                 ("fnet", 256, 256, "TM", S["fnet_" + n], BF16, None),
                 ("hy", 512, 768, "FM", S["hy_" + n], F32, None),
                 ("q", 1280, 256, "FM", S["q_" + n], BF16, "q"),
                 ("k", 1536, 256, "FM", S["k_" + n], BF16, None),
                 ("v", 1792, 256, "TM", S["v_" + n], BF16, None),
                 ("gate", 2048, 4096, "FM", S["gate_" + n], BF16, "sig")]
        if groups is not None:
            glist = [g for g in glist if g[0] in groups]
        with P.phase():
            hx = P.sb([128, 8, L], BF16, "hx")
            with P.phase():
                xr = P.ring(2, [128, 8, BS], F32, "xr")
                sqr = P.ring(2, [128, 8, BS], BF16, "sq")
                rsr = P.ring(2, [128, BS], F32, "rs")
                tmpr = P.ring(3, [128, BS], F32, "tmp")
                for blk in range(NB):
                    def out_fn(k, tmp, blk=blk):
                        P.act(hx.t[:, k, blk * BS:(blk + 1) * BS], tmp.t[:], AF.Identity, [tmp.b, self.modv.b], [hx.b],
                              bias=self.mcol(0, k, w), scale=1.0)
                    self.norm_block(s, xsrc, blk, lambda k: self.avec.t[:, 2 * k + w:2 * k + w + 1], None,
                                    xr, sqr, rsr, tmpr, out_fn)
            wst = P.ring(2, [128, 8, 256], F32, "wst")
            wbf = P.ring(2, [128, 8, 256], BF16, "wbf")
            ofm32 = P.ring(5, [128, BS], F32, "ofm32")
            ofm16 = P.ring(5, [128, BS], BF16, "ofm16")
            otm = P.ring(5, [128, 256], BF16, "otm")
            wv = I["w_in"][l].rearrange("(k p) n -> p k n", p=128)
            for (gname, c0, ncols, mode, dest, dt, special) in glist:
                for cb in range(ncols // 256):
                    cs = c0 + cb * 256
                    ws = wst.next()
                    P.dma(ws.t[:], wv[:, :, cs:cs + 256], writes=[ws.b])
                    wb = wbf.next()
                    P.cp(self.evac_eng(), wb.t[:], ws.t[:], [ws.b], [wb.b])
                    if mode == "FM":
                        items = [(cc, blk) for cc in range(2) for blk in range(NB)]
                        for i0 in range(0, len(items), 4):
                            grp = items[i0:i0 + 4]
                            pss = [self.psf() for _ in grp]
                            self.mm_jobs([(pss[i].t[:, :BS], pss[i].b,
                                           [(wb.t[:, k, cc * 128:(cc + 1) * 128], hx.t[:, k, blk * BS:(blk + 1) * BS], [wb.b, hx.b])
                                            for k in range(8)]) for i, (cc, blk) in enumerate(grp)])
                            for i, (cc, blk) in enumerate(grp):
                                ps = pss[i]
                                row0 = cb * 256 + cc * 128
                                o = (ofm32 if dt == F32 else ofm16).next()
                                if special == "sig":
                                    P.act(o.t[:], ps.t[:, :BS], AF.Sigmoid, [ps.b], [o.b])
                                elif special == "q":
                                    P.act(o.t[:], ps.t[:, :BS], AF.Copy, [ps.b], [o.b], scale=0.125)
                                else:
                                    P.cp(self.evac_eng(), o.t[:], ps.t[:, :BS], [ps.b], [o.b])
                                P.dma(dest[row0:row0 + 128, blk * BS:(blk + 1) * BS], o.t[:], reads=[o.b], q="pool")
                    else:
                        for t0 in range(0, s.NT, 4):
                            grp = list(range(t0, min(t0 + 4, s.NT)))
                            pss = [self.psf() for _ in grp]
                            self.mm_jobs([(pss[i].t[:, :256], pss[i].b,
                                           [(hx.t[:, k, tt * 128:(tt + 1) * 128], wb.t[:, k, :], [wb.b, hx.b]) for k in range(8)])
                                          for i, tt in enumerate(grp)])
                            for i, tt in enumerate(grp):
                                o = otm.next()
                                P.cp(self.evac_eng(), o.t[:], pss[i].t[:, :256], [pss[i].b], [o.b])
                                P.dma(dest[tt * 128:(tt + 1) * 128, cb * 256:(cb + 1) * 256], o.t[:], reads=[o.b], q="pool")

    def seq_transform(self, s, outs, handler, mring, KT=4):
        P = self.P
        BS, NB, NT = s.BS, s.NB, s.NT
        KT = min(KT, NT)
        for nb in range(NB):
            pss = [self.psf() for _ in outs]
            for g in range(NT // KT):
                loaded = {}
                for (tm, c0, mat) in outs:
                    if id(mat) not in loaded:
                        mt = mring.next()
                        P.dma(mt.t[:, :KT, :BS],
                              mat.rearrange("(g p) n -> p g n", p=128)[:, g * KT:(g + 1) * KT, nb * BS:(nb + 1) * BS],
                              writes=[mt.b])
                        loaded[id(mat)] = mt
                for kk in range(KT):
                    tt = g * KT + kk
                    for oi, (tm, c0, mat) in enumerate(outs):
                        mt = loaded[id(mat)]
                        P.mm(pss[oi].t[:, :BS], tm.t[:, tt, c0:c0 + 128], mt.t[:, kk, :BS], tt == 0, tt == NT - 1,
                             [tm.b, mt.b], [pss[oi].b])
            handler(nb, pss)

    def seq_transform2(self, outs, handler, mring, nt_in, blocks, KT=2):
        P = self.P
        KT = min(KT, nt_in)
        pending = None
        for bi, (b0, bs) in enumerate(blocks):
            pss = [self.psf() for _ in outs]
            for k0 in range(0, nt_in, KT):
                kn = min(KT, nt_in - k0)
                loaded = {}
                for (tm, c0, mat) in outs:
                    if id(mat) not in loaded:
                        mt = mring.next()
                        P.dma(mt.t[:, :kn, :bs], mat.rearrange("(g p) n -> p g n", p=128)[:, k0:k0 + kn, b0:b0 + bs],
                              writes=[mt.b])
                        loaded[id(mat)] = mt
                for kk in range(kn):
                    tt = k0 + kk
                    for oi, (tm, c0, mat) in enumerate(outs):
                        mt = loaded[id(mat)]
                        P.mm(pss[oi].t[:, :bs], tm.t[:, tt, c0:c0 + 128], mt.t[:, kk, :bs], tt == 0, tt == nt_in - 1,
                             [tm.b, mt.b], [pss[oi].b])
            self.held = set(id(p) for p in pss)
            if pending is not None:
                pending()
                pending = None
            self.held = set()
            pending = handler(bi, b0, bs, pss)
        if pending is not None:
            pending()

    def fm_to_tm(self, src_tl, src_fn, nblk, tm, t0, c0):
        P = self.P
        j = 0
        while j < nblk:
            nn = min(16, nblk - j)
            nbk = min(4, nn)
            pbs = [self.psb() for _ in range(nbk)]
            for i in range(nn):
                pb = pbs[i % nbk]
                sl_ = i // nbk
                P.tr(pb.tb[:, sl_ * 128:(sl_ + 1) * 128], src_fn(j + i), self.ident.t[:], [src_tl.b, self.ident.b], [pb.b])
            for bi in range(nbk):
                cnt = len(range(bi, nn, nbk))
                dst = tm.t[:, t0 + j + bi:t0 + j + bi + (cnt - 1) * nbk + 1:nbk, c0:c0 + 128]
                P.cp(self.evac_eng(), dst, pbs[bi].tb[:, :cnt * 128].rearrange("p (a c) -> p a c", c=128), [pbs[bi].b], [tm.b])
            j += nn

    def phase_filter(self, l, s):
        P, I, S = self.P, self.I, self.S
        n, L, BS, NB, NT = s.name, s.L, s.BS, s.NB, s.NT
        N2 = 2 * L
        Lh = L // 2
        FB = Lh + 128
        NTh = Lh // 128
        fblocks = [(b0, min(512, FB - b0)) for b0 in range(0, FB, 512)]
        kf = S["kf_" + n]
        with P.phase():
            w3 = P.sb([64, 1024], F32, "fw3")
            h2 = P.sb([64, L], F32, "h2")
            P.dma(w3.t[:], I["fw3"][l], writes=[w3.b])
            with P.phase():
                ft = P.sb([33, L], F32, "feats")
                w1 = P.sb([33, 64], F32, "fw1")
                w2 = P.sb([64, 64], F32, "fw2")
                h1 = P.sb([64, L], F32, "h1")
                P.dma(ft.t[:], I["feats_" + n], writes=[ft.b])
                P.dma(w1.t[:], I["fw1"][l], writes=[w1.b])
                P.dma(w2.t[:], I["fw2"][l], writes=[w2.b])
                ar = P.ring(2, [64, BS], F32, "farg")
                mr = P.ring(2, [64, BS], F32, "fmask")
                for (src, wgt, kdim, dst, bname) in ((ft, w1, 33, h1, "fb1"), (h1, w2, 64, h2, "fb2")):
                    for blk in range(NB):
                        ps = self.psf()
                        P.mm(ps.t[:64, :BS], wgt.t[:kdim, :], src.t[:kdim, blk * BS:(blk + 1) * BS], True, True,
                             [wgt.b, src.b], [ps.b])
                        a = ar.next()
                        m = mr.next()
                        P.ts("dve", a.t[:], ps.t[:64, :BS], self.vecs.t[:64, VOFF["freq"] + l:VOFF["freq"] + l + 1],
                             self.vecs.t[:64, VOFF[bname] + l:VOFF[bname] + l + 1], ALU.mult, ALU.add,
                             [ps.b, self.vecs.b], [a.b])
                        for _ in range(2):
                            P.add("dve", lambda e, a=a, m=m: e.tensor_single_scalar(m.t[:], a.t[:], float(np.pi), ALU.is_gt),
                                  [a.b], [m.b])
                            P.stt("dve", a.t[:], m.t[:], float(-2 * np.pi), a.t[:], ALU.mult, ALU.add, [a.b, m.b], [a.b])
                            P.add("dve", lambda e, a=a, m=m: e.tensor_single_scalar(m.t[:], a.t[:], float(-np.pi), ALU.is_lt),
                                  [a.b], [m.b])
                            P.stt("dve", a.t[:], m.t[:], float(2 * np.pi), a.t[:], ALU.mult, ALU.add, [a.b, m.b], [a.b])
                        P.act(dst.t[:, blk * BS:(blk + 1) * BS], a.t[:], AF.Sin, [a.b], [dst.b])
            hT = P.sb([128, 2, L], F32, "hT")
            sd = P.sb([128, 2, L], BF16, "hsd")
            dec = P.sb([128, L], F32, "dec")
            tms = [P.sb([128, NTh, 256], BF16, "hftm") for _ in range(4)]
            sums = P.sb([128, 8], F32, "fsums")
            mring = P.ring(8, [128, 2, 512], BF16, "mring")
            kst = P.ring(12, [128, 512], F32, "kst")
            mec, mes, moc, mos = (I[k + n] for k in ("mec_", "mes_", "moc_", "mos_"))
            wsc = 2.0 / N2
            for o in range(2):
                for cc in range(2):
                    P.dma(dec.t[:], I["dec_" + n][cc * 128:(cc + 1) * 128, :], writes=[dec.b])
                    for d in range(2):
                        j = o * 4 + d * 2 + cc
                        for blk in range(NB):
                            ps = self.psf()
                            P.mm(ps.t[:, :BS], w3.t[:, j * 128:(j + 1) * 128], h2.t[:, blk * BS:(blk + 1) * BS], True, True,
                                 [w3.b, h2.b], [ps.b])
                            P.stt("dve", hT.t[:, d, blk * BS:(blk + 1) * BS], ps.t[:, :BS],
                                  self.vecs.t[:, VOFF["fb3"] + l * 8 + j:VOFF["fb3"] + l * 8 + j + 1],
                                  dec.t[:, blk * BS:(blk + 1) * BS], ALU.add, ALU.mult,
                                  [ps.b, dec.b, self.vecs.b], [hT.b])
                        P.add("dve", lambda e, d=d: e.tensor_reduce(sums.t[:, d:d + 1], hT.t[:, d, :], AX.X, ALU.add,
                                                                    apply_absolute_value=True), [hT.b], [sums.b])
                    P.tt("dve", sums.t[:, 2:3], sums.t[:, 0:1], sums.t[:, 1:2], ALU.add, [sums.b], [sums.b])
                    P.ts1("dve", sums.t[:, 2:3], sums.t[:, 2:3], EPS, ALU.add, [sums.b], [sums.b])
                    P.add("dve", lambda e: e.reciprocal(sums.t[:, 2:3], sums.t[:, 2:3]), [sums.b], [sums.b])
                    P.tt("dve", sums.t[:, 4 + cc:5 + cc], hT.t[:, 1, 0:1], sums.t[:, 2:3], ALU.mult, [hT.b, sums.b], [sums.b])
                    for d in range(2):
                        P.ts1("dve", hT.t[:, d, :], hT.t[:, d, :], sums.t[:, 2:3], ALU.mult, [hT.b, sums.b], [hT.b])
                    P.tt("pool", sd.t[:, 0, :], hT.t[:, 0, :], hT.t[:, 1, :], ALU.add, [hT.b], [sd.b])
                    P.tt("pool", sd.t[:, 1, :], hT.t[:, 0, :], hT.t[:, 1, :], ALU.subtract, [hT.b], [sd.b])
                    for q in range(2):
                        for e_ in range(2):
                            self.fm_to_tm(sd, lambda jj, q=q, e_=e_: sd.t[:, q, 256 * jj + e_:256 * (jj + 1):2], NTh,
                                          tms[2 * q + e_], 0, cc * 128)
                for cc in range(2):
                    outs = [(tms[0], cc * 128, mec), (tms[1], cc * 128, moc), (tms[2], cc * 128, mes), (tms[3], cc * 128, mos)]

                    def handler(bi, b0, bs, pss, o=o, cc=cc):
                        ec, oc, es, os_ = pss
                        hb0 = sums.t[:, 4 + cc:5 + cc]
                        toc, tos = kst.next(), kst.next()
                        P.cp("act", toc.t[:, :bs], oc.t[:, :bs], [oc.b], [toc.b])
                        P.cp("act", tos.t[:, :bs], os_.t[:, :bs], [os_.b], [tos.b])
                        ks = [kst.next() for _ in range(4)]
                        P.stt("dve", ks[0].t[:, :bs], ec.t[:, :bs], hb0, toc.t[:, :bs], ALU.subtract, ALU.add,
                              [ec.b, sums.b, toc.b], [ks[0].b])
                        P.tt("dve", ks[1].t[:, :bs], es.t[:, :bs], tos.t[:, :bs], ALU.add, [es.b, tos.b], [ks[1].b])
                        P.stt("dve", ks[2].t[:, :bs], ec.t[:, :bs], hb0, toc.t[:, :bs], ALU.subtract, ALU.subtract,
                              [ec.b, sums.b, toc.b], [ks[2].b])
                        P.tt("dve", ks[3].t[:, :bs], tos.t[:, :bs], es.t[:, :bs], ALU.subtract, [es.b, tos.b], [ks[3].b])
                        for q in range(4):
                            P.ts1("pool", ks[q].t[:, :bs], ks[q].t[:, :bs], wsc, ALU.mult, [ks[q].b], [ks[q].b])
                            if b0 == 0:
                                P.ts1("pool", ks[q].t[:, 0:1], ks[q].t[:, 0:1], 0.5, ALU.mult, [ks[q].b], [ks[q].b])
                            P.dma(kf[o, cc, q, :, b0:b0 + bs], ks[q].t[:, :bs], reads=[ks[q].b], q="pool")

                    self.seq_transform2(outs, handler, mring, NTh, fblocks)

    def phase_pool(self, l, s):
        P, I, S = self.P, self.I, self.S
        n, L, BS, NB = s.name, s.L, s.BS, s.NB
        LP = L + 32
        if True:
            u = P.sb([128, 2, LP], F32, "pu")
            A = P.sb([128, 2, LP], F32, "pA")
            B = P.sb([128, 2, LP], F32, "pB")
            r = P.sb([128, 2, L], BF16, "pr")
            corr = P.sb([128, 2, 16], F32, "pcorr")
            pwf = P.sb([128, 2, 128], F32, "pwf")
            pwb = P.sb([128, 2, 128], BF16, "pwb")
            tmp8 = P.sb([128, 2, 8], F32, "ptmp")
            ost = P.ring(3, [128, BS], BF16, "post")
            P.add("pool", lambda e: e.memset(u.t[:], 0.0), [], [u.b])
            P.add("pool", lambda e: e.memset(pwf.t[:], 0.0), [], [pwf.b])
            P.dma(u.t[:, :, 16:16 + L], S["pool_" + n].rearrange("(c p) t -> p c t", p=128), reads=[], writes=[u.b])
            P.dma(corr.t[:], I["pcorr_" + n], writes=[corr.b])
            for g in range(4):
                hp = (g % 2) * 64
                P.dma(pwf.t[hp:hp + 64, g // 2, hp:hp + 64], I["pool_w"][l, g], writes=[pwf.b])
            P.cp("dve", pwb.t[:], pwf.t[:], [pwf.b], [pwb.b])
            for g in range(4):
                hp, c = (g % 2) * 64, g // 2
                sl = slice(hp, hp + 64)
                eng = "dve" if g % 2 == 0 else "pool"
                P.tt(eng, A.t[sl, c, 1:LP], u.t[sl, c, 0:LP - 1], u.t[sl, c, 1:LP], ALU.add, [u.b], [A.b])
                cur = A
                if g >= 1:
                    P.tt(eng, B.t[sl, c, 2:LP - 1], A.t[sl, c, 1:LP - 2], A.t[sl, c, 3:LP], ALU.add, [A.b], [B.b])
                    cur = B
                if g >= 2:
                    P.tt(eng, A.t[sl, c, 4:LP - 3], B.t[sl, c, 2:LP - 5], B.t[sl, c, 6:LP - 1], ALU.add, [B.b], [A.b])
                    cur = A
                if g >= 3:
                    P.tt(eng, B.t[sl, c, 8:LP - 7], A.t[sl, c, 4:LP - 11], A.t[sl, c, 12:LP - 3], ALU.add, [A.b], [B.b])
                    cur = B
                wdt = (2, 4, 8, 16)[g]
                P.stt(eng, r.t[sl, c, :], cur.t[sl, c, 16:16 + L], 1.0 / wdt, u.t[sl, c, 16:16 + L], ALU.mult, ALU.subtract,
                      [cur.b, u.b], [r.b])
                for (c0, k0) in ((0, 0), (L - 8, 8)):
                    P.tt(eng, tmp8.t[sl, c, :], cur.t[sl, c, 16 + c0:16 + c0 + 8], corr.t[sl, c, k0:k0 + 8], ALU.mult,
                         [cur.b, corr.b], [tmp8.b])
                    P.tt(eng, r.t[sl, c, c0:c0 + 8], tmp8.t[sl, c, :], u.t[sl, c, 16 + c0:16 + c0 + 8], ALU.subtract,
                         [tmp8.b, u.b, r.b], [r.b])
            for c in range(2):
                for blk in range(NB):
                    ps = self.psf()
                    P.mm(ps.t[:, :BS], pwb.t[:, c, :], r.t[:, c, blk * BS:(blk + 1) * BS], True, True, [pwb.b, r.b], [ps.b])
                    o = ost.next()
                    P.act(o.t[:], ps.t[:, :BS], AF.Copy, [ps.b, self.vecs.b], [o.b], scale=self.vcol("pscale", l, c))
                    P.dma(S["ybr_" + n][0, c * 128:(c + 1) * 128, blk * BS:(blk + 1) * BS], o.t[:], reads=[o.b], q="pool")

    def phase_fnet(self, l, s):
        P, I, S = self.P, self.I, self.S
        n, L, BS, NB, NT = s.name, s.L, s.BS, s.NB, s.NT
        with P.phase():
            utm = P.sb([128, NT, 256], BF16, "futm")
            cd = P.sb([128, 256], BF16, "chdft")
            P.dma(utm.t[:], S["fnet_" + n].rearrange("(g p) c -> p g c", p=128), writes=[utm.b])
            P.dma(cd.t[:], I["chdft_" + n], writes=[cd.b])
            mring = P.ring(4, [128, 4, BS], BF16, "mring")
            ab = P.ring(8, [128, BS], BF16, "fab")
            ost = P.ring(3, [128, BS], BF16, "fost")
            fc, fs = I["fc_" + n], I["fs_" + n]
            outs = [(utm, 0, fc), (utm, 0, fs), (utm, 128, fc), (utm, 128, fs)]

            def handler(nb, pss):
                tiles = []
                for c in range(2):
                    a_c, a_s = ab.next(), ab.next()
                    P.cp("act", a_c.t[:], pss[2 * c].t[:, :BS], [pss[2 * c].b], [a_c.b])
                    P.cp("dve", a_s.t[:], pss[2 * c + 1].t[:, :BS], [pss[2 * c + 1].b], [a_s.b])
                    tiles.append((a_c, a_s))
                ps2 = [self.psf(), self.psf()]
                self.mm_jobs([(ps2[c].t[:, :BS], ps2[c].b,
                               [(cd.t[:, 0:128], tiles[c][0].t[:], [cd.b, tiles[c][0].b]),
                                (cd.t[:, 128:256], tiles[c][1].t[:], [cd.b, tiles[c][1].b])]) for c in range(2)])
                for c in range(2):
                    o = ost.next()
                    P.cp(self.evac_eng(), o.t[:], ps2[c].t[:, :BS], [ps2[c].b], [o.b])
                    P.dma(S["ybr_" + n][1, c * 128:(c + 1) * 128, nb * BS:(nb + 1) * BS], o.t[:], reads=[o.b], q="pool")

            self.seq_transform(s, outs, handler, mring)
            self.phase_pool(l, s)

    def phase_hyena(self, l, s):
        P, I, S = self.P, self.I, self.S
        n, L, BS, NB, NT = s.name, s.L, s.BS, s.NB, s.NT
        with P.phase():
            v = P.sb([128, 2, L], F32, "hv")
            xg = P.sb([128, 2, L], BF16, "hxg")
            x2st = P.ring(2, [128, L], BF16, "hx2st")
            with P.phase():
                ur = P.ring(2, [128, L + 2], F32, "hur")
                tr_ = P.ring(2, [128, L], F32, "htr")
                uv = S["hy_" + n]
                for c in range(6):
                    ut = ur.next()
                    P.add("pool", lambda e, ut=ut: e.memset(ut.t[:, 0:1], 0.0), [], [ut.b])
                    P.add("pool", lambda e, ut=ut: e.memset(ut.t[:, L + 1:L + 2], 0.0), [], [ut.b])
                    P.dma(ut.t[:, 1:L + 1], uv[c * 128:(c + 1) * 128, :], writes=[ut.b])
                    t1 = tr_.next()
                    eng = "dve" if c % 2 == 0 else "pool"
                    P.ts(eng, t1.t[:], ut.t[:, 0:L], self.vcol("hcw", l, 0 * 6 + c), self.vcol("hcb", l, c), ALU.mult, ALU.add,
                         [ut.b, self.vecs.b], [t1.b])
                    P.stt(eng, t1.t[:], ut.t[:, 1:L + 1], self.vcol("hcw", l, 1 * 6 + c), t1.t[:], ALU.mult, ALU.add,
                          [ut.b, t1.b, self.vecs.b], [t1.b])
                    if c < 2:
                        dst, dstb, x2t = v.t[:, c, :], v.b, None
                    elif c < 4:
                        dst, dstb, x2t = xg.t[:, c - 2, :], xg.b, None
                    else:
                        x2t = x2st.next()
                        dst, dstb = x2t.t[:], x2t.b
                    P.stt(eng, dst, ut.t[:, 2:L + 2], self.vcol("hcw", l, 2 * 6 + c), t1.t[:], ALU.mult, ALU.add,
                          [ut.b, t1.b, self.vecs.b], [dstb])
                    if x2t is not None:
                        P.dma(S["x2_" + n][(c - 4) * 128:(c - 3) * 128, :], x2t.t[:], reads=[x2t.b], q="pool")
            Lh = L // 2
            FB = Lh + 128
            NTh = Lh // 128
            NFT = FB // 128
            fblocks = [(b0, min(512, FB - b0)) for b0 in range(0, FB, 512)]
            tblocks = [(b0, min(512, Lh - b0)) for b0 in range(0, Lh, 512)]
            mec, mes, moc, mos = (I[k + n] for k in ("mec_", "mes_", "moc_", "mos_"))
            iec, ies, ioc, ios = (I[k + n] for k in ("iec_", "ies_", "ioc_", "ios_"))
            kfv = S["kf_" + n]
            for o in range(2):
                with P.phase():
                    zt = [P.sb([128, NTh, 256], BF16, "zteo") for _ in range(2)]
                    with P.phase():
                        zbf = P.sb([128, 2, L], BF16, "zbf")
                        for c in range(2):
                            P.cp("act" if c == 0 else "dve", zbf.t[:, c, :], v.t[:, c, :], [v.b], [zbf.b])
                            for e_ in range(2):
                                self.fm_to_tm(zbf, lambda jj, c=c, e_=e_: zbf.t[:, c, 256 * jj + e_:256 * (jj + 1):2], NTh,
                                              zt[e_], 0, c * 128)
                        if o == 1:
                            P.dma(xg.t[:], S["x2_" + n].rearrange("(c p) t -> p c t", p=128), writes=[xg.b])
                    pq = [P.sb([128, NFT, 256], BF16, "pqtm") for _ in range(4)]
                    mring = P.ring(8, [128, 2, 512], BF16, "mring")
                    kr_ = P.ring(2, [128, 4, 512], F32, "kfr")
                    t4 = P.ring(12, [128, 512], F32, "ht4")
                    zb = P.ring(8, [128, 512], BF16, "hzb")
                    yb = P.ring(12, [128, 512], BF16, "hyb")
                    ost = P.ring(3, [128, 1024], BF16, "host")
                    for c in range(2):
                        outs = [(zt[0], c * 128, mec), (zt[0], c * 128, mes), (zt[1], c * 128, moc), (zt[1], c * 128, mos)]

                        def handler(bi, b0, bs, pss, o=o, c=c):
                            er, ei, or_, oi = pss
                            kt = kr_.next()
                            P.dma(kt.t[:, :, :bs], kfv[o, c].rearrange("q p t -> p q t")[:, :, b0:b0 + bs], writes=[kt.b])
                            tor, toi = t4.next(), t4.next()
                            P.cp("act", tor.t[:, :bs], or_.t[:, :bs], [or_.b], [tor.b])
                            P.cp("act", toi.t[:, :bs], oi.t[:, :bs], [oi.b], [toi.b])
                            zrl, zil, zrh, zih = zb.next(), zb.next(), zb.next(), zb.next()
                            P.tt("dve", zrl.t[:, :bs], er.t[:, :bs], tor.t[:, :bs], ALU.add, [er.b, tor.b], [zrl.b])
                            P.tt("dve", zil.t[:, :bs], ei.t[:, :bs], toi.t[:, :bs], ALU.add, [ei.b, toi.b], [zil.b])
                            P.tt("dve", zrh.t[:, :bs], er.t[:, :bs], tor.t[:, :bs], ALU.subtract, [er.b, tor.b], [zrh.b])
                            P.tt("dve", zih.t[:, :bs], toi.t[:, :bs], ei.t[:, :bs], ALU.subtract, [ei.b, toi.b], [zih.b])
                            ys = []
                            for hi_, (zr, zi) in enumerate(((zrl, zil), (zrh, zih))):
                                krr, kii = kt.t[:, 2 * hi_, :bs], kt.t[:, 2 * hi_ + 1, :bs]
                                a1, a2, a3, a4 = t4.next(), t4.next(), t4.next(), t4.next()
                                P.tt("dve", a1.t[:, :bs], zr.t[:, :bs], krr, ALU.mult, [zr.b, kt.b], [a1.b])
                                P.tt("pool", a2.t[:, :bs], zi.t[:, :bs], kii, ALU.mult, [zi.b, kt.b], [a2.b])
                                P.tt("dve", a3.t[:, :bs], zr.t[:, :bs], kii, ALU.mult, [zr.b, kt.b], [a3.b])
                                P.tt("pool", a4.t[:, :bs], zi.t[:, :bs], krr, ALU.mult, [zi.b, kt.b], [a4.b])
                                P.tt("dve", a1.t[:, :bs], a1.t[:, :bs], a2.t[:, :bs], ALU.subtract, [a1.b, a2.b], [a1.b])
                                P.tt("pool", a3.t[:, :bs], a3.t[:, :bs], a4.t[:, :bs], ALU.add, [a3.b, a4.b], [a3.b])
                                ys.append((a1, a3))
                            (yrl, yil), (yrh, yih) = ys
                            outs_ = [yb.next() for _ in range(4)]
                            P.tt("dve", outs_[0].t[:, :bs], yrl.t[:, :bs], yrh.t[:, :bs], ALU.add, [yrl.b, yrh.b], [outs_[0].b])
                            P.tt("pool", outs_[1].t[:, :bs], yil.t[:, :bs], yih.t[:, :bs], ALU.subtract, [yil.b, yih.b], [outs_[1].b])
                            P.tt("dve", outs_[2].t[:, :bs], yrl.t[:, :bs], yrh.t[:, :bs], ALU.subtract, [yrl.b, yrh.b], [outs_[2].b])
                            P.tt("pool", outs_[3].t[:, :bs], yil.t[:, :bs], yih.t[:, :bs], ALU.add, [yil.b, yih.b], [outs_[3].b])
                            nj = bs // 128

                            def part2(outs_=outs_, nj=nj, b0=b0, c=c):
                                for q in range(4):
                                    self.fm_to_tm(outs_[q], lambda jj, t=outs_[q]: t.t[:, jj * 128:(jj + 1) * 128], nj, pq[q],
                                                  b0 // 128, c * 128)
                            return part2

                        self.seq_transform2(outs, handler, mring, NTh, fblocks)
                    for c in range(2):
                        outs2 = [(pq[0], c * 128, iec), (pq[1], c * 128, ies), (pq[2], c * 128, ioc), (pq[3], c * 128, ios)]

                        def handler2(bi, b0, bs, pss, o=o, c=c):
                            ot = ost.next() if o == 1 else None
                            for e_ in range(2):
                                pa, pb_ = pss[2 * e_], pss[2 * e_ + 1]
                                tb_ = t4.next()
                                P.cp("act", tb_.t[:, :bs], pb_.t[:, :bs], [pb_.b], [tb_.b])
                                tm = t4.next()
                                vsl = v.t[:, c, 2 * b0 + e_:2 * (b0 + bs):2]
                                P.stt("dve", tm.t[:, :bs], vsl, self.vcol("hskip", l, o * 2 + c), pa.t[:, :bs],
                                      ALU.mult, ALU.add, [v.b, pa.b, self.vecs.b], [tm.b])
                                P.tt("pool", tm.t[:, :bs], tm.t[:, :bs], tb_.t[:, :bs], ALU.add, [tm.b, tb_.b], [tm.b])
                                gsl = xg.t[:, c, 2 * b0 + e_:2 * (b0 + bs):2]
                                if o == 0:
                                    P.tt("dve", vsl, tm.t[:, :bs], gsl, ALU.mult, [tm.b, xg.b, v.b], [v.b])
                                else:
                                    P.tt("dve", ot.t[:, e_:2 * bs:2], tm.t[:, :bs], gsl, ALU.mult, [tm.b, xg.b, ot.b], [ot.b])
                            if o == 1:
                                P.dma(S["ybr_" + n][2, c * 128:(c + 1) * 128, 2 * b0:2 * (b0 + bs)], ot.t[:, :2 * bs],
                                      reads=[ot.b], q="pool")

                        self.seq_transform2(outs2, handler2, mring, NFT, tblocks)

    def phase_attn(self, l, s):
        P, I, S = self.P, self.I, self.S
        n, L = s.name, s.L
        grid = s.which == 0
        with P.phase():
            qT = P.sb([128, 2, L], BF16, "qT")
            kcT = P.sb([128, 2, LC], BF16, "kcT")
            vc = P.sb([128, 2, 256], BF16, "vc")
            yT = P.sb([128, 2, L], BF16, "yT")
            P.dma(qT.t[:], S["q_" + n].rearrange("(c p) t -> p c t", p=128), writes=[qT.b])
            P.dma(kcT.t[:], S["k_c"].rearrange("(c p) t -> p c t", p=128), writes=[kcT.b])
            P.dma(vc.t[:], S["v_c"][0:LC, :].rearrange("(g p) c -> p g c", p=128), writes=[vc.b])
            if grid:
                NT = s.NT
                kT = P.sb([128, 2, L], BF16, "kT")
                ve = P.sb([128, NT, 256], BF16, "ve")
                bias = P.sb([128, 4, 5, 576], F32, "bias")
                P.dma(kT.t[:], S["k_" + n].rearrange("(c p) t -> p c t", p=128), writes=[kT.b])
                P.dma(ve.t[:], S["v_" + n][0:L, :].rearrange("(g p) c -> p g c", p=128), writes=[ve.b])
                P.dma(bias.t[:], I["bias"][l], writes=[bias.b])
            sr = P.ring(6, [128, 832], F32, "as")
            pr = P.ring(9, [128, 832], BF16, "ap")
            ptr_ = P.ring(8, [128, 7, 128], BF16, "apT")
            st = P.ring(12, [128, 4], F32, "astat")
            orow = P.ring(3, [128, 256], BF16, "aorow")
            npair = L // 128

            def stage_a(pi):
                r = 2 * pi
                if grid:
                    r0a = min(max(r - 4, 0), 56)
                    r0b = min(max(r - 3, 0), 56)
                    nine = r0b != r0a
                    pat = {0: 1, 2: 2, 60: 3, 62: 4}.get(r, 0)
                    nnb = 576 if nine else 512
                else:
                    r0a, nine, nnb = 0, False, 0
                nk = nnb + 256
                heads = []
                for h in range(4):
                    hp, hc = (h % 2) * 64, h // 2
                    hs = slice(hp, hp + 64)
                    sb_ = sr.next()
                    stt_ = st.next()
                    qa = qT.t[hs, hc, r * 64:r * 64 + 128]
                    p2 = self.psf()
                    if grid:
                        p1 = self.psf()
                        P.mm(p1.t[:, :512], qa, kT.t[hs, hc, r0a * 64:r0a * 64 + 512], True, True, [qT.b, kT.b], [p1.b])
                        P.mm(p2.t[:, 64:320], qa, kcT.t[hs, hc, :], True, True, [qT.b, kcT.b], [p2.b])
                        if nine:
                            P.mm(p2.t[:, 0:64], qa, kT.t[hs, hc, r0a * 64 + 512:r0a * 64 + 576], True, True, [qT.b, kT.b], [p2.b])
                        P.tt("dve", sb_.t[:, 0:512], p1.t[:, :512], bias.t[:, h, pat, 0:512], ALU.add, [p1.b, bias.b], [sb_.b])
                        if nine:
                            P.tt("dve", sb_.t[:, 512:576], p2.t[:, 0:64], bias.t[:, h, pat, 512:576], ALU.add, [p2.b, bias.b], [sb_.b])
                    else:
                        P.mm(p2.t[:, 64:320], qa, kcT.t[hs, hc, :], True, True, [qT.b, kcT.b], [p2.b])
                    P.cp("act", sb_.t[:, nnb:nnb + 256], p2.t[:, 64:320], [p2.b], [sb_.b])
                    P.add("dve", lambda e, sb_=sb_, stt_=stt_, nk=nk: e.reduce_max(stt_.t[:, 0:1], sb_.t[:, :nk], AX.X),
                          [sb_.b], [stt_.b])
                    P.ts1("dve", stt_.t[:, 1:2], stt_.t[:, 0:1], -1.0, ALU.mult, [stt_.b], [stt_.b])
                    P.add("pool", lambda e, stt_=stt_: e.memset(stt_.t[:, 2:3], 0.0), [stt_.b], [stt_.b])
                    pb_ = pr.next()
                    P.act(pb_.t[:, :nk], sb_.t[:, :nk], AF.Exp, [sb_.b, stt_.b], [pb_.b, stt_.b],
                          bias=stt_.t[:, 1:2], scale=1.0, accum_out=stt_.t[:, 2:3])
                    P.add("dve", lambda e, stt_=stt_: e.reciprocal(stt_.t[:, 3:4], stt_.t[:, 2:3]), [stt_.b], [stt_.b])
                    heads.append((pb_, stt_))
                return (r, r0a, nine, nnb, heads)

            def stage_b(state):
                r, r0a, nine, nnb, heads = state
                chunks = []
                if grid:
                    for j in range(4):
                        chunks.append((j * 128, 128, "nb", j))
                    if nine:
                        chunks.append((512, 64, "nb", 4))
                chunks.append((nnb, 128, "cx", 0))
                chunks.append((nnb + 128, 128, "cx", 1))
                nj = len(chunks)
                ot = orow.next()
                pbks = [self.psb() for _ in range(4)]
                for j, (c0, ncol, kind, idx) in enumerate(chunks):
                    for h in range(4):
                        pb_ = heads[h][0]
                        P.tr(pbks[h].tb[:ncol, j * 128:(j + 1) * 128], pb_.t[:, c0:c0 + ncol], self.ident.t[:],
                             [pb_.b, self.ident.b], [pbks[h].b])
                pTs = []
                for h in range(4):
                    pT = ptr_.next()
                    eng = "dve" if h % 2 == 0 else "act"
                    if nine:
                        P.cp(eng, pT.t[:, 0:4, :], pbks[h].tb[:, 0:512].rearrange("p (a c) -> p a c", c=128), [pbks[h].b], [pT.b])
                        P.cp(eng, pT.t[:64, 4, :], pbks[h].tb[:64, 512:640], [pbks[h].b], [pT.b])
                        P.cp(eng, pT.t[:, 5:7, :], pbks[h].tb[:, 640:896].rearrange("p (a c) -> p a c", c=128), [pbks[h].b], [pT.b])
                    else:
                        P.cp(eng, pT.t[:, :nj, :], pbks[h].tb[:, :nj * 128].rearrange("p (a c) -> p a c", c=128),
                             [pbks[h].b], [pT.b])
                    pTs.append(pT)
                pos = [self.psf() for _ in range(4)]
                jobs = []
                for h in range(4):
                    steps = []
                    for j, (c0, ncol, kind, idx) in enumerate(chunks):
                        if kind == "nb":
                            vt, vb = ve.t[:ncol, r0a // 2 + idx, h * 64:(h + 1) * 64], ve.b
                        else:
                            vt, vb = vc.t[:, idx, h * 64:(h + 1) * 64], vc.b
                        steps.append((pTs[h].t[:ncol, j, :], vt, [pTs[h].b, vb]))
                    jobs.append((pos[h].t[:, :64], pos[h].b, steps))
                self.mm_jobs(jobs)
                for h in range(4):
                    stt_ = heads[h][1]
                    P.act(ot.t[:, h * 64:(h + 1) * 64], pos[h].t[:, :64], AF.Copy, [pos[h].b, stt_.b], [ot.b], scale=stt_.t[:, 3:4])
                pbk2 = [self.psb(), self.psb()]
                for c in range(2):
                    P.tr(pbk2[c].tb[:, 0:128], ot.t[:, c * 128:(c + 1) * 128], self.ident.t[:],
                         [ot.b, self.ident.b], [pbk2[c].b])
                    P.cp("dve" if c == 0 else "act", yT.t[:, c, r * 64:r * 64 + 128], pbk2[c].tb[:, :128], [pbk2[c].b], [yT.b])

            prev = stage_a(0)
            for pi in range(1, npair):
                cur = stage_a(pi)
                stage_b(prev)
                prev = cur
            stage_b(prev)
            P.dma(S["ybr_" + n][3].rearrange("(c p) t -> p c t", p=128), yT.t[:], reads=[yT.b], q="pool")

    def phase_merge(self, l, s, xsrc, xdst):
        P, I, S = self.P, self.I, self.S
        n, L, BS, NB, w = s.name, s.L, s.BS, s.NB, s.which
        with P.phase():
            wbr = P.sb([128, 8, 1024], BF16, "wbr")
            wo = P.sb([128, 8, 1024], BF16, "wo")
            wst = P.ring(2, [128, 2, 1024], F32, "mwst")
            wbv = I["w_branch"][l].rearrange("b (k p) n -> p (b k) n", p=128)
            wov = I["w_out"][l].rearrange("(k p) n -> p k n", p=128)
            for i in range(4):
                ws = wst.next()
                P.dma(ws.t[:], wbv[:, 2 * i:2 * i + 2, :], writes=[ws.b])
                P.cp(self.evac_eng(), wbr.t[:, 2 * i:2 * i + 2, :], ws.t[:], [ws.b], [wbr.b])
            for i in range(4):
                ws = wst.next()
                P.dma(ws.t[:], wov[:, 2 * i:2 * i + 2, :], writes=[ws.b])
                P.cp(self.evac_eng(), wo.t[:, 2 * i:2 * i + 2, :], ws.t[:], [ws.b], [wo.b])
            ybr = P.ring(2, [128, 8, BS], BF16, "mybr")
            gr = P.ring(3, [128, 4, BS], BF16, "mgate")
            tb = P.ring(8, [128, BS], F32, "mtb")
            mg = P.ring(2, [128, 8, BS], BF16, "mmg")
            xr = P.ring(5, [128, BS], F32, "mxr")
            xo = P.ring(5, [128, BS], F32, "mxo")
            yv = S["ybr_" + n].rearrange("b (k p) t -> p (b k) t", p=128)
            gv = S["gate_" + n].rearrange("(b c p) t -> p b c t", p=128, c=8)
            for blk in range(NB):
                sl = slice(blk * BS, (blk + 1) * BS)
                yt = ybr.next()
                P.dma(yt.t[:], yv[:, :, sl], writes=[yt.b])
                m = mg.next()
                for dc in range(8):
                    g = gr.next()
                    P.dma(g.t[:], gv[:, :, dc, sl], writes=[g.b])
                    ts_ = []
                    pss = [self.psf() for _ in range(4)]
                    self.mm_jobs([(pss[b].t[:, :BS], pss[b].b,
                                   [(wbr.t[:, 2 * b + kc, dc * 128:(dc + 1) * 128], yt.t[:, 2 * b + kc, :], [wbr.b, yt.b])
                                    for kc in range(2)]) for b in range(4)])
                    for b in range(4):
                        t = tb.next()
                        P.tt("dve", t.t[:], pss[b].t[:, :BS], g.t[:, b, :], ALU.mult, [pss[b].b, g.b], [t.b])
                        ts_.append(t)
                    P.tt("pool", ts_[0].t[:], ts_[0].t[:], ts_[1].t[:], ALU.add, [ts_[0].b, ts_[1].b], [ts_[0].b])
                    P.tt("pool", ts_[2].t[:], ts_[2].t[:], ts_[3].t[:], ALU.add, [ts_[2].b, ts_[3].b], [ts_[2].b])
                    P.tt("pool", m.t[:, dc, :], ts_[0].t[:], ts_[2].t[:], ALU.add, [ts_[0].b, ts_[2].b], [m.b])
                for d0 in (0, 4):
                    pss = [self.psf() for _ in range(4)]
                    self.mm_jobs([(pss[i].t[:, :BS], pss[i].b,
                                   [(wo.t[:, k, (d0 + i) * 128:(d0 + i + 1) * 128], m.t[:, k, :], [wo.b, m.b]) for k in range(8)])
                                  for i in range(4)])
                    for i in range(4):
                        dc = d0 + i
                        xt = xr.next()
                        P.dma(xt.t[:], xsrc[dc * 128:(dc + 1) * 128, sl], writes=[xt.b])
                        o = xo.next()
                        P.stt("dve", o.t[:], pss[i].t[:, :BS], self.mcol(2, dc, w), xt.t[:], ALU.mult, ALU.add,
                              [pss[i].b, xt.b, self.modv.b], [o.b])
                        P.dma(xdst[dc * 128:(dc + 1) * 128, sl], o.t[:], reads=[o.b], q="pool")

    def phase_ffn_up(self, l, s, xsrc):
        P, I, S = self.P, self.I, self.S
        n, L, BS, NB, w = s.name, s.L, s.BS, s.NB, s.which
        with P.phase():
            hx = P.sb([128, 8, L], BF16, "hx2")
            with P.phase():
                xr = P.ring(2, [128, 8, BS], F32, "xr")
                sqr = P.ring(2, [128, 8, BS], BF16, "sq")
                rsr = P.ring(2, [128, BS], F32, "rs")
                tmpr = P.ring(3, [128, BS], F32, "tmp")
                for blk in range(NB):
                    def out_fn(k, tmp, blk=blk):
                        P.act(hx.t[:, k, blk * BS:(blk + 1) * BS], tmp.t[:], AF.Identity, [tmp.b, self.modv.b], [hx.b],
                              bias=self.mcol(3, k, w), scale=1.0)
                    self.norm_block(s, xsrc, blk, lambda k: self.avec.t[:, 16 + 2 * k + w:16 + 2 * k + w + 1], None,
                                    xr, sqr, rsr, tmpr, out_fn)
            wst = P.ring(2, [128, 8, 256], F32, "fwst")
            wbf = P.ring(3, [128, 8, 256], BF16, "fwbf")
            ub = P.ring(2, [128, L], BF16, "fub")
            gb = P.ring(2, [128, L + 2], F32, "fgb")
            t1r = P.ring(1, [128, L], F32, "ft1")
            ar = P.ring(2, [128, L], BF16, "far")
            wv = I["ffn_w_up"][l].rearrange("(k p) n -> p k n", p=128)

            def load_w(j):
                ws = wst.next()
                P.dma(ws.t[:, :, 0:128], wv[:, :, j * 128:(j + 1) * 128], writes=[ws.b])
                P.dma(ws.t[:, :, 128:256], wv[:, :, FF + j * 128:FF + (j + 1) * 128], writes=[ws.b])
                wb_ = wbf.next()
                P.cp(self.evac_eng(), wb_.t[:], ws.t[:], [ws.b], [wb_.b])
                return wb_

            wq = [load_w(0), load_w(1)]
            for j in range(NFF):
                wb = wq[j]
                u = ub.next()
                g = gb.next()
                P.add("pool", lambda e, g=g: e.memset(g.t[:, 0:1], 0.0), [], [g.b])
                P.add("pool", lambda e, g=g: e.memset(g.t[:, L + 1:L + 2], 0.0), [], [g.b])
                for b0 in range(0, NB, 2):
                    blks = list(range(b0, min(b0 + 2, NB)))
                    pss = [(self.psf(), self.psf()) for _ in blks]
                    jobs = []
                    for i, blk in enumerate(blks):
                        sl = slice(blk * BS, (blk + 1) * BS)
                        jobs.append((pss[i][0].t[:, :BS], pss[i][0].b, [(wb.t[:, k, 0:128], hx.t[:, k, sl], [wb.b, hx.b]) for k in range(8)]))
                        jobs.append((pss[i][1].t[:, :BS], pss[i][1].b, [(wb.t[:, k, 128:256], hx.t[:, k, sl], [wb.b, hx.b]) for k in range(8)]))
                    self.mm_jobs(jobs)
                    for i, blk in enumerate(blks):
                        sl = slice(blk * BS, (blk + 1) * BS)
                        pu, pg = pss[i]
                        P.cp("dve", u.t[:, sl], pu.t[:, :BS], [pu.b], [u.b])
                        P.cp("act", g.t[:, 1 + blk * BS:1 + (blk + 1) * BS], pg.t[:, :BS], [pg.b], [g.b])
                if j + 2 < NFF:
                    wq.append(load_w(j + 2))
                t1 = t1r.next()
                eng = "pool"
                P.ts(eng, t1.t[:], g.t[:, 0:L], self.vcol("fcw", l, 0 * NFF + j), self.vcol("fcb", l, j), ALU.mult, ALU.add,
                     [g.b, self.vecs.b], [t1.b])
                P.stt(eng, t1.t[:], g.t[:, 1:L + 1], self.vcol("fcw", l, 1 * NFF + j), t1.t[:], ALU.mult, ALU.add,
                      [g.b, t1.b, self.vecs.b], [t1.b])
                P.stt("dve", t1.t[:], g.t[:, 2:L + 2], self.vcol("fcw", l, 2 * NFF + j), t1.t[:], ALU.mult, ALU.add,
                      [g.b, t1.b, self.vecs.b], [t1.b])
                P.act(t1.t[:], t1.t[:], AF.Silu, [t1.b], [t1.b])
                a = ar.next()
                P.tt("dve", a.t[:], t1.t[:], u.t[:], ALU.mult, [t1.b, u.b], [a.b])
                P.dma(S["aT_" + n][j * 128:(j + 1) * 128, :], a.t[:], reads=[a.b], q="pool")

    def phase_ffn_down(self, l, s, xsrc, xdst):
        P, I, S = self.P, self.I, self.S
        n, L, BS, NB, w = s.name, s.L, s.BS, s.NB, s.which
        with P.phase():
            wd = P.sb([128, NFF, 1024], BF16, "wd")
            wst = P.ring(2, [128, 2, 1024], F32, "dwst")
            wv = I["ffn_w_down"][l].rearrange("(k p) n -> p k n", p=128)
            for i in range(NFF // 2):
                ws = wst.next()
                P.dma(ws.t[:], wv[:, 2 * i:2 * i + 2, :], writes=[ws.b])
                P.cp(self.evac_eng(), wd.t[:, 2 * i:2 * i + 2, :], ws.t[:], [ws.b], [wd.b])
            ar = P.ring(2, [128, NFF, BS], BF16, "dar")
            xr = P.ring(5, [128, BS], F32, "dxr")
            xo = P.ring(5, [128, BS], F32, "dxo")
            av = S["aT_" + n].rearrange("(k p) t -> p k t", p=128)
            for blk in range(NB):
                sl = slice(blk * BS, (blk + 1) * BS)
                a = ar.next()
                P.dma(a.t[:], av[:, :, sl], writes=[a.b])
                for d0 in (0, 4):
                    pss = [self.psf() for _ in range(4)]
                    self.mm_jobs([(pss[i].t[:, :BS], pss[i].b,
                                   [(wd.t[:, k, (d0 + i) * 128:(d0 + i + 1) * 128], a.t[:, k, :], [wd.b, a.b]) for k in range(NFF)])
                                  for i in range(4)])
                    for i in range(4):
                        dc = d0 + i
                        xt = xr.next()
                        P.dma(xt.t[:], xsrc[dc * 128:(dc + 1) * 128, sl], writes=[xt.b])
                        o = xo.next()
                        P.stt("dve", o.t[:], pss[i].t[:, :BS], self.mcol(5, dc, w), xt.t[:], ALU.mult, ALU.add,
                              [pss[i].b, xt.b, self.modv.b], [o.b])
                        P.dma(xdst[dc * 128:(dc + 1) * 128, sl], o.t[:], reads=[o.b], q="pool")

    def phase_final(self, s, xsrc, outT):
        P = self.P
        BS, NB = s.BS, s.NB
        with P.phase():
            xr = P.ring(2, [128, 8, BS], F32, "xr")
            sqr = P.ring(2, [128, 8, BS], BF16, "sq")
            rsr = P.ring(2, [128, BS], F32, "rs")
            tmpr = P.ring(4, [128, BS], F32, "tmp")
            for blk in range(NB):
                def out_fn(k, tmp, blk=blk):
                    P.dma(outT[k * 128:(k + 1) * 128, blk * BS:(blk + 1) * BS], tmp.t[:], reads=[tmp.b], q="pool")
                self.norm_block(s, xsrc, blk, lambda k: self.vcol("fng", None, k), None, xr, sqr, rsr, tmpr, out_fn)


VSPEC = [("eps", 1, 1), ("n1g", 8, DEPTH), ("n2g", 8, DEPTH), ("bmod", 96, DEPTH), ("pscale", 2, DEPTH),
         ("hcw", 18, DEPTH), ("hcb", 6, DEPTH), ("freq", 1, DEPTH), ("fb1", 1, DEPTH), ("fb2", 1, DEPTH),
         ("fb3", 8, DEPTH), ("hskip", 4, DEPTH), ("fcw", 3 * NFF, DEPTH), ("fcb", NFF, DEPTH), ("fng", 8, 1)]
VOFF, VLEN = {}, {}
_o = 0
for _n, _ln, _rep in VSPEC:
    VOFF[_n] = _o
    VLEN[_n] = _ln
    _o += _ln * _rep
NVEC = _o


def fm(v):
    v = np.asarray(v, np.float32)
    return np.ascontiguousarray(v.reshape(-1, 128).T)


def pad64(v):
    o = np.zeros((128, 1), np.float32)
    o[:64, 0] = v
    return o


_CONST = {}


def consts():
    if _CONST:
        return _CONST
    C = {}
    C["ident"] = np.eye(128, dtype=np.float32).astype(NPBF)
    for name, L in (("x", LX), ("c", LC)):
        t = np.arange(L, dtype=np.int64)
        tk = (t[:, None] * t[None, :])
        ang = 2.0 * np.pi * (tk % L).astype(np.float64) / L
        C["fc_" + name] = np.cos(ang).astype(np.float32).astype(NPBF)
        C["fs_" + name] = np.sin(ang).astype(np.float32).astype(NPBF)
        N2 = 2 * L
        Lh, FB = L // 2, L // 2 + 128
        tau = np.arange(Lh, dtype=np.int64)[:, None]
        ff = np.arange(FB, dtype=np.int64)[None, :]
        valid = (ff <= Lh).astype(np.float64)
        for par, nm in ((0, "e"), (1, "o")):
            ang2 = 2.0 * np.pi * (((2 * tau + par) * ff) % N2).astype(np.float64) / N2
            mc = np.cos(ang2) * valid
            ms = -np.sin(ang2) * valid
            C["m%sc_%s" % (nm, name)] = mc.astype(np.float32).astype(NPBF)
            C["m%ss_%s" % (nm, name)] = ms.astype(np.float32).astype(NPBF)
            C["i%sc_%s" % (nm, name)] = np.ascontiguousarray(mc.T).astype(np.float32).astype(NPBF)
            C["i%ss_%s" % (nm, name)] = np.ascontiguousarray(ms.T).astype(np.float32).astype(NPBF)
        tt = np.linspace(0.0, 1.0, L, dtype=np.float32)[:, None]
        bands = np.linspace(1e-4, 15, 16, dtype=np.float32)[None, :]
        angf = (np.float32(2.0 * math.pi / L) * np.arange(L, dtype=np.float32)[:, None]) * bands
        feats = np.concatenate([tt, np.cos(angf), -np.sin(angf)], axis=-1).astype(np.float32)
        C["feats_" + name] = np.ascontiguousarray(feats.T)
        deltas = np.linspace(math.log(1e-2) / 1.5, math.log(1e-2) / 0.3, 256, dtype=np.float32)
        dec = np.exp(-tt * np.abs(deltas)[None, :]).astype(np.float32) + np.float32(0.05)
        C["dec_" + name] = np.ascontiguousarray(dec.T)
        corr = np.zeros((128, 2, 16), np.float32)
        for g, wd in enumerate((2, 4, 8, 16)):
            pos = np.concatenate([np.arange(8), np.arange(L - 8, L)])
            lo = np.clip(pos - wd // 2, 0, L)
            hi = np.clip(pos - wd // 2 + wd, 0, L)
            hp = (g % 2) * 64
            corr[hp:hp + 64, g // 2, :] = (1.0 / (hi - lo).astype(np.float32))[None, :]
        C["pcorr_" + name] = corr
    c = np.arange(64)
    angc = 2.0 * np.pi * ((c[:, None] * c[None, :]) % 64) / 64.0
    cc = np.zeros((128, 128))
    ss = np.zeros((128, 128))
    for g in range(2):
        cc[g * 64:(g + 1) * 64, g * 64:(g + 1) * 64] = np.cos(angc)
        ss[g * 64:(g + 1) * 64, g * 64:(g + 1) * 64] = -np.sin(angc)
    for name, L in (("x", LX), ("c", LC)):
        sc_ = 1.0 / math.sqrt(64.0 * L)
        C["chdft_" + name] = np.concatenate([cc * sc_, ss * sc_], axis=1).astype(np.float32).astype(NPBF)
    _CONST.update(C)
    return _CONST


def host_inputs(inp, b, seqs=("c", "x")):
    C = consts()
    m = {}
    m["xT"] = np.ascontiguousarray(inp["x"][b].T)
    m["cT"] = np.ascontiguousarray(inp["ctx"][b].T)
    cv = np.zeros((128, 16), np.float32)
    cf, ccf = fm(inp["c"][b]), fm(inp["c_ctx"])
    cv[:, 0::2] = cf
    cv[:, 1::2] = ccf
    m["cvec"] = cv
    for k in ("w_mod", "w_in", "w_branch", "w_out", "ffn_w_up", "ffn_w_down", "pool_w"):
        m[k] = np.ascontiguousarray(inp[k], dtype=np.float32)
    m["fw1"] = np.ascontiguousarray(inp["hyena_filt_w1"], dtype=np.float32)
    m["fw2"] = np.ascontiguousarray(inp["hyena_filt_w2"], dtype=np.float32)
    m["fw3"] = np.ascontiguousarray(inp["hyena_filt_w3"], dtype=np.float32)
    V = np.zeros((128, NVEC), np.float32)

    def put(name, l, arr):
        o = VOFF[name] + (0 if l is None else l * VLEN[name])
        V[:, o:o + arr.shape[1]] = arr

    put("eps", None, np.full((128, 1), EPS, np.float32))
    for l in range(DEPTH):
        put("n1g", l, fm(inp["norm1_g"][l]))
        put("n2g", l, fm(inp["norm2_g"][l]))
        put("bmod", l, np.repeat(fm(inp["b_mod"][l]), 2, axis=1))
        put("pscale", l, fm(inp["pool_scale"][l]))
        put("hcw", l, np.concatenate([fm(inp["hyena_conv_w"][l][j]) for j in range(3)], axis=1))
        put("hcb", l, fm(inp["hyena_conv_b"][l]))
        put("freq", l, pad64(inp["hyena_freq"][l]))
        put("fb1", l, pad64(inp["hyena_filt_b1"][l]))
        put("fb2", l, pad64(inp["hyena_filt_b2"][l]))
        put("fb3", l, fm(inp["hyena_filt_b3"][l]))
        put("hskip", l, np.concatenate([fm(inp["hyena_skip"][l][o]) for o in range(2)], axis=1))
        put("fcw", l, np.concatenate([fm(inp["ffn_conv_w"][l][j]) for j in range(3)], axis=1))
        put("fcb", l, fm(inp["ffn_conv_b"][l]))
    put("fng", None, fm(inp["final_norm_g"]))
    m["vecs"] = V
    rpb = np.asarray(inp["na_rpb"], np.float32)
    col = np.arange(64)
    c0 = np.clip(col - 8, 0, 48)
    col_ok = (col[None, :] >= c0[:, None]) & (col[None, :] < c0[:, None] + 16)
    dc = np.clip(col[None, :] - col[:, None], -15, 15) + 15
    bias = np.full((DEPTH, 128, 4, 5, 9, 64), np.float32(-1e30), np.float32)
    for pat, r in enumerate((4, 0, 2, 60, 62)):
        U = min(max(r - 4, 0), 56)
        for half, rr in enumerate((r, r + 1)):
            r0 = min(max(rr - 4, 0), 56)
            for i in range(8):
                ui = r0 + i - U
                dr = r0 + i - rr + 7
                g = rpb[:, :, dr][:, :, dc]
                g = np.where(col_ok[None, None], g, np.float32(-1e30))
                bias[:, half * 64:(half + 1) * 64, :, pat, ui, :] = np.transpose(g, (0, 2, 1, 3))
    m["bias"] = bias.reshape(DEPTH, 128, 4, 5, 576)
    m["ident"] = C["ident"]
    m["alt"] = np.where(np.arange(128) % 2 == 0, 1.0, -1.0).astype(np.float32).reshape(128, 1).astype(NPBF)
    for s in seqs:
        for k in ("fc_", "fs_", "mec_", "mes_", "moc_", "mos_", "iec_", "ies_", "ioc_", "ios_", "feats_", "dec_", "pcorr_", "chdft_"):
            m[k + s] = C[k + s]
    return m


def finish_consts(m, seqs):
    return m


_NC = {}


def kernel(**inputs):
    inp = {k: np.asarray(v) for k, v in inputs.items()}
    if "nc" not in _NC:
        _NC["nc"] = Builder().build()
    nc = _NC["nc"]
    in_maps = [host_inputs(inp, b) for b in range(8)]
    res = run_bass_kernel_spmd(nc, in_maps, core_ids=list(range(8)))
    out = np.stack([np.ascontiguousarray(r["outT"].T) for r in res.results], axis=0)
    return out.astype(np.float32)
```
